# Optimizing a Trainium2 kernel written in Bass

```python
import math
import jax, jax.numpy as jnp
from jax import lax
import numpy as np

D_MODEL = 2048
BATCH = 4
SEQ = 2048
DEPTH = 1
DEC_BATCH = 128
DEC_SEQ = 4
PAST_LEN = 16384
PAGE_SIZE = 128

D_MIX = D_MODEL
D_CONV = D_MIX // 2
D_SSM = D_MIX - D_CONV
CONV_WIDTH = 31
SSM_HEAD_DIM = 64
SSM_HEADS = D_SSM // SSM_HEAD_DIM
SSM_GROUPS = 2
SSM_STATE = 128
SSM_CONV_WIDTH = 4
SSM_CHUNK = 128
D_XBC = D_SSM + 2 * SSM_GROUPS * SSM_STATE
D_IN = 2 * D_CONV + D_SSM + D_XBC + SSM_HEADS
D_FF = ((8 * D_MODEL // 3 + 127) // 128) * 128
N_MEM = 256
X_HEADS = 4
X_HEAD_DIM = D_MODEL // X_HEADS
EPS = 1e-6

kernel_name = 'hymba_conformer_ssd_macaron_decoder_step'


def rms_norm(x, g):
    xf = x.astype(jnp.float32)
    y = xf * lax.rsqrt(jnp.mean(xf * xf, axis=-1, keepdims=True) + EPS)
    return (y * g.astype(jnp.float32)).astype(x.dtype)


def layer_norm(x, g, b):
    xf = x.astype(jnp.float32)
    xc = xf - jnp.mean(xf, axis=-1, keepdims=True)
    y = xc * lax.rsqrt(jnp.mean(xc * xc, axis=-1, keepdims=True) + EPS)
    return (y * g.astype(jnp.float32) + b.astype(jnp.float32)).astype(x.dtype)


def swiglu(x, w_gate, w_up, w_down):
    return (jax.nn.silu(x @ w_gate) * (x @ w_up)) @ w_down


def causal_dwconv(buf, x, w, b):
    xe = jnp.concatenate([buf, x], axis=1)
    y = lax.conv_general_dilated(xe, w[:, None, :], window_strides=(1,), padding='VALID',
                                 dimension_numbers=('NWC', 'WIO', 'NWC'),
                                 feature_group_count=x.shape[-1])
    return y + b, xe[:, -(w.shape[0] - 1):]


def ssd_scan(x, dt, A, B, C, h0, chunk):
    b, L, H, P = x.shape
    G, N = B.shape[2], B.shape[3]
    Hg = H // G
    c = L // chunk
    x = x.reshape(b, c, chunk, G, Hg, P)
    dt = dt.reshape(b, c, chunk, G, Hg)
    B = B.reshape(b, c, chunk, G, N)
    C = C.reshape(b, c, chunk, G, N)
    a_cum = jnp.cumsum(dt * A.reshape(G, Hg), axis=2)
    seg = a_cum[:, :, :, None] - a_cum[:, :, None]
    causal = jnp.tril(jnp.ones((chunk, chunk), dtype=bool))[:, :, None, None]
    decay = jnp.exp(jnp.where(causal, seg, -jnp.inf))
    cb = jnp.einsum('bclgn,bcsgn->bclsg', C, B)
    w_intra = cb[..., None] * decay * dt[:, :, None]
    y_diag = jnp.einsum('bclsgh,bcsghp->bclghp', w_intra, x)
    decay_to_end = jnp.exp(a_cum[:, :, -1:] - a_cum)
    states = jnp.einsum('bclgn,bclgh,bclghp->bcghpn', B, decay_to_end * dt, x)
    chunk_decay = jnp.exp(a_cum[:, :, -1])

    def step(h, inp):
        s, d = inp
        return h * d[..., None, None] + s, h

    h_final, h_start = lax.scan(step, h0.reshape(b, G, Hg, P, N),
                                (jnp.moveaxis(states, 1, 0), jnp.moveaxis(chunk_decay, 1, 0)))
    h_start = jnp.moveaxis(h_start, 0, 1)
    y_off = jnp.einsum('bclgn,bcghpn,bclgh->bclghp', C, h_start, jnp.exp(a_cum))
    y = (y_diag + y_off).reshape(b, L, H, P)
    return y, h_final.reshape(b, H, P, N)


def mixer(h, conv_buf, ssm_conv_buf, ssm_state, chunk, p):
    bsz, T, _ = h.shape
    f32 = jnp.float32
    proj = h @ p['w_in']
    s1 = D_CONV
    s2 = 2 * D_CONV
    s3 = s2 + D_SSM
    s4 = s3 + D_XBC
    conv_val, conv_gate, z, xbc, dt_raw = jnp.split(proj, [s1, s2, s3, s4], axis=-1)
    u = conv_val * jax.nn.sigmoid(conv_gate)
    u, new_conv_buf = causal_dwconv(conv_buf, u, p['conv_w'], p['conv_b'])
    a_out = jax.nn.silu(layer_norm(u, p['conv_ln_g'], p['conv_ln_b']))
    xbc, new_ssm_conv_buf = causal_dwconv(ssm_conv_buf, xbc, p['ssm_conv_w'], p['ssm_conv_b'])
    xbc = jax.nn.silu(xbc)
    xs, Bm, Cm = jnp.split(xbc, [D_SSM, D_SSM + SSM_GROUPS * SSM_STATE], axis=-1)
    xs = xs.reshape(bsz, T, SSM_HEADS, SSM_HEAD_DIM).astype(f32)
    Bm = Bm.reshape(bsz, T, SSM_GROUPS, SSM_STATE).astype(f32)
    Cm = Cm.reshape(bsz, T, SSM_GROUPS, SSM_STATE).astype(f32)
    dt = jax.nn.softplus(dt_raw.astype(f32) + p['dt_bias'].astype(f32))
    A = -jnp.exp(p['a_log'].astype(f32))
    y, new_state = ssd_scan(xs, dt, A, Bm, Cm, ssm_state.astype(f32), chunk)
    y = y + p['d_skip'].astype(f32)[:, None] * xs
    y = y.reshape(bsz, T, D_SSM) * jax.nn.silu(z.astype(f32))
    y = rms_norm(y.reshape(bsz, T, SSM_GROUPS, D_SSM // SSM_GROUPS),
                 p['ssm_norm_g'].reshape(SSM_GROUPS, D_SSM // SSM_GROUPS))
    y = y.reshape(bsz, T, D_SSM).astype(h.dtype)
    out = jnp.concatenate([a_out, y], axis=-1) @ p['w_out']
    return out, new_conv_buf, new_ssm_conv_buf, new_state.astype(ssm_state.dtype)


def memory_kv(mem, p):
    bsz = mem.shape[0]
    m = rms_norm(mem, p['mem_norm_g'])
    k = (m @ p['w_xk']).reshape(bsz, N_MEM, X_HEADS, X_HEAD_DIM)
    v = (m @ p['w_xv']).reshape(bsz, N_MEM, X_HEADS, X_HEAD_DIM)
    return k, v


def cross_attend(h, k, v, p):
    bsz, T, _ = h.shape
    q = (h @ p['w_xq']).reshape(bsz, T, X_HEADS, X_HEAD_DIM)
    s = jnp.einsum('bthd,bmhd->bhtm', q, k, preferred_element_type=jnp.float32) * (X_HEAD_DIM ** -0.5)
    w = jax.nn.softmax(s, axis=-1).astype(v.dtype)
    o = jnp.einsum('bhtm,bmhd->bthd', w, v).reshape(bsz, T, D_MODEL)
    return o @ p['w_xo']


def decoder_layer(x, mem_k, mem_v, conv_buf, ssm_conv_buf, ssm_state, chunk, p):
    f1 = swiglu(rms_norm(x, p['ffn1_pre_g']), p['ffn1_w_gate'], p['ffn1_w_up'], p['ffn1_w_down'])
    x = x + 0.5 * rms_norm(f1, p['ffn1_post_g'])
    m, cb, scb, st = mixer(rms_norm(x, p['mix_pre_g']), conv_buf, ssm_conv_buf, ssm_state, chunk, p)
    x = x + rms_norm(m, p['mix_post_g'])
    a = cross_attend(rms_norm(x, p['xattn_pre_g']), mem_k, mem_v, p)
    x = x + rms_norm(a, p['xattn_post_g'])
    f2 = swiglu(rms_norm(x, p['ffn2_pre_g']), p['ffn2_w_gate'], p['ffn2_w_up'], p['ffn2_w_down'])
    x = x + 0.5 * rms_norm(f2, p['ffn2_post_g'])
    return x, cb, scb, st


def setup_inputs(seed: int = 0) -> dict:
    key = jax.random.key(seed)
    ks = iter(jax.random.split(key, 64))
    f32 = jnp.float32

    def nrm(shape, scale):
        return jax.random.normal(next(ks), shape, f32) * scale

    def gain(n):
        return 1.0 + nrm((DEPTH, n), 0.02)

    def lin(fi, fo):
        return nrm((DEPTH, fi, fo), fi ** -0.5)

    inp = {}
    inp['x_prompt'] = nrm((BATCH, SEQ, D_MODEL), 1.0)
    inp['x_sample'] = nrm((DEC_BATCH, DEC_SEQ, D_MODEL), 1.0)
    inp['mem_prompt'] = nrm((BATCH, N_MEM, D_MODEL), 1.0)
    inp['cache_mem_k'] = nrm((DEPTH, DEC_BATCH, N_MEM, X_HEADS, X_HEAD_DIM), 1.0)
    inp['cache_mem_v'] = nrm((DEPTH, DEC_BATCH, N_MEM, X_HEADS, X_HEAD_DIM), 1.0)
    inp['state_conv'] = nrm((DEPTH, DEC_BATCH, CONV_WIDTH - 1, D_CONV), 0.5)
    inp['state_ssm_conv'] = nrm((DEPTH, DEC_BATCH, SSM_CONV_WIDTH - 1, D_XBC), 1.0)
    inp['state_ssm'] = nrm((DEPTH, DEC_BATCH, SSM_HEADS, SSM_HEAD_DIM, SSM_STATE), 0.5)
    inp['ffn1_pre_g'] = gain(D_MODEL)
    inp['ffn1_w_gate'] = lin(D_MODEL, D_FF)
    inp['ffn1_w_up'] = lin(D_MODEL, D_FF)
    inp['ffn1_w_down'] = lin(D_FF, D_MODEL)
    inp['ffn1_post_g'] = gain(D_MODEL)
    inp['mix_pre_g'] = gain(D_MODEL)
    inp['w_in'] = lin(D_MODEL, D_IN)
    inp['conv_w'] = nrm((DEPTH, CONV_WIDTH, D_CONV), CONV_WIDTH ** -0.5)
    inp['conv_b'] = nrm((DEPTH, D_CONV), 0.02)
    inp['conv_ln_g'] = gain(D_CONV)
    inp['conv_ln_b'] = nrm((DEPTH, D_CONV), 0.02)
    inp['ssm_conv_w'] = nrm((DEPTH, SSM_CONV_WIDTH, D_XBC), SSM_CONV_WIDTH ** -0.5)
    inp['ssm_conv_b'] = nrm((DEPTH, D_XBC), 0.02)
    dt0 = jnp.exp(jax.random.uniform(next(ks), (DEPTH, SSM_HEADS), f32, math.log(1e-3), math.log(1e-1)))
    inp['dt_bias'] = dt0 + jnp.log(-jnp.expm1(-dt0))
    inp['a_log'] = jnp.log(jax.random.uniform(next(ks), (DEPTH, SSM_HEADS), f32, 1.0, 16.0))
    inp['d_skip'] = gain(SSM_HEADS)
    inp['ssm_norm_g'] = gain(D_SSM)
    inp['w_out'] = lin(D_MIX, D_MODEL)
    inp['mix_post_g'] = gain(D_MODEL)
    inp['xattn_pre_g'] = gain(D_MODEL)
    inp['mem_norm_g'] = gain(D_MODEL)
    inp['w_xq'] = lin(D_MODEL, D_MODEL)
    inp['w_xk'] = lin(D_MODEL, D_MODEL)
    inp['w_xv'] = lin(D_MODEL, D_MODEL)
    inp['w_xo'] = lin(D_MODEL, D_MODEL)
    inp['xattn_post_g'] = gain(D_MODEL)
    inp['ffn2_pre_g'] = gain(D_MODEL)
    inp['ffn2_w_gate'] = lin(D_MODEL, D_FF)
    inp['ffn2_w_up'] = lin(D_MODEL, D_FF)
    inp['ffn2_w_down'] = lin(D_FF, D_MODEL)
    inp['ffn2_post_g'] = gain(D_MODEL)
    return inp


def reference(x_prompt, x_sample, mem_prompt, cache_mem_k, cache_mem_v, state_conv, state_ssm_conv, state_ssm,
              ffn1_pre_g, ffn1_w_gate, ffn1_w_up, ffn1_w_down, ffn1_post_g,
              mix_pre_g, w_in, conv_w, conv_b, conv_ln_g, conv_ln_b, ssm_conv_w, ssm_conv_b,
              dt_bias, a_log, d_skip, ssm_norm_g, w_out, mix_post_g,
              xattn_pre_g, mem_norm_g, w_xq, w_xk, w_xv, w_xo, xattn_post_g,
              ffn2_pre_g, ffn2_w_gate, ffn2_w_up, ffn2_w_down, ffn2_post_g):
    params = dict(ffn1_pre_g=ffn1_pre_g, ffn1_w_gate=ffn1_w_gate, ffn1_w_up=ffn1_w_up, ffn1_w_down=ffn1_w_down,
                  ffn1_post_g=ffn1_post_g, mix_pre_g=mix_pre_g, w_in=w_in, conv_w=conv_w, conv_b=conv_b,
                  conv_ln_g=conv_ln_g, conv_ln_b=conv_ln_b, ssm_conv_w=ssm_conv_w, ssm_conv_b=ssm_conv_b,
                  dt_bias=dt_bias, a_log=a_log, d_skip=d_skip, ssm_norm_g=ssm_norm_g, w_out=w_out,
                  mix_post_g=mix_post_g, xattn_pre_g=xattn_pre_g, mem_norm_g=mem_norm_g, w_xq=w_xq,
                  w_xk=w_xk, w_xv=w_xv, w_xo=w_xo, xattn_post_g=xattn_post_g, ffn2_pre_g=ffn2_pre_g,
                  ffn2_w_gate=ffn2_w_gate, ffn2_w_up=ffn2_w_up, ffn2_w_down=ffn2_w_down, ffn2_post_g=ffn2_post_g)
    bp = x_prompt.shape[0]
    dt_ = x_prompt.dtype
    yp, ys = x_prompt, x_sample
    pk, pv, pc, psc, pst, sc, ssc, sst = [], [], [], [], [], [], [], []
    for layer in range(DEPTH):
        pl = {name: w[layer] for name, w in params.items()}
        mk, mv = memory_kv(mem_prompt, pl)
        yp, c1, c2, c3 = decoder_layer(
            yp, mk, mv,
            jnp.zeros((bp, CONV_WIDTH - 1, D_CONV), dt_),
            jnp.zeros((bp, SSM_CONV_WIDTH - 1, D_XBC), dt_),
            jnp.zeros((bp, SSM_HEADS, SSM_HEAD_DIM, SSM_STATE), dt_),
            SSM_CHUNK, pl)
        ys, d1, d2, d3 = decoder_layer(
            ys, cache_mem_k[layer], cache_mem_v[layer],
            state_conv[layer], state_ssm_conv[layer], state_ssm[layer],
            x_sample.shape[1], pl)
        pk.append(mk)
        pv.append(mv)
        pc.append(c1)
        psc.append(c2)
        pst.append(c3)
        sc.append(d1)
        ssc.append(d2)
        sst.append(d3)
    return (yp, ys, jnp.stack(pk), jnp.stack(pv), jnp.stack(pc), jnp.stack(psc), jnp.stack(pst),
            jnp.stack(sc), jnp.stack(ssc), jnp.stack(sst))
```

```python
from contextlib import ExitStack
import numpy as np
import concourse.bass as bass
import concourse.mybir as mybir
from concourse.bass_utils import run_bass_kernel_spmd

F32 = mybir.dt.float32
BF16 = mybir.dt.bfloat16
AF = mybir.ActivationFunctionType
ALU = mybir.AluOpType
AX = mybir.AxisListType

SAME_ENG_SYNC = True
EPS = 1e-6
DFF = 5504
NFC = 43


class Prog:
    CE = ('pe', 'act', 'dve', 'pool')

    def __init__(self, nc, stack):
        self.nc = nc
        self.stack = stack
        self.eng = {'pe': nc.tensor, 'act': nc.scalar, 'dve': nc.vector,
                    'pool': nc.gpsimd, 'sp': nc.sync}
        self.sem = {e: stack.enter_context(nc.semaphore('s_' + e)) for e in self.CE}
        self.cnt = {e: 0 for e in self.CE}
        self.pend = {e: False for e in self.CE}
        self.dsem = {}
        self.dcnt = {}
        self.last_w = {}
        self.readers = {}
        self.seen = {e: {} for e in self.eng}

    def _tok_sem(self, tok):
        if tok[0] == 'e':
            return self.sem[tok[1]], tok[2]
        return self.dsem[tok[1]], tok[2]

    def _wait(self, eng, toks):
        for tok in toks:
            if tok[0] == 'e' and tok[1] == eng:
                if eng == 'pe' or not SAME_ENG_SYNC:
                    continue
            key = (tok[0], tok[1])
            if self.seen[eng].get(key, 0) >= tok[2]:
                continue
            self.seen[eng][key] = tok[2]
            sem, val = self._tok_sem(tok)
            self.eng[eng].wait_ge(sem, val)

    def _deps(self, reads, writes):
        deps = []
        for r in reads:
            t = self.last_w.get(r)
            if t is not None:
                deps.append(t)
        for w in writes:
            t = self.last_w.get(w)
            if t is not None:
                deps.append(t)
            deps.extend(self.readers.get(w, ()))
        return deps

    def _commit(self, tok, reads, writes):
        for w in writes:
            self.last_w[w] = tok
            self.readers[w] = []
        for r in reads:
            if r in writes:
                continue
            self.readers.setdefault(r, []).append(tok)

    def op(self, eng, fn, reads=(), writes=(), sig=True):
        self._wait(eng, self._deps(reads, writes))
        inst = fn(self.eng[eng])
        if sig:
            self.cnt[eng] += 1
            inst.then_inc(self.sem[eng], 1)
            tok = ('e', eng, self.cnt[eng])
            self.pend[eng] = False
        else:
            assert eng == 'pe'
            tok = ('e', eng, self.cnt[eng] + 1)
            self.pend[eng] = True
        self._commit(tok, reads, writes)
        return tok

    def dma(self, key, fn, reads=(), writes=(), q='sp'):
        if key not in self.dsem:
            self.dsem[key] = self.stack.enter_context(
                self.nc.semaphore('d_%d' % len(self.dsem)))
            self.dcnt[key] = 0
        self._wait(q, self._deps(reads, writes))
        insts = fn(self.eng[q])
        if not isinstance(insts, (list, tuple)):
            insts = [insts]
        for i in insts:
            i.then_inc(self.dsem[key], 16)
            self.dcnt[key] += 16
        tok = ('d', key, self.dcnt[key])
        self._commit(tok, reads, writes)
        return tok

    def barrier(self):
        for e in self.CE:
            assert not self.pend[e], e
        toks = [('e', e, self.cnt[e]) for e in self.CE if self.cnt[e] > 0]
        toks += [('d', k, v) for k, v in self.dcnt.items() if v > 0]
        for e in self.eng:
            self._wait(e, [t for t in toks if not (t[0] == 'e' and t[1] == e)])
        self.last_w = {}
        self.readers = {}


def piece_index():
    idx = {}
    n = 0
    for f in ('f1', 'f2'):
        for kind in ('g', 'u', 'd'):
            for j in range(NFC):
                idx[(f, kind, j)] = n
                n += 1
    for j in range(37):
        idx[('win', j)] = n
        n += 1
    for nm in ('wout', 'xq', 'xk', 'xv', 'xo'):
        for j in range(16):
            idx[(nm, j)] = n
            n += 1
    return idx, n


def fm_piece(W, j, width=128):
    blk = W[:, j * width:(j + 1) * width].reshape(16, 128, width).transpose(1, 0, 2)
    out = np.zeros((128, 2048), np.float32)
    out[:, :16 * width] = blk.reshape(128, 16 * width)
    return out


def tm_piece(W, j):
    nb, half = j // 2, j % 2
    blk = W[half * 1024:(half + 1) * 1024, nb * 256:(nb + 1) * 256].reshape(8, 128, 256).transpose(1, 0, 2)
    return np.ascontiguousarray(blk).reshape(128, 2048)


def build_wall(inp):
    idx, n = piece_index()
    wall = np.empty((n, 128, 2048), np.float32)
    for f, pre in (('f1', 'ffn1'), ('f2', 'ffn2')):
        Wg, Wu, Wd = inp[pre + '_w_gate'][0], inp[pre + '_w_up'][0], inp[pre + '_w_down'][0]
        for j in range(NFC):
            wall[idx[(f, 'g', j)]] = fm_piece(Wg, j)
            wall[idx[(f, 'u', j)]] = fm_piece(Wu, j)
            wall[idx[(f, 'd', j)]] = Wd[j * 128:(j + 1) * 128, :]
    Win = inp['w_in'][0]
    for j in range(36):
        wall[idx[('win', j)]] = fm_piece(Win, j)
    wall[idx[('win', 36)]] = fm_piece(Win[:, 4608:4624], 0, 16)
    for j in range(16):
        wall[idx[('wout', j)]] = tm_piece(inp['w_out'][0], j)
        wall[idx[('xq', j)]] = fm_piece(inp['w_xq'][0], j)
        wall[idx[('xk', j)]] = tm_piece(inp['w_xk'][0], j)
        wall[idx[('xv', j)]] = tm_piece(inp['w_xv'][0], j)
        wall[idx[('xo', j)]] = tm_piece(inp['w_xo'][0], j)
    return wall


C_PRE = {'f1': 0, 'mix': 16, 'xa': 32, 'f2': 48, 'mem': 64}
C_CW, C_CB, C_LG, C_LB = 80, 328, 336, 344
C_SW, C_SB, C_NG, C_DS, C_DTB, C_AL, C_FLAG = 352, 400, 412, 420, 428, 444, 460
NCST = 464
M_ID, M_U, M_L, M_ONE, M_NEG, M_MSK = 0, 128, 256, 384, 512, 640
M_UB, M_LB, M_NEGB, M_MSKB, M_SAME, M_BM = 768, 896, 1024, 1152, 1280, 1408
M_BMC = 2432
NCM = 2448


def build_consts(inp, half):
    c = np.zeros((128, NCST), np.float32)

    def col16(v):
        return v.reshape(16, 128).T
    for nm, key in (('f1', 'ffn1_pre_g'), ('mix', 'mix_pre_g'), ('xa', 'xattn_pre_g'),
                    ('f2', 'ffn2_pre_g'), ('mem', 'mem_norm_g')):
        c[:, C_PRE[nm]:C_PRE[nm] + 16] = col16(inp[key][0])
    cw = inp['conv_w'][0]
    c[:, C_CW:C_CW + 248] = cw.T.reshape(8, 128, 31).transpose(1, 0, 2).reshape(128, 248)
    c[:, C_CB:C_CB + 8] = inp['conv_b'][0].reshape(8, 128).T
    c[:, C_LG:C_LG + 8] = inp['conv_ln_g'][0].reshape(8, 128).T
    c[:, C_LB:C_LB + 8] = inp['conv_ln_b'][0].reshape(8, 128).T
    sw = inp['ssm_conv_w'][0]
    c[:, C_SW:C_SW + 48] = sw.T.reshape(12, 128, 4).transpose(1, 0, 2).reshape(128, 48)
    c[:, C_SB:C_SB + 12] = inp['ssm_conv_b'][0].reshape(12, 128).T
    c[:, C_NG:C_NG + 8] = inp['ssm_norm_g'][0].reshape(8, 128).T
    c[:, C_DS:C_DS + 8] = np.repeat(inp['d_skip'][0], 64).reshape(8, 128).T
    c[:, C_DTB:C_DTB + 16] = inp['dt_bias'][0][None, :]
    c[:, C_AL:C_AL + 16] = inp['a_log'][0][None, :]
    c[:, C_FLAG] = float(half)
    return c


def build_cmat():
    m = np.zeros((128, NCM), np.float32)
    j = np.arange(128)[:, None]
    l = np.arange(128)[None, :]
    m[:, M_ID:M_ID + 128] = (j == l)
    m[:, M_U:M_U + 128] = (j <= l)
    m[:, M_L:M_L + 128] = (j > l)
    m[:, M_ONE:M_ONE + 128] = 1.0
    m[:, M_NEG:M_NEG + 128] = np.where(l < j, -30000.0, 0.0)
    m[:, M_MSK:M_MSK + 128] = (l >= j)
    same = (j // 4 == l // 4) & (j < 64) & (l < 64)
    m[:, M_UB:M_UB + 128] = same & (j <= l)
    m[:, M_LB:M_LB + 128] = same & (j > l)
    m[:, M_NEGB:M_NEGB + 128] = np.where(same & (l >= j), 0.0, -30000.0)
    m[:, M_MSKB:M_MSKB + 128] = same & (l >= j)
    m[:, M_SAME:M_SAME + 128] = same
    bm = (np.arange(64)[None, :] // 4 == np.arange(16)[:, None]).astype(np.float32)
    m[:, M_BM:M_BM + 1024] = bm.reshape(1, 1024)
    m[:64, M_BMC:M_BMC + 16] = bm.T
    return m


NTM = 1088
NTX = 1118
ARENA = 48896


def build(seq, stop=None):
    nc = bass.Bass("TRN2", target_bir_lowering=False)
    pidx, npieces = piece_index()

    def DT(name, shape, kind=None):
        if kind is None:
            return nc.dram_tensor(name, shape, F32)
        return nc.dram_tensor(name, shape, F32, kind=kind)
    xin = DT("xin", [NTM, 2048], "ExternalInput")
    mem = DT("mem", [256, 2048], "ExternalInput")
    ck = DT("ck", [16 * 256, 2048], "ExternalInput")
    cv = DT("cv", [16 * 256, 2048], "ExternalInput")
    sconv = DT("sconv", [16 * 30, 1024], "ExternalInput")
    sssc = DT("sssc", [16 * 3, 1536], "ExternalInput")
    sst = DT("sst", [16 * 1024, 128], "ExternalInput")
    wall = DT("wall", [npieces * 128, 2048], "ExternalInput")
    cst_d = DT("cst", [128, NCST], "ExternalInput")
    cmat_d = DT("cmat", [128, NCM], "ExternalInput")
    postg = DT("postg", [4, 2048], "ExternalInput")
    yout = DT("yout", [NTM, 2048], "ExternalOutput")
    kout = DT("kout", [256, 2048], "ExternalOutput")
    vout = DT("vout", [256, 2048], "ExternalOutput")
    convp = DT("convp", [30, 1024], "ExternalOutput")
    sscp = DT("sscp", [3, 1536], "ExternalOutput")
    sstp = DT("sstp", [1024, 128], "ExternalOutput")
    convs = DT("convs", [16 * 30, 1024], "ExternalOutput")
    sscs = DT("sscs", [16 * 3, 1536], "ExternalOutput")
    ssts = DT("ssts", [16 * 1024, 128], "ExternalOutput")
    x1 = DT("x1", [NTM, 2048])
    x2 = DT("x2", [NTM, 2048])
    x3 = DT("x3", [NTM, 2048])
    ex_i1 = DT("ex_i1", [128, 240])
    ex_o1 = DT("ex_o1", [256, 240])
    ex_i2 = DT("ex_i2", [128, 1024])
    ex_o2 = DT("ex_o2", [256, 1024])
    PAIRS = [[0, 1], [2, 3], [4, 5], [6, 7]]

    def rows(t, r0, n, c0=0, w=None):
        a = t.ap()
        w = a.shape[1] - c0 if w is None else w
        return a[r0:r0 + n, c0:c0 + w]

    st = ExitStack()
    with st:
        P = Prog(nc, st)
        arena = st.enter_context(nc.sbuf_tensor("arena", [128, ARENA], F32))
        ps = st.enter_context(nc.psum_tensor("ps", [128, 8, 512], F32))
        psb = ps[:].rearrange("p b f -> p (b f)").bitcast(BF16)

        def V(off, dt, shape):
            n = int(np.prod(shape))
            assert off + (n if dt == F32 else (n + 1) // 2) <= ARENA, (off, n)
            if dt == F32:
                v = arena[:, off:off + n]
            else:
                v = arena[:, off:off + (n + 1) // 2].bitcast(BF16)[:, 0:n]
            if len(shape) == 2:
                v = v.rearrange("p (a b) -> p a b", a=shape[0])
            elif len(shape) == 3:
                v = v.rearrange("p (a b c) -> p a b c", a=shape[0], b=shape[1])
            return v

        def PSB(bank, nbanks, shape):
            n = int(np.prod(shape))
            assert n <= nbanks * 1024
            v = psb[:, bank * 1024:bank * 1024 + n]
            if len(shape) == 2:
                v = v.rearrange("p (a b) -> p a b", a=shape[0])
            return v

        o = 0
        wst = V(o, F32, [3, 2048]); o += 6144
        wbf = V(o, BF16, [6, 2048]); o += 6144
        cst = V(o, F32, [NCST]); o += NCST
        cm = V(o, F32, [NCM]); o += NCM
        identb = V(o, BF16, [128]); o += 64
        ssm = V(o, F32, [64]); o += 64
        Aneg = V(o, F32, [16]); o += 16
        dtp = V(o, F32, [9, 16]); o += 144
        R0 = o
        identf = cm[:, M_ID:M_ID + 128]
        onesf = cm[:, M_ONE:M_ONE + 128]
        flag = cst[:, C_FLAG:C_FLAG + 1]

        class WS:
            NS, NB, DC, DD = 3, 6, 4, 6

            def __init__(self):
                self.i = 0
                self.req = []
                self.nd = 0
                self.ncst = 0

            def _dma(self, j, pid):
                s = j % self.NS
                P.dma(('wst', s), lambda e: e.dma_start(out=wst[:, s, :], in_=rows(wall, pid * 128, 128)),
                      writes=[('wst', s)])

            def _cast(self, j):
                s, t = j % self.NS, j % self.NB
                P.op('pool', lambda e: e.tensor_copy(out=wbf[:, t, :], in_=wst[:, s, :]),
                     reads=[('wst', s)], writes=[('wbf', t)])

            def next(self, name):
                pid = pidx[name]
                i = self.i
                self.i += 1
                self.req.append(pid)
                if seq is not None:
                    assert seq[i] == pid
                    src, dc, dd = seq, self.DC, self.DD
                else:
                    src, dc, dd = self.req, 0, 0
                last = len(src) - 1

                def ensure_cast(j):
                    while self.ncst <= j:
                        ensure_dma(self.ncst)
                        self._cast(self.ncst)
                        self.ncst += 1

                def ensure_dma(j):
                    while self.nd <= j:
                        if self.nd - self.NS >= 0:
                            ensure_cast(self.nd - self.NS)
                        self._dma(self.nd, src[self.nd])
                        self.nd += 1
                ensure_dma(min(i + dd, last))
                ensure_cast(min(i + dc, last))
                return ('wbf', i % self.NB), wbf[:, i % self.NB, :]
        ws = WS()

        def mm(out, lhsT, rhs, start, stop, reads, writes, sig):
            P.op('pe', lambda e: e.matmul(out, lhsT=lhsT, rhs=rhs, start=start, stop=stop),
                 reads=reads, writes=writes, sig=sig)

        def tr(out, in_, ident, reads, writes, sig):
            P.op('pe', lambda e: e.transpose(out=out, in_=in_, identity=ident),
                 reads=reads, writes=writes, sig=sig)

        def act(out, in_, func, reads, writes, **kw):
            if func == AF.Copy and 'bias' not in kw and 'accum_out' not in kw:
                sc = kw.get('scale', None)
                eng = 'dve' if ('PSUM' in str(in_.space).upper() or 'PSUM' in str(out.space).upper() or sc is not None) else 'pool'
                if sc is None:
                    P.op(eng, lambda e: e.tensor_copy(out=out, in_=in_), reads=reads, writes=writes)
                else:
                    P.op(eng, lambda e: e.tensor_scalar(out=out, in0=in_, scalar1=sc, scalar2=None, op0=ALU.mult),
                         reads=reads, writes=writes)
                return
            P.op('act', lambda e: e.activation(out=out, in_=in_, func=func, **kw), reads=reads, writes=writes)

        def tt(out, in0, in1, op, reads, writes):
            P.op('dve', lambda e: e.tensor_tensor(out=out, in0=in0, in1=in1, op=op), reads=reads, writes=writes)

        def ts(out, in0, s1, s2, op0, op1, reads, writes):
            if op1 is None:
                P.op('dve', lambda e: e.tensor_scalar(out=out, in0=in0, scalar1=s1, scalar2=None, op0=op0), reads=reads, writes=writes)
            else:
                P.op('dve', lambda e: e.tensor_scalar(out=out, in0=in0, scalar1=s1, scalar2=s2, op0=op0, op1=op1), reads=reads, writes=writes)

        def stt(out, in0, scalar, in1, op0, op1, reads, writes):
            P.op('dve', lambda e: e.scalar_tensor_tensor(out=out, in0=in0, scalar=scalar, in1=in1, op0=op0, op1=op1),
                 reads=reads, writes=writes)

        def exchange(tag, src_sb, ib, ob, dst_sb, rkeys, wkeys):
            P.dma(('exi', tag), lambda e: e.dma_start(out=ib.ap(), in_=src_sb), reads=rkeys, writes=[('ib', tag)])
            key = ('cc', tag)
            if key not in P.dsem:
                P.dsem[key] = st.enter_context(nc.semaphore('cc_' + tag))
                P.dcnt[key] = 0
            P._wait('pool', P._deps([('ib', tag)], [('ob', tag)]))
            ins = nc.gpsimd.collective_compute("AllGather", ALU.bypass, replica_groups=PAIRS,
                                               ins=[ib.ap().opt()], outs=[ob.ap().opt()])
            ins.then_inc(P.dsem[key])
            P.dcnt[key] += 1
            P._commit(('d', key, P.dcnt[key]), [('ib', tag)], [('ob', tag)])
            P.dma(('exo', tag), lambda e: e.dma_start(out=dst_sb, in_=ob.ap()[0:128, :]), reads=[('ob', tag)], writes=wkeys)

        P.dma('c0', lambda e: [e.dma_start(out=cst, in_=cst_d.ap()), e.dma_start(out=cm, in_=cmat_d.ap())],
              writes=['cst', 'cm'])
        P.op('dve', lambda e: e.tensor_copy(out=identb, in_=identf), reads=['cm'], writes=['identb'])
        act(Aneg, cst[:, C_AL:C_AL + 16], AF.Exp, ['cst'], ['Aneg'])
        ts(Aneg, Aneg, -1.0, None, ALU.mult, None, ['Aneg'], ['Aneg'])
        P.barrier()

        def rstd_chain(n, sb, s, eps_scale):
            ts(ssm[:n, sb + 1:sb + 2], ssm[:n, sb:sb + 1], eps_scale, EPS, ALU.mult, ALU.add, [('ss', s)], [('ss', s)])
            act(ssm[:n, sb + 2:sb + 3], ssm[:n, sb + 1:sb + 2], AF.Sqrt, [('ss', s)], [('ss', s)])
            P.op('dve', lambda e: e.reciprocal(out=ssm[:n, sb + 3:sb + 4], in_=ssm[:n, sb + 2:sb + 3]),
                 reads=[('ss', s)], writes=[('ss', s)])

        def prenorm(src, tiles, gcol, xT, tmp_off):
            xt = V(tmp_off, F32, [2, 2048])
            junk = V(tmp_off + 4096, BF16, [2048])
            xs = V(tmp_off + 5120, BF16, [2, 2048])
            for ti, (r0, n) in enumerate(tiles):
                s = ti % 2
                sb = 8 * s
                P.dma(('xt', s), lambda e: e.dma_start(out=xt[:n, s, :], in_=src(r0, n)), writes=[('xt', s)])
                act(junk[:n], xt[:n, s, :], AF.Square, [('xt', s)], ['junk', ('ss', s)], accum_out=ssm[:n, sb:sb + 1])
                rstd_chain(n, sb, s, 1.0 / 2048)
                act(xs[:n, s, :], xt[:n, s, :], AF.Copy, [('xt', s), ('ss', s)], [('xs', s)], scale=ssm[:n, sb + 3:sb + 4])
                pT = PSB(2 * s, 2, [16, 128])
                for k in range(16):
                    tr(pT[:, k, :n], xs[:n, s, k * 128:(k + 1) * 128], identb[:n, :n],
                       [('xs', s), 'identb'], [('ps', 2 * s), ('ps', 2 * s + 1)], k == 15)
                tt(xT[:, :, r0:r0 + n], pT[:, :, :n], gcol.unsqueeze(2).to_broadcast([128, 16, n]), ALU.mult,
                   [('ps', 2 * s), ('ps', 2 * s + 1), 'cst'], ['xT'])

        def postnorm(yfn, tiles, prow, scale, xold, xnew, tmp_off):
            xo = V(tmp_off, F32, [2, 2048])
            junk = V(tmp_off + 4096, BF16, [2048])
            gb = V(tmp_off + 5120, F32, [2048])
            P.dma('gb', lambda e: e.dma_start(out=gb, in_=bass.AP(postg, prow * 2048, [[0, 128], [1, 2048]])),
                  writes=['gb'])
            for ti, (r0, n) in enumerate(tiles):
                s = ti % 2
                sb = 8 * s
                y = yfn(ti)
                yk = ('yacc', ti)
                P.dma(('xo', s), lambda e: e.dma_start(out=xo[:n, s, :], in_=xold(r0, n)), writes=[('xo', s)])
                act(junk[:n], y[:n], AF.Square, [yk], ['junk', ('ss', s)], accum_out=ssm[:n, sb:sb + 1])
                rstd_chain(n, sb, s, 1.0 / 2048)
                stt(y[:n], y[:n], ssm[:n, sb + 3:sb + 4], gb[:n], ALU.mult, ALU.mult, [yk, ('ss', s), 'gb'], [yk])
                stt(y[:n], y[:n], float(scale), xo[:n, s, :], ALU.mult, ALU.add, [yk, ('xo', s)], [yk])
                P.dma(('yo', ti % 4), lambda e: e.dma_start(out=xnew(r0, n), in_=y[:n]), reads=[yk], writes=[('xnew', ti)])

        def mk_tiles(ntok):
            return [(r, min(128, ntok - r)) for r in range(0, ntok, 128)]

        def mk_blocks(ntok):
            return [(c, min(512, ntok - c)) for c in range(0, ntok, 512)]

        TILES, BLOCKS = mk_tiles(NTM), mk_blocks(NTM)
        BLOCKSX = mk_blocks(NTX)

        def ffn(f, gkey, prow, src, dst):
            ntok = NTM
            tiles, blocks = TILES, BLOCKS
            nt = len(tiles)
            xT = V(R0, BF16, [16, ntok])
            o1 = R0 + 8 * ntok
            prenorm(src, tiles, cst[:, C_PRE[gkey]:C_PRE[gkey] + 16], xT, o1)
            P.barrier()
            yacc = V(o1, F32, [nt, 2048])
            o2 = o1 + nt * 2048
            hT = V(o2, BF16, [4, ntok])
            o3 = o2 + 2 * ntok
            sg = V(o3, BF16, [2, 512])
            cnt = {'t': 0, 'q': 0}

            def gate_up(j):
                kg, wg = ws.next((f, 'g', j))
                ku, wu = ws.next((f, 'u', j))
                wg = wg.rearrange("p (k c) -> p k c", k=16)
                wu = wu.rearrange("p (k c) -> p k c", k=16)
                for (c0, n) in blocks:
                    t = cnt['t'] % 2
                    cnt['t'] += 1
                    for k in range(16):
                        mm(ps[:, t, :n], wg[:, k, :], xT[:, k, c0:c0 + n], k == 0, k == 15, [kg, 'xT'], [('ps', t)], k == 15)
                    for k in range(16):
                        mm(ps[:, 2 + t, :n], wu[:, k, :], xT[:, k, c0:c0 + n], k == 0, k == 15, [ku, 'xT'], [('ps', 2 + t)], k == 15)
                    act(sg[:, t, :n], ps[:, t, :n], AF.Silu, [('ps', t)], [('sg', t)])
                    tt(hT[:, j % 4, c0:c0 + n], sg[:, t, :n], ps[:, 2 + t, :n], ALU.mult, [('sg', t), ('ps', 2 + t)], [('hT', j % 4)])

            def down(grp, first):
                wd = [ws.next((f, 'd', j)) for j in grp]
                for ti, (r0, n) in enumerate(tiles):
                    for nb in range(4):
                        q = 4 + cnt['q'] % 4
                        cnt['q'] += 1
                        for gi, j in enumerate(grp):
                            mm(ps[:n, q, :], hT[:, j % 4, r0:r0 + n], wd[gi][1][:, nb * 512:(nb + 1) * 512],
                               gi == 0, gi == len(grp) - 1, [wd[gi][0], ('hT', j % 4)], [('ps', q)], gi == len(grp) - 1)
                        ysl = yacc[:n, ti, nb * 512:(nb + 1) * 512]
                        if first:
                            act(ysl, ps[:n, q, :], AF.Copy, [('ps', q)], [('yacc', ti)])
                        else:
                            tt(ysl, ysl, ps[:n, q, :], ALU.add, [('ps', q), ('yacc', ti)], [('yacc', ti)])

            groups = [list(range(j, min(j + 2, NFC))) for j in range(0, NFC, 2)]
            for gi, grp in enumerate(groups):
                for j in grp:
                    gate_up(j)
                if gi > 0:
                    down(groups[gi - 1], gi == 1)
            down(groups[-1], False)
            P.barrier()
            postnorm(lambda ti: yacc[:, ti, :], tiles, prow, 0.5, src, dst, R0)
            P.barrier()

        def linear_tm(name, inT, tiles, acc):
            cntq = 0
            for nb in range(8):
                p0 = ws.next((name, 2 * nb))
                p1 = ws.next((name, 2 * nb + 1))
                for ti, (r0, n) in enumerate(tiles):
                    q = 4 + cntq % 4
                    cntq += 1
                    for k in range(16):
                        pk, pa = (p0, p1)[k // 8]
                        mm(ps[:n, q, 0:256], inT(k)[:, r0:r0 + n], pa[:, (k % 8) * 256:(k % 8 + 1) * 256],
                           k == 0, k == 15, [pk, 'linin'], [('ps', q)], k == 15)
                    act(acc(ti)[:n, nb * 256:(nb + 1) * 256], ps[:n, q, 0:256], AF.Copy, [('ps', q)], [('yacc', ti)])

        def win_fm(j, xT, blocks, evac, name='win'):
            kw, w = ws.next((name, j))
            w = w.rearrange("p (k c) -> p k c", k=16)
            for bi, (c0, n) in enumerate(blocks):
                t = win_fm.t % 4
                win_fm.t += 1
                for k in range(16):
                    mm(ps[:, t, :n], w[:, k, :], xT[:, k, c0:c0 + n], k == 0, k == 15, [kw, 'xT'], [('ps', t)], k == 15)
                evac(bi, c0, n, ps[:, t, :n], ('ps', t))
        win_fm.t = 0

        def dwconv(buf, acc, wcol, bcol, ntap, L):
            ts(acc, buf[:, :, 0:L], wcol[:, 0:1], bcol, ALU.mult, ALU.add, ['cbuf', 'cst'], ['cacc'])
            for k in range(1, ntap):
                stt(acc, buf[:, :, k:k + L], wcol[:, k:k + 1], acc, ALU.mult, ALU.add, ['cbuf', 'cacc', 'cst'], ['cacc'])

        if stop == 1:
            ffn('f1', 'f1', 0, lambda r0, n: rows(xin, r0, n), lambda r0, n: rows(yout, r0, n))
            return nc, ws.req
        if stop == 1.5:
            ffn('f1', 'f1', 0, lambda r0, n: rows(xin, r0, n), lambda r0, n: rows(x1, r0, n))
            ffn('f2', 'f2', 3, lambda r0, n: rows(x1, r0, n), lambda r0, n: rows(x2, r0, n))
            ffn('f1', 'f1', 0, lambda r0, n: rows(x2, r0, n), lambda r0, n: rows(x3, r0, n))
            ffn('f2', 'f2', 3, lambda r0, n: rows(x3, r0, n), lambda r0, n: rows(yout, r0, n))
            return nc, ws.req
        ffn('f1', 'f1', 0, lambda r0, n: rows(xin, r0, n), lambda r0, n: rows(x1, r0, n))

        def mixer():
            tiles, blocks = TILES, BLOCKSX
            xT = V(R0, BF16, [16, NTX]); o1 = R0 + 8 * NTX
            prenorm(lambda r0, n: rows(x1, r0, n), tiles, cst[:, C_PRE['mix']:C_PRE['mix'] + 16], xT, o1)
            hb = V(o1 + 8000, BF16, [16, 30])
            hb2 = V(o1 + 8300, BF16, [16, 30])
            P.op('dve', lambda e: e.tensor_copy(out=hb, in_=xT[:, :, 994:1024]), reads=['xT'], writes=['hb'])
            exchange('h', arena[:, o1 + 8000:o1 + 8240], ex_i1, ex_o1, arena[:, o1 + 8300:o1 + 8540], ['hb'], ['hb2'])
            ts(xT[:, :, 1088:1118], hb2, flag, None, ALU.mult, None, ['hb2', 'cst'], ['xT'])
            P.barrier()
            if stop == 2.1:
                return
            aT = V(o1, BF16, [8, NTM]); o1 += 4 * NTM
            oA = o1
            assert oA == R0 + 13296
            yc = V(oA, F32, [8, NTM]); o2 = oA + 8 * NTM
            cbp = V(o2, F32, [1, 1054]); o2 += 1054
            cbs = V(o2, F32, [16, 34]); o2 += 544
            sgk = V(o2, F32, [3, 512]); o2 += 1536
            scT = V(o2, F32, [8, 480]); o2 += 3840
            tl = V(o2, F32, [1024]); o2 += 1024
            ld = V(o2, F32, [1024]); o2 += 1024
            for i in range(4):
                P.dma('ld', lambda e: e.dma_start(out=ld[:120], in_=rows(sconv, i * 120, 120)), writes=['ld'])
                P.dma('cp1', lambda e: [e.dma_start(out=rows(convs, (4 * i + s_) * 30, 26), in_=ld[30 * s_ + 4:30 * s_ + 30, :])
                                        for s_ in range(4)], reads=['ld'], writes=[('convs_old', i)])
                for c in range(8):
                    tr(ps[:, 4 + c % 2, 0:120], ld[:120, c * 128:(c + 1) * 128], identf[:120, :120], ['ld', 'cm'],
                       [('ps', 4 + c % 2)], True)
                    act(scT[:, c, i * 120:(i + 1) * 120], ps[:, 4 + c % 2, 0:120], AF.Copy, [('ps', 4 + c % 2)], ['scT'])
            for c in range(8):
                def ev_g(bi, c0, n, pa, pk):
                    act(sgk[:, bi, :n], pa, AF.Sigmoid, [pk], [('sgk', bi)])

                def ev_v(bi, c0, n, pa, pk):
                    if bi < 2:
                        tt(cbp[:, 0, 30 + c0:30 + c0 + n], sgk[:, bi, :n], pa, ALU.mult, [pk, ('sgk', bi)], ['cbuf'])
                    else:
                        tt(cbs[:, :, 30:34], sgk[:, bi, 0:64].rearrange("p (s t) -> p s t", s=16),
                           pa[:, 0:64].rearrange("p (s t) -> p s t", s=16), ALU.mult, [pk, ('sgk', bi)], ['cbuf'])
                        tt(cbp[:, 0, 0:30], sgk[:, bi, 64:94], pa[:, 64:94], ALU.mult, [pk, ('sgk', bi)], ['cbuf'])
                win_fm(8 + c, xT, blocks, ev_g)
                act(cbs[:, :, 0:30], scT[:, c, :].rearrange("p (s t) -> p s t", s=16), AF.Copy, ['scT'], ['cbuf'])
                win_fm(c, xT, blocks, ev_v)
                wcol = cst[:, C_CW + 31 * c:C_CW + 31 * c + 31]
                bcol = cst[:, C_CB + c:C_CB + c + 1]
                dwconv(cbp, yc[:, c, 0:1024].unsqueeze(1), wcol, bcol, 31, 1024)
                dwconv(cbs, yc[:, c, 1024:1088].rearrange("p (s t) -> p s t", s=16), wcol, bcol, 31, 4)
                tr(ps[:30, 6, 0:128], cbp[:, 0, 1024:1054], identf, ['cbuf', 'cm'], [('ps', 6)], True)
                act(tl[:30, c * 128:(c + 1) * 128], ps[:30, 6, 0:128], AF.Copy, [('ps', 6)], ['tl'])
                P.op('dve', lambda e: e.tensor_copy(out=sgk[:, 0, 0:64].rearrange("p (s t) -> p s t", s=16), in_=cbs[:, :, 30:34]),
                     reads=['cbuf'], writes=[('sgk', 0)])
                tr(ps[:64, 7, 0:128], sgk[:, 0, 0:64], identf, [('sgk', 0), 'cm'], [('ps', 7)], True)
                act(ld[:64, c * 128:(c + 1) * 128], ps[:64, 7, 0:128], AF.Copy, [('ps', 7)], ['ld2'])
            P.dma('o1', lambda e: e.dma_start(out=convp.ap(), in_=tl[:30, :]), reads=['tl'], writes=['convp'])
            P.dma('o2', lambda e: [e.dma_start(out=rows(convs, s_ * 30 + 26, 4), in_=ld[4 * s_:4 * s_ + 4, :]) for s_ in range(16)],
                  reads=['ld2'], writes=['convs_new'])
            P.barrier()
            sq = V(oA + 8 * NTM, F32, [512])
            st1 = V(oA + 8 * NTM + 512, F32, [3, 512])
            for (c0, n) in BLOCKS:
                for c in range(8):
                    mm(ps[:, 0, :n], onesf, yc[:, c, c0:c0 + n], c == 0, c == 7, ['yc', 'cm'], [('ps', 0)], c == 7)
                for c in range(8):
                    act(sq[:, :n], yc[:, c, c0:c0 + n], AF.Square, ['yc'], ['sq'])
                    mm(ps[:, 1, :n], onesf, sq[:, :n], c == 0, c == 7, ['sq', 'cm'], [('ps', 1)], True)
                mean, var, rstd = st1[:, 0, :n], st1[:, 1, :n], st1[:, 2, :n]
                act(mean, ps[:, 0, :n], AF.Copy, [('ps', 0)], ['st1'], scale=1.0 / 1024)
                tt(var, mean, mean, ALU.mult, ['st1'], ['st1'])
                stt(var, ps[:, 1, :n], 1.0 / 1024, var, ALU.mult, ALU.subtract, ['st1', ('ps', 1)], ['st1'])
                ts(var, var, EPS, None, ALU.add, None, ['st1'], ['st1'])
                act(var, var, AF.Sqrt, ['st1'], ['st1'])
                P.op('dve', lambda e: e.reciprocal(out=rstd, in_=var), reads=['st1'], writes=['st1'])
                for c in range(8):
                    tt(sq[:, :n], yc[:, c, c0:c0 + n], mean, ALU.subtract, ['yc', 'st1'], ['sq'])
                    tt(sq[:, :n], sq[:, :n], rstd, ALU.mult, ['sq', 'st1'], ['sq'])
                    act(aT[:, c, c0:c0 + n], sq[:, :n], AF.Silu, ['sq', 'cst'], ['aT'],
                        scale=cst[:, C_LG + c:C_LG + c + 1], bias=cst[:, C_LB + c:C_LB + c + 1])
            P.barrier()
            if stop == 2.2:
                return
            xsT = V(oA, F32, [8, NTM]); o2 = oA + 8 * NTM
            bcT = V(o2, BF16, [4, NTM]); o2 += 2 * NTM
            szT = V(o2, BF16, [8, NTM]); o2 += 4 * NTM
            oB = o2
            ynT = V(o2, BF16, [8, NTM])
            oTail = o2 + 4 * NTM
            xbp = V(o2, F32, [1, 1028]); o2 += 1028
            xbs = V(o2, F32, [16, 7]); o2 += 112
            cac = V(o2, F32, [NTM]); o2 += NTM
            s3T = V(o2, F32, [12, 48]); o2 += 576
            tl3 = V(o2, F32, [128]); o2 += 128
            ld3 = V(o2, F32, [1536]); o2 += 1536
            raw = V(o2, F32, [48]); o2 += 48
            P.dma('ld', lambda e: e.dma_start(out=ld3[:48], in_=sssc.ap()), writes=['ld'])
            for c in range(12):
                tr(ps[:, 4 + c % 2, 0:48], ld3[:48, c * 128:(c + 1) * 128], identf[:48, :48], ['ld', 'cm'], [('ps', 4 + c % 2)], True)
                act(s3T[:, c, :], ps[:, 4 + c % 2, 0:48], AF.Copy, [('ps', 4 + c % 2)], ['s3T'])
            P.barrier()
            for c in range(12):
                def ev(bi, c0, n, pa, pk):
                    if bi < 2:
                        act(xbp[:, 0, 3 + c0:3 + c0 + n], pa, AF.Copy, [pk], ['cbuf'])
                    else:
                        act(xbs[:, :, 3:7], pa[:, 0:64].rearrange("p (s t) -> p s t", s=16), AF.Copy, [pk], ['cbuf'])
                        act(xbp[:, 0, 0:3], pa[:, 91:94], AF.Copy, [pk], ['cbuf'])
                act(xbs[:, :, 0:3], s3T[:, c, :].rearrange("p (s t) -> p s t", s=16), AF.Copy, ['s3T'], ['cbuf'])
                win_fm(24 + c, xT, blocks, ev)
                wcol = cst[:, C_SW + 4 * c:C_SW + 4 * c + 4]
                bcol = cst[:, C_SB + c:C_SB + c + 1]
                dwconv(xbp, cac[:, 0:1024].unsqueeze(1), wcol, bcol, 4, 1024)
                dwconv(xbs, cac[:, 1024:1088].rearrange("p (s t) -> p s t", s=16), wcol, bcol, 4, 4)
                dst = xsT[:, c, :] if c < 8 else bcT[:, c - 8, :]
                act(dst, cac, AF.Silu, ['cacc'], ['xsT'])
                tr(ps[:3, 6, 0:128], xbp[:, 0, 1024:1027], identf, ['cbuf', 'cm'], [('ps', 6)], True)
                act(tl3[:3, :], ps[:3, 6, 0:128], AF.Copy, [('ps', 6)], ['tl'])
                P.dma('o1', lambda e: e.dma_start(out=rows(sscp, 0, 3, c * 128, 128), in_=tl3[:3, :]), reads=['tl'], writes=[('sscp', c)])
                P.op('dve', lambda e: e.tensor_copy(out=raw.rearrange("p (s t) -> p s t", s=16), in_=xbs[:, :, 4:7]), reads=['cbuf'], writes=['raw'])
                tr(ps[:48, 7, 0:128], raw, identf, ['raw', 'cm'], [('ps', 7)], True)
                act(ld3[:48, c * 128:(c + 1) * 128], ps[:48, 7, 0:128], AF.Copy, [('ps', 7)], ['ld2'])
            P.dma('o2', lambda e: e.dma_start(out=sscs.ap(), in_=ld3[:48, :]), reads=['ld2'], writes=['sscs'])
            for c in range(8):
                def evz(bi, c0, n, pa, pk):
                    nn = min(n, NTM - c0)
                    act(szT[:, c, c0:c0 + nn], pa[:, :nn], AF.Silu, [pk], ['szT'])
                win_fm(16 + c, xT, blocks, evz)
            kw, w = ws.next(('win', 36))
            w = w[:, 0:256].rearrange("p (k c) -> p k c", k=16)
            for ti, (r0, n) in enumerate(tiles):
                t = 4 + ti % 2
                for k in range(16):
                    mm(ps[:n, t, 0:16], xT[:, k, r0:r0 + n], w[:, k, :], k == 0, k == 15, [kw, 'xT'], [('ps', t)], k == 15)
                tt(dtp[:n, ti, :], ps[:n, t, 0:16], cst[:n, C_DTB:C_DTB + 16], ALU.add, [('ps', t), 'cst'], ['dtp'])
            for (a, b, n) in ((0, 8, 128), (8, 9, 64)):
                act(dtp[:n, a:b, :], dtp[:n, a:b, :], AF.Exp, ['dtp'], ['dtp'])
                act(dtp[:n, a:b, :], dtp[:n, a:b, :], AF.Ln, ['dtp'], ['dtp'], bias=1.0)
            P.barrier()
            if stop == 2.3:
                return
            o3 = R0
            xs_tm = V(o3, BF16, [1024]); o3 += 512
            b_tm = V(o3, BF16, [256]); o3 += 128
            xdt = V(o3, BF16, [1024]); o3 += 512
            xdte = V(o3, BF16, [1024]); o3 += 512
            abuf = V(o3, F32, [16]); o3 += 16
            eab = V(o3, F32, [48]); o3 += 48
            oAU = o3
            AU = V(o3, F32, [16, 128]); o3 += 2048
            oWT = o3
            WT = V(o3, BF16, [16, 128]); o3 += 1024
            xsb = V(o3, BF16, [128]); o3 += 64
            oEt = o3
            Et = V(o3, F32, [1, 512]); o3 += 512
            cbm = V(o3, F32, [2, 128]); o3 += 256
            negU = V(o3, F32, [128]); o3 += 128
            Hs = V(o3, BF16, [1024]); o3 += 512
            H = V(o3, F32, [1024]); o3 += 1024
            yg = V(o3, F32, [8, 128]); o3 += 1024
            yoS = V(o3, F32, [8, 64]); o3 += 512
            assert o3 <= R0 + 8 * NTX, (o3, R0)
            rs = V(oTail, F32, [2, 128])
            ysq = V(oTail + 256, F32, [128])
            cdc = V(oTail + 384, F32, [16])
            aex = V(oTail + 400, F32, [128])

            def to_tm(t0, n):
                pT = PSB(0, 2, [10, 128])
                for c in range(10):
                    if c < 8:
                        act(xsb[:, :n], xsT[:, c, t0:t0 + n], AF.Copy, ['xsT'], ['xsb'])
                        src = xsb[:, :n]
                    else:
                        src = bcT[:, c - 8, t0:t0 + n]
                    tr(pT[:n, c, :], src, identb, ['xsT', 'xsb'], [('ps', 0), ('ps', 1)], True)
                act(xs_tm[:n], pT[:n, 0:8, :].rearrange("p a b -> p (a b)"), AF.Copy, [('ps', 0), ('ps', 1)], ['xs_tm'])
                act(b_tm[:n], pT[:n, 8:10, :].rearrange("p a b -> p (a b)"), AF.Copy, [('ps', 0), ('ps', 1)], ['b_tm'])

            def scalars(dt_t, n, Um, Lm):
                tt(abuf[:n], dt_t, Aneg[:n], ALU.mult, ['dtp', 'Aneg'], ['abuf'])
                mm(ps[:n, 3, 0:16], Um[:n, :n], abuf[:n], True, True, ['abuf', 'cm'], [('ps', 3)], False)
                mm(ps[:n, 3, 16:32], Lm[:n, :n], abuf[:n], True, True, ['abuf', 'cm'], [('ps', 3)], False)
                mm(ps[:, 3, 32:48], onesf[:n, :], abuf[:n], True, True, ['abuf', 'cm'], [('ps', 3)], True)
                act(eab[:, 0:48], ps[:, 3, 0:48], AF.Exp, [('ps', 3)], ['eabuf'])
                tt(eab[:n, 16:32], eab[:n, 16:32], dt_t, ALU.mult, ['eabuf', 'dtp'], ['eabuf'])

            def mk_xdte(n):
                tt(xdte[:n].rearrange("p (h q) -> p h q", h=16), xs_tm[:n].rearrange("p (h q) -> p h q", h=16),
                   eab[:n, 16:32].unsqueeze(2).to_broadcast([n, 16, 64]), ALU.mult, ['xs_tm', 'eabuf'], ['xdte'])

            def state_step():
                for g in range(2):
                    mm(ps[:, 4 + g, :], b_tm[:, g * 128:(g + 1) * 128], xdte[:, g * 512:(g + 1) * 512], True, True,
                       ['b_tm', 'xdte'], [('ps', 4 + g)], True)
                H3 = H.rearrange("p (h q) -> p h q", h=16)
                tt(H3, H3, eab[:, 32:48].unsqueeze(2).to_broadcast([128, 16, 64]), ALU.mult, ['H', 'eabuf', 'Hs'], ['H'])
                for g in range(2):
                    tt(H[:, g * 512:(g + 1) * 512], H[:, g * 512:(g + 1) * 512], ps[:, 4 + g, :], ALU.add,
                       ['H', ('ps', 4 + g)], ['H'])

            Up, Lp = cm[:, M_U:M_U + 128], cm[:, M_L:M_L + 128]
            P.op('dve', lambda e: e.memset(H, 0.0), writes=['H'])
            for ci in range(8):
                to_tm(ci * 128, 128)
                scalars(dtp[:, ci, :], 128, Up, Lp)
                mk_xdte(128)
                state_step()
            Hin = V(oAU, F32, [1024])
            exchange('s', H, ex_i2, ex_o2, Hin, ['H'], ['Hin'])
            ts(H, Hin, flag, None, ALU.mult, None, ['Hin', 'cst', 'H'], ['H'])
            if stop == 2.4:
                P.barrier()
                return

            def ssd_chunk(t0, n, Um, Lm, NEGm, MSKm, yoff_fn):
                to_tm(t0, n)
                dt_t = dtp[:n, t0 // 128, :]
                scalars(dt_t, n, Um, Lm)
                tt(xdt[:n].rearrange("p (h q) -> p h q", h=16), xs_tm[:n].rearrange("p (h q) -> p h q", h=16),
                   dt_t.unsqueeze(2).to_broadcast([n, 16, 64]), ALU.mult, ['xs_tm', 'dtp'], ['xdt'])
                mk_xdte(n)
                for g in range(2):
                    mm(ps[:n, 2, g * 128:g * 128 + n], bcT[:, g, t0:t0 + n], bcT[:, 2 + g, t0:t0 + n], True, True, ['xsT'], [('ps', 2)], g == 1)
                tt(cbm[:n, :, :n], ps[:n, 2, 0:256].rearrange("p (g l) -> p g l", g=2)[:, :, :n],
                   MSKm[:n, :n].unsqueeze(1).to_broadcast([n, 2, n]), ALU.mult, [('ps', 2), 'cm'], ['cbm'])
                tt(AU[:n, :, :n], Um[:n, :n].unsqueeze(1).to_broadcast([n, 16, n]),
                   abuf[:n].unsqueeze(2).to_broadcast([n, 16, n]), ALU.mult, ['abuf', 'cm'], ['AU'])
                ts(negU[:n, :n], Um[:n, :n], -1.0, None, ALU.mult, None, ['cm'], ['negU'])
                for b4 in range(4):
                    bk = 4 + b4 % 2
                    pb = ps[:n, bk, :].rearrange("p (h l) -> p h l", h=4)[:, :, :n]
                    mm(pb, onesf[:n, :n], AU[:n, 4 * b4:4 * b4 + 4, :n], True, False, ['AU', 'cm'], [('ps', bk)], False)
                    mm(pb, negU[:n, :n], abuf[:n, 4 * b4:4 * b4 + 4].unsqueeze(2).to_broadcast([n, 4, n]), False, False,
                       ['negU', 'abuf'], [('ps', bk)], False)
                    mm(pb, identf[:n, :n], NEGm[:n, :n].unsqueeze(1).to_broadcast([n, 4, n]), False, True, ['cm'], [('ps', bk)], True)
                    Eb = Et[:n, 0, :].rearrange("p (h l) -> p h l", h=4)[:, :, :n]
                    act(Eb, pb, AF.Exp, [('ps', bk)], [('Et', 0)])
                    tt(WT[:n, 4 * b4:4 * b4 + 4, :n], Eb, cbm[:n, b4 // 2, :n].unsqueeze(1).to_broadcast([n, 4, n]), ALU.mult,
                       [('Et', 0), 'cbm'], ['WT'])
                for c in range(8):
                    bk = 6 + c % 2
                    pk = ('ps', bk)
                    pa = ps[:, bk, :]
                    for hh in range(2):
                        h = 2 * c + hh
                        mm(pa[64 * hh:64 * hh + 64, 0:n], xdt[:n, h * 64:(h + 1) * 64], WT[:n, h, :n], True, True,
                           ['xdt', 'WT'], [pk], False)
                        mm(pa[64 * hh:64 * hh + 64, 256:256 + n], onesf[:n, 0:64], AU[:n, h, :n], True, True, ['AU', 'cm'], [pk], hh == 1)
                    yo_ap, yo_keys = yoff_fn(c, pa, pk)
                    act(rs[:, 0, :n], pa[:, 256:256 + n], AF.Exp, [pk], ['rs'])
                    tt(rs[:, 0, :n], rs[:, 0, :n], yo_ap, ALU.mult, ['rs', pk] + yo_keys, ['rs'])
                    tt(rs[:, 0, :n], rs[:, 0, :n], pa[:, 0:n], ALU.add, ['rs', pk], ['rs'])
                    stt(rs[:, 0, :n], xsT[:, c, t0:t0 + n], cst[:, C_DS + c:C_DS + c + 1], rs[:, 0, :n], ALU.mult, ALU.add,
                        ['rs', 'xsT', 'cst'], ['rs'])
                    tt(yg[:, c, :n], rs[:, 0, :n], szT[:, c, t0:t0 + n], ALU.mult, ['rs', 'szT'], ['yg'])
                for g in range(2):
                    for c4 in range(4):
                        c = 4 * g + c4
                        act(ysq[:, :n], yg[:, c, :n], AF.Square, ['yg'], ['ysq'])
                        mm(ps[:, 3, 64:64 + n], onesf, ysq[:, :n], c4 == 0, c4 == 3, ['ysq', 'cm'], [('ps', 3)], True)
                    ts(rs[:, 1, :n], ps[:, 3, 64:64 + n], 1.0 / 512, EPS, ALU.mult, ALU.add, [('ps', 3)], ['rs1'])
                    act(rs[:, 1, :n], rs[:, 1, :n], AF.Sqrt, ['rs1'], ['rs1'])
                    P.op('dve', lambda e: e.reciprocal(out=rs[:, 1, :n], in_=rs[:, 1, :n]), reads=['rs1'], writes=['rs1'])
                    for c4 in range(4):
                        c = 4 * g + c4
                        stt(ynT[:, c, t0:t0 + n], yg[:, c, :n], cst[:, C_NG + c:C_NG + c + 1], rs[:, 1, :n], ALU.mult, ALU.mult,
                            ['yg', 'rs1', 'cst'], ['ynT'])

            for ci in range(8):
                t0 = ci * 128
                act(Hs, H, AF.Copy, ['H'], ['Hs'])

                def yoff_p(c, pa, pk):
                    mm(pa[:, 128:256], Hs[:, c * 128:(c + 1) * 128], bcT[:, 2 + c // 4, t0:t0 + 128], True, True, ['Hs', 'xsT'], [pk], True)
                    return pa[:, 128:256], []
                ssd_chunk(t0, 128, Up, Lp, cm[:, M_NEG:M_NEG + 128], cm[:, M_MSK:M_MSK + 128], yoff_p)
                state_step()
            P.barrier()
            stg = V(oAU, F32, [8, 128])
            for c in range(8):
                tr(ps[:, 4 + c % 2, 0:128], H[:, c * 128:(c + 1) * 128], identf, ['H', 'cm'], [('ps', 4 + c % 2)], True)
                act(stg[:, c, :], ps[:, 4 + c % 2, 0:128], AF.Copy, [('ps', 4 + c % 2)], ['stg'])
            P.dma('o3', lambda e: e.dma_start(out=sstp.ap().rearrange("(c p) n -> p c n", p=128), in_=stg), reads=['stg'], writes=['sstp'])
            P.barrier()
            if stop == 2.5:
                return
            Sin = V(oAU, F32, [2, 1024])
            STb = V(oWT, BF16, [2, 1024])
            CmT = V(oEt, BF16, [2, 1024])
            xdm = V(oWT, BF16, [1024])
            T0 = 1024
            bmv = cm[:, M_BM:M_BM + 1024].rearrange("p (s t) -> p s t", s=16)
            for g in range(2):
                tt(CmT[:, g, :].rearrange("p (s t) -> p s t", s=16), bcT[:, 2 + g, T0:T0 + 64].unsqueeze(1).to_broadcast([128, 16, 64]),
                   bmv, ALU.mult, ['xsT', 'cm'], ['CmT'])

            def load_state(s_):
                sl = s_ % 2
                P.dma(('sin', sl), lambda e: e.dma_start(out=Sin[:, sl, :], in_=bass.AP(sst, s_ * 1024 * 128, [[1024, 128], [1, 1024]])),
                      writes=[('sin', sl)])
                return sl
            yacc_ps = ps[:, 2, :].rearrange("p (c l) -> p c l", c=8)
            for s_ in range(16):
                sl = load_state(s_)
                for r in range(8):
                    tr(ps[:, 4 + r // 4, (r % 4) * 128:(r % 4 + 1) * 128], Sin[:, sl, r * 128:(r + 1) * 128], identf,
                       [('sin', sl), 'cm'], [('ps', 4 + r // 4)], r % 4 == 3)
                for hb_ in range(2):
                    act(STb[:, sl, :].rearrange("p (q r) -> p r q", r=8)[:, 4 * hb_:4 * hb_ + 4, :],
                        ps[:, 4 + hb_, :].rearrange("p (r q) -> p r q", r=4), AF.Copy, [('ps', 4 + hb_)], [('STb', sl)])
                for c in range(8):
                    mm(yacc_ps[:, c, :], STb[:, sl, c * 128:(c + 1) * 128], CmT[:, c // 4, s_ * 64:(s_ + 1) * 64],
                       s_ == 0 and c == 0, s_ == 15 and c == 7, [('STb', sl), 'CmT'], [('ps', 2)], c == 7)
            act(yoS, yacc_ps, AF.Copy, [('ps', 2)], ['yoS'])
            P.barrier()

            def yoff_s(c, pa, pk):
                return yoS[:, c, :], ['yoS']
            ssd_chunk(T0, 64, cm[:, M_UB:M_UB + 128], cm[:, M_LB:M_LB + 128], cm[:, M_NEGB:M_NEGB + 128],
                      cm[:, M_MSKB:M_MSKB + 128], yoff_s)
            P.barrier()
            P.op('dve', lambda e: e.tensor_copy(out=aex[:64].rearrange("p (h r) -> p h r", h=16),
                                                in_=abuf[:64].unsqueeze(2).to_broadcast([64, 16, 8])), reads=['abuf'], writes=['aex'])
            mm(ps[:, 3, 0:16], aex[:64, :], cm[:64, M_BMC:M_BMC + 16], True, True, ['aex', 'cm'], [('ps', 3)], True)
            act(cdc, ps[:, 3, 0:16], AF.Exp, [('ps', 3)], ['cdc'])
            for s_ in range(16):
                sl = load_state(s_)
                ts(xdm[:64], xdte[:64], cm[:64, M_BMC + s_:M_BMC + s_ + 1], None, ALU.mult, None, ['xdte', 'cm'], ['xdm'])
                xv = xdm[:64].rearrange("p (q r) -> p r q", r=8)
                for r in range(8):
                    bk = 4 + r // 4
                    for g in range(2):
                        mm(ps[64 * g:64 * g + 64, bk, (r % 4) * 128:(r % 4 + 1) * 128], xv[:, r, 64 * g:64 * g + 64],
                           b_tm[:64, g * 128:(g + 1) * 128], True, True, ['xdm', 'b_tm'], [('ps', bk)], (r % 4 == 3) and g == 1)
                for hb_ in range(2):
                    stt(Sin[:, sl, hb_ * 512:(hb_ + 1) * 512], Sin[:, sl, hb_ * 512:(hb_ + 1) * 512], cdc[:, s_:s_ + 1],
                        ps[:, 4 + hb_, :], ALU.mult, ALU.add, [('sin', sl), ('ps', 4 + hb_), 'cdc'], [('sin', sl)])
                P.dma(('sout', sl), lambda e: e.dma_start(out=bass.AP(ssts, s_ * 1024 * 128, [[1024, 128], [1, 1024]]), in_=Sin[:, sl, :]),
                      reads=[('sin', sl)], writes=[('ssts', s_)])
            P.barrier()
            if stop == 2.6:
                return
            mA = V(R0, F32, [4, 2048])
            mB = V(oA, F32, [5, 2048])
            macc = lambda ti: (mA[:, ti, :] if ti < 4 else mB[:, ti - 4, :])
            linear_tm('wout', lambda k: (aT[:, k, :] if k < 8 else ynT[:, k - 8, :]), TILES, macc)
            P.barrier()
            postnorm(macc, TILES, 1, 1.0, lambda r0, n: rows(x1, r0, n), lambda r0, n: rows(yout if stop == 2 else x2, r0, n), oA + 10240)
            P.barrier()

        mixer()
        if stop is not None and 2 <= stop < 3:
            return nc, ws.req

        def attention():
            SC = 512 ** -0.5
            xT = V(R0, BF16, [16, NTM])
            qT = V(R0 + 8704, BF16, [16, NTM])
            oX = R0 + 18432
            prenorm(lambda r0, n: rows(x2, r0, n), TILES, cst[:, C_PRE['xa']:C_PRE['xa'] + 16], xT, oX)
            P.barrier()
            mT = V(oX, BF16, [16, 256]); o2 = oX + 2048
            prenorm(lambda r0, n: rows(mem, r0, n), [(0, 128), (128, 128)], cst[:, C_PRE['mem']:C_PRE['mem'] + 16], mT, o2)
            P.barrier()
            Kn = V(o2, F32, [2, 2048]); o2 += 4096
            Vn = V(o2, F32, [2, 2048]); o2 += 4096
            for j in range(16):
                def evq(bi, c0, n, pa, pk):
                    act(qT[:, j, c0:c0 + n], pa, AF.Copy, [pk], ['qT'])
                win_fm(j, xT, BLOCKS, evq, name='xq')
            for name, dstt, outd in (('xk', Kn, kout), ('xv', Vn, vout)):
                linear_tm(name, lambda k: mT[:, k, :], [(0, 128), (128, 128)], lambda ti, dstt=dstt: dstt[:, ti, :])
                P.dma(('okv', name), lambda e: e.dma_start(out=outd.ap().rearrange("(t p) d -> p t d", p=128), in_=dstt),
                      reads=[('yacc', 0), ('yacc', 1)], writes=[('okv', name)])
            P.barrier()
            o3 = R0
            KT = V(o3, BF16, [16, 256]); o3 += 2048
            Vb = V(o3, BF16, [2, 2048]); o3 += 2048
            Pn = V(o3, BF16, [4, 256]); o3 += 512
            Pf = V(o3, F32, [4, 256]); o3 += 1024
            PT = V(o3, BF16, [8, 128]); o3 += 512
            Pms = V(o3, BF16, [8, 64]); o3 += 256
            qm = V(o3, BF16, [16, 64]); o3 += 512
            sm4 = V(o3, F32, [16]); o3 += 16
            assert o3 <= R0 + 8704
            otm = V(R0 + 17408, BF16, [2048])
            oT = V(oX, BF16, [16, NTM])
            stg = V(oX + 8704, F32, [2, 2048])

            def make_KT(src, keys):
                for half in range(2):
                    for dc in range(8):
                        for mt in range(2):
                            d = half * 8 + dc
                            bk = 4 + dc // 2
                            tr(ps[:, bk, (dc % 2) * 256 + mt * 128:(dc % 2) * 256 + mt * 128 + 128], src[:, mt, d * 128:(d + 1) * 128],
                               identf, keys + ['cm'], [('ps', bk)], (dc % 2 == 1) and mt == 1)
                    for b2 in range(4):
                        act(KT[:, half * 8 + 2 * b2:half * 8 + 2 * b2 + 2, :], ps[:, 4 + b2, :].rearrange("p (a m) -> p a m", a=2),
                            AF.Copy, [('ps', 4 + b2)], ['KT'])

            def softmax_rows(n, nh, sp_views, spk):
                for h in range(nh):
                    sv = sp_views[h]
                    P.op('dve', lambda e: e.tensor_reduce(out=sm4[:n, h:h + 1], in_=sv, axis=AX.X, op=ALU.max), reads=spk, writes=['sm4'])
                    ts(sm4[:n, 4 + h:5 + h], sm4[:n, h:h + 1], -SC, None, ALU.mult, None, ['sm4'], ['sm4'])
                    act(Pf[:n, h, :], sv, AF.Exp, spk + ['sm4'], ['Pf'], scale=SC, bias=sm4[:n, 4 + h:5 + h], accum_out=sm4[:n, 8 + h:9 + h])
                    P.op('dve', lambda e: e.reciprocal(out=sm4[:n, 12 + h:13 + h], in_=sm4[:n, 8 + h:9 + h]), reads=['sm4'], writes=['sm4'])
                    ts(Pn[:n, h, :], Pf[:n, h, :], sm4[:n, 12 + h:13 + h], None, ALU.mult, None, ['Pf', 'sm4'], ['Pn'])

            make_KT(Kn, [('yacc', 0), ('yacc', 1)])
            act(Vb, Vn, AF.Copy, [('yacc', 0), ('yacc', 1)], ['Vb'])
            P.barrier()
            for ti in range(8):
                t0 = ti * 128
                for hp in range(2):
                    views = []
                    for hh in range(2):
                        h = 2 * hp + hh
                        for dc in range(4):
                            mm(ps[:, hp, hh * 256:(hh + 1) * 256], qT[:, 4 * h + dc, t0:t0 + 128], KT[:, 4 * h + dc, :], dc == 0, dc == 3,
                               ['qT', 'KT'], [('ps', hp)], dc == 3)
                        views.append(ps[:, hp, hh * 256:(hh + 1) * 256])
                    for hh in range(2):
                        h = 2 * hp + hh
                        sv = views[hh]
                        P.op('dve', lambda e: e.tensor_reduce(out=sm4[:, h:h + 1], in_=sv, axis=AX.X, op=ALU.max), reads=[('ps', hp)], writes=['sm4'])
                        ts(sm4[:, 4 + h:5 + h], sm4[:, h:h + 1], -SC, None, ALU.mult, None, ['sm4'], ['sm4'])
                        act(Pf[:, h, :], sv, AF.Exp, [('ps', hp), 'sm4'], ['Pf'], scale=SC, bias=sm4[:, 4 + h:5 + h], accum_out=sm4[:, 8 + h:9 + h])
                        P.op('dve', lambda e: e.reciprocal(out=sm4[:, 12 + h:13 + h], in_=sm4[:, 8 + h:9 + h]), reads=['sm4'], writes=['sm4'])
                        ts(Pn[:, h, :], Pf[:, h, :], sm4[:, 12 + h:13 + h], None, ALU.mult, None, ['Pf', 'sm4'], ['Pn'])
                pT = PSB(2, 1, [8, 128])
                for h in range(4):
                    for mt in range(2):
                        tr(pT[:, 2 * h + mt, :], Pn[:, h, mt * 128:(mt + 1) * 128], identb, ['Pn'], [('ps', 2)], h == 3 and mt == 1)
                act(PT, pT, AF.Copy, [('ps', 2)], ['PT'])
                for h in range(4):
                    bk = 4 + h
                    for dc in range(4):
                        for mt in range(2):
                            mm(ps[:, bk, dc * 128:(dc + 1) * 128], Vb[:, mt, (4 * h + dc) * 128:(4 * h + dc + 1) * 128], PT[:, 2 * h + mt, :],
                               mt == 0, mt == 1, ['Vb', 'PT'], [('ps', bk)], dc == 3 and mt == 1)
                    act(oT[:, 4 * h:4 * h + 4, t0:t0 + 128], ps[:, bk, :].rearrange("p (a t) -> p a t", a=4), AF.Copy, [('ps', bk)], ['oT'])
            P.barrier()
            T0 = 1024
            bmv = cm[:, M_BM:M_BM + 1024].rearrange("p (s t) -> p s t", s=16)
            sp = [ps[:64, 0, 0:256], ps[:64, 0, 256:512], ps[:64, 1, 0:256], ps[:64, 1, 256:512]]
            for s_ in range(16):
                P.dma('kst', lambda e: e.dma_start(out=stg, in_=bass.AP(ck, s_ * 256 * 2048, [[2048, 128], [128 * 2048, 2], [1, 2048]])),
                      writes=['stg'])
                make_KT(stg, ['stg'])
                tt(qm, qT[:, :, T0:T0 + 64], bmv[:, s_, :].unsqueeze(1).to_broadcast([128, 16, 64]), ALU.mult, ['qT', 'cm'], ['qm'])
                for h in range(4):
                    for dc in range(4):
                        first = (s_ == 0 and dc == 0 and h % 2 == 0)
                        last = (s_ == 15 and dc == 3 and h % 2 == 1)
                        mm(sp[h], qm[:, 4 * h + dc, :], KT[:, 4 * h + dc, :], first, last, ['qm', 'KT'], [('ps', h // 2)], dc == 3)
            softmax_rows(64, 4, sp, [('ps', 0), ('ps', 1)])
            pT = PSB(2, 1, [8, 64])
            for h in range(4):
                for mt in range(2):
                    tr(pT[:, 2 * h + mt, :], Pn[:64, h, mt * 128:(mt + 1) * 128], identb[:64, :64], ['Pn'], [('ps', 2)], h == 3 and mt == 1)
            act(PT[:, :, 0:64], pT, AF.Copy, [('ps', 2)], ['PT'])
            for s_ in range(16):
                P.dma('kst', lambda e: e.dma_start(out=stg, in_=bass.AP(cv, s_ * 256 * 2048, [[2048, 128], [128 * 2048, 2], [1, 2048]])),
                      writes=['stg'])
                act(Vb, stg, AF.Copy, ['stg'], ['Vb'])
                tt(Pms, PT[:, :, 0:64], bmv[:, s_, :].unsqueeze(1).to_broadcast([128, 8, 64]), ALU.mult, ['PT', 'cm'], ['Pms'])
                for h in range(4):
                    for mt in range(2):
                        mm(ps[:64, 4 + h, :], Pms[:, 2 * h + mt, :], Vb[:, mt, h * 512:(h + 1) * 512],
                           s_ == 0 and mt == 0, s_ == 15 and mt == 1, ['Pms', 'Vb'], [('ps', 4 + h)], mt == 1)
            for h in range(4):
                act(otm[:64, h * 512:(h + 1) * 512], ps[:64, 4 + h, :], AF.Copy, [('ps', 4 + h)], ['otm'])
            pT2 = PSB(0, 2, [16, 64])
            for k in range(16):
                tr(pT2[:, k, :], otm[:64, k * 128:(k + 1) * 128], identb[:64, :64], ['otm'], [('ps', 0), ('ps', 1)], k == 15)
            act(oT[:, :, T0:T0 + 64], pT2, AF.Copy, [('ps', 0), ('ps', 1)], ['oT'])
            P.barrier()
            aacc = V(R0, F32, [9, 2048])
            assert R0 + 9 * 2048 <= oX
            linear_tm('xo', lambda k: oT[:, k, :], TILES, lambda ti: aacc[:, ti, :])
            P.barrier()
            postnorm(lambda ti: aacc[:, ti, :], TILES, 2, 1.0, lambda r0, n: rows(x2, r0, n), lambda r0, n: rows(yout if stop == 3 else x3, r0, n), oX)
            P.barrier()

        attention()
        if stop == 3:
            return nc, ws.req

        ffn('f2', 'f2', 3, lambda r0, n: rows(x3, r0, n), lambda r0, n: rows(yout, r0, n))
        P.barrier()
    return nc, ws.req


_CACHE = {}


def kernel(**inp):
    inp = {k: np.asarray(v) for k, v in inp.items()}
    if 'prog' not in _CACHE:
        _, seq = build(None)
        nc, _ = build(seq)
        _CACHE['prog'] = nc
    nc = _CACHE['prog']
    wall = build_wall(inp).reshape(-1, 2048)
    cmat = build_cmat()
    postg = np.stack([inp['ffn1_post_g'][0], inp['mix_post_g'][0], inp['xattn_post_g'][0], inp['ffn2_post_g'][0]]).astype(np.float32)
    in_maps = []
    for cid in range(8):
        b, half = cid // 2, cid % 2
        xin = np.concatenate([inp['x_prompt'][b, half * 1024:(half + 1) * 1024],
                              inp['x_sample'][cid * 16:(cid + 1) * 16].reshape(64, 2048)], axis=0)
        in_maps.append({
            "xin": np.ascontiguousarray(xin, np.float32),
            "mem": np.ascontiguousarray(inp['mem_prompt'][b]),
            "ck": np.ascontiguousarray(inp['cache_mem_k'][0, cid * 16:(cid + 1) * 16].reshape(16 * 256, 2048)),
            "cv": np.ascontiguousarray(inp['cache_mem_v'][0, cid * 16:(cid + 1) * 16].reshape(16 * 256, 2048)),
            "sconv": np.ascontiguousarray(inp['state_conv'][0, cid * 16:(cid + 1) * 16].reshape(16 * 30, 1024)),
            "sssc": np.ascontiguousarray(inp['state_ssm_conv'][0, cid * 16:(cid + 1) * 16].reshape(16 * 3, 1536)),
            "sst": np.ascontiguousarray(inp['state_ssm'][0, cid * 16:(cid + 1) * 16].reshape(16 * 1024, 128)),
            "wall": wall,
            "cst": build_consts(inp, half),
            "cmat": cmat,
            "postg": postg,
        })
    res = run_bass_kernel_spmd(nc, in_maps, core_ids=list(range(8))).results
    yp = np.empty((4, 2048, 2048), np.float32)
    ys = np.empty((128, 4, 2048), np.float32)
    nk = np.empty((1, 4, 256, 4, 512), np.float32)
    nv = np.empty((1, 4, 256, 4, 512), np.float32)
    ncp = np.empty((1, 4, 30, 1024), np.float32)
    nscp = np.empty((1, 4, 3, 1536), np.float32)
    nsp = np.empty((1, 4, 16, 64, 128), np.float32)
    ncs = np.empty((1, 128, 30, 1024), np.float32)
    nscs = np.empty((1, 128, 3, 1536), np.float32)
    nss = np.empty((1, 128, 16, 64, 128), np.float32)
    for cid in range(8):
        b, half = cid // 2, cid % 2
        r = res[cid]
        yp[b, half * 1024:(half + 1) * 1024] = r["yout"][0:1024]
        ys[cid * 16:(cid + 1) * 16] = r["yout"][1024:1088].reshape(16, 4, 2048)
        ncs[0, cid * 16:(cid + 1) * 16] = r["convs"].reshape(16, 30, 1024)
        nscs[0, cid * 16:(cid + 1) * 16] = r["sscs"].reshape(16, 3, 1536)
        nss[0, cid * 16:(cid + 1) * 16] = r["ssts"].reshape(16, 16, 64, 128)
        if half == 0:
            nk[0, b] = r["kout"].reshape(256, 4, 512)
            nv[0, b] = r["vout"].reshape(256, 4, 512)
        else:
            ncp[0, b] = r["convp"]
            nscp[0, b] = r["sscp"]
            nsp[0, b] = r["sstp"].reshape(16, 64, 128)
    return (yp, ys, nk, nv, ncp, nscp, nsp, ncs, nscs, nss)
```

```python
from contextlib import ExitStack
import numpy as np
import concourse.bass as bass
import concourse.mybir as mybir
from concourse.bass_utils import run_bass_kernel_spmd

F32 = mybir.dt.float32
BF16 = mybir.dt.bfloat16
AF = mybir.ActivationFunctionType
ALU = mybir.AluOpType
AX = mybir.AxisListType

SAME_ENG_SYNC = True
EPS = 1e-6
DFF = 5504
NFC = 43


class Prog:
    CE = ('pe', 'act', 'dve', 'pool')

    def __init__(self, nc, stack):
        self.nc = nc
        self.stack = stack
        self.eng = {'pe': nc.tensor, 'act': nc.scalar, 'dve': nc.vector,
                    'pool': nc.gpsimd, 'sp': nc.sync}
        self.sem = {e: stack.enter_context(nc.semaphore('s_' + e)) for e in self.CE}
        self.cnt = {e: 0 for e in self.CE}
        self.pend = {e: False for e in self.CE}
        self.dsem = {}
        self.dcnt = {}
        self.last_w = {}
        self.readers = {}
        self.seen = {e: {} for e in self.eng}

    def _tok_sem(self, tok):
        if tok[0] == 'e':
            return self.sem[tok[1]], tok[2]
        return self.dsem[tok[1]], tok[2]

    def _wait(self, eng, toks):
        for tok in toks:
            if tok[0] == 'e' and tok[1] == eng:
                if eng == 'pe' or not SAME_ENG_SYNC:
                    continue
            key = (tok[0], tok[1])
            if self.seen[eng].get(key, 0) >= tok[2]:
                continue
            self.seen[eng][key] = tok[2]
            sem, val = self._tok_sem(tok)
            self.eng[eng].wait_ge(sem, val)

    def _deps(self, reads, writes):
        deps = []
        for r in reads:
            t = self.last_w.get(r)
            if t is not None:
                deps.append(t)
        for w in writes:
            t = self.last_w.get(w)
            if t is not None:
                deps.append(t)
            deps.extend(self.readers.get(w, ()))
        return deps

    def _commit(self, tok, reads, writes):
        for w in writes:
            self.last_w[w] = tok
            self.readers[w] = []
        for r in reads:
            if r in writes:
                continue
            self.readers.setdefault(r, []).append(tok)

    def op(self, eng, fn, reads=(), writes=(), sig=True):
        self._wait(eng, self._deps(reads, writes))
        inst = fn(self.eng[eng])
        if sig:
            self.cnt[eng] += 1
            inst.then_inc(self.sem[eng], 1)
            tok = ('e', eng, self.cnt[eng])
            self.pend[eng] = False
        else:
            assert eng == 'pe'
            tok = ('e', eng, self.cnt[eng] + 1)
            self.pend[eng] = True
        self._commit(tok, reads, writes)
        return tok

    def dma(self, key, fn, reads=(), writes=(), q='sp'):
        if key not in self.dsem:
            self.dsem[key] = self.stack.enter_context(
                self.nc.semaphore('d_%d' % len(self.dsem)))
            self.dcnt[key] = 0
        self._wait(q, self._deps(reads, writes))
        insts = fn(self.eng[q])
        if not isinstance(insts, (list, tuple)):
            insts = [insts]
        for i in insts:
            i.then_inc(self.dsem[key], 16)
            self.dcnt[key] += 16
        tok = ('d', key, self.dcnt[key])
        self._commit(tok, reads, writes)
        return tok

    def barrier(self):
        for e in self.CE:
            assert not self.pend[e], e
        toks = [('e', e, self.cnt[e]) for e in self.CE if self.cnt[e] > 0]
        toks += [('d', k, v) for k, v in self.dcnt.items() if v > 0]
        for e in self.eng:
            self._wait(e, [t for t in toks if not (t[0] == 'e' and t[1] == e)])
        self.last_w = {}
        self.readers = {}


def piece_index():
    idx = {}
    n = 0
    for f in ('f1', 'f2'):
        for kind in ('g', 'u', 'd'):
            for j in range(NFC):
                idx[(f, kind, j)] = n
                n += 1
    for j in range(37):
        idx[('win', j)] = n
        n += 1
    for nm in ('wout', 'xq', 'xk', 'xv', 'xo'):
        for j in range(16):
            idx[(nm, j)] = n
            n += 1
    return idx, n


def fm_piece(W, j, width=128):
    blk = W[:, j * width:(j + 1) * width].reshape(16, 128, width).transpose(1, 0, 2)
    out = np.zeros((128, 2048), np.float32)
    out[:, :16 * width] = blk.reshape(128, 16 * width)
    return out


def tm_piece(W, j):
    nb, half = j // 2, j % 2
    blk = W[half * 1024:(half + 1) * 1024, nb * 256:(nb + 1) * 256].reshape(8, 128, 256).transpose(1, 0, 2)
    return np.ascontiguousarray(blk).reshape(128, 2048)


def build_wall(inp):
    idx, n = piece_index()
    wall = np.empty((n, 128, 2048), np.float32)
    for f, pre in (('f1', 'ffn1'), ('f2', 'ffn2')):
        Wg, Wu, Wd = inp[pre + '_w_gate'][0], inp[pre + '_w_up'][0], inp[pre + '_w_down'][0]
        for j in range(NFC):
            wall[idx[(f, 'g', j)]] = fm_piece(Wg, j)
            wall[idx[(f, 'u', j)]] = fm_piece(Wu, j)
            wall[idx[(f, 'd', j)]] = Wd[j * 128:(j + 1) * 128, :]
    Win = inp['w_in'][0]
    for j in range(36):
        wall[idx[('win', j)]] = fm_piece(Win, j)
    wall[idx[('win', 36)]] = fm_piece(Win[:, 4608:4624], 0, 16)
    for j in range(16):
        wall[idx[('wout', j)]] = tm_piece(inp['w_out'][0], j)
        wall[idx[('xq', j)]] = fm_piece(inp['w_xq'][0], j)
        wall[idx[('xk', j)]] = tm_piece(inp['w_xk'][0], j)
        wall[idx[('xv', j)]] = tm_piece(inp['w_xv'][0], j)
        wall[idx[('xo', j)]] = tm_piece(inp['w_xo'][0], j)
    return wall


C_PRE = {'f1': 0, 'mix': 16, 'xa': 32, 'f2': 48, 'mem': 64}
C_CW, C_CB, C_LG, C_LB = 80, 328, 336, 344
C_SW, C_SB, C_NG, C_DS, C_DTB, C_AL, C_FLAG = 352, 400, 412, 420, 428, 444, 460
NCST = 464
M_ID, M_U, M_L, M_ONE, M_NEG, M_MSK = 0, 128, 256, 384, 512, 640
M_UB, M_LB, M_NEGB, M_MSKB, M_SAME, M_BM = 768, 896, 1024, 1152, 1280, 1408
M_BMC = 2432
NCM = 2448


def build_consts(inp, half):
    c = np.zeros((128, NCST), np.float32)

    def col16(v):
        return v.reshape(16, 128).T
    for nm, key in (('f1', 'ffn1_pre_g'), ('mix', 'mix_pre_g'), ('xa', 'xattn_pre_g'),
                    ('f2', 'ffn2_pre_g'), ('mem', 'mem_norm_g')):
        c[:, C_PRE[nm]:C_PRE[nm] + 16] = col16(inp[key][0])
    cw = inp['conv_w'][0]
    c[:, C_CW:C_CW + 248] = cw.T.reshape(8, 128, 31).transpose(1, 0, 2).reshape(128, 248)
    c[:, C_CB:C_CB + 8] = inp['conv_b'][0].reshape(8, 128).T
    c[:, C_LG:C_LG + 8] = inp['conv_ln_g'][0].reshape(8, 128).T
    c[:, C_LB:C_LB + 8] = inp['conv_ln_b'][0].reshape(8, 128).T
    sw = inp['ssm_conv_w'][0]
    c[:, C_SW:C_SW + 48] = sw.T.reshape(12, 128, 4).transpose(1, 0, 2).reshape(128, 48)
    c[:, C_SB:C_SB + 12] = inp['ssm_conv_b'][0].reshape(12, 128).T
    c[:, C_NG:C_NG + 8] = inp['ssm_norm_g'][0].reshape(8, 128).T
    c[:, C_DS:C_DS + 8] = np.repeat(inp['d_skip'][0], 64).reshape(8, 128).T
    c[:, C_DTB:C_DTB + 16] = inp['dt_bias'][0][None, :]
    c[:, C_AL:C_AL + 16] = inp['a_log'][0][None, :]
    c[:, C_FLAG] = float(half)
    return c


def build_cmat():
    m = np.zeros((128, NCM), np.float32)
    j = np.arange(128)[:, None]
    l = np.arange(128)[None, :]
    m[:, M_ID:M_ID + 128] = (j == l)
    m[:, M_U:M_U + 128] = (j <= l)
    m[:, M_L:M_L + 128] = (j > l)
    m[:, M_ONE:M_ONE + 128] = 1.0
    m[:, M_NEG:M_NEG + 128] = np.where(l < j, -30000.0, 0.0)
    m[:, M_MSK:M_MSK + 128] = (l >= j)
    same = (j // 4 == l // 4) & (j < 64) & (l < 64)
    m[:, M_UB:M_UB + 128] = same & (j <= l)
    m[:, M_LB:M_LB + 128] = same & (j > l)
    m[:, M_NEGB:M_NEGB + 128] = np.where(same & (l >= j), 0.0, -30000.0)
    m[:, M_MSKB:M_MSKB + 128] = same & (l >= j)
    m[:, M_SAME:M_SAME + 128] = same
    bm = (np.arange(64)[None, :] // 4 == np.arange(16)[:, None]).astype(np.float32)
    m[:, M_BM:M_BM + 1024] = bm.reshape(1, 1024)
    m[:64, M_BMC:M_BMC + 16] = bm.T
    return m


NTM = 1088
NTX = 1118
ARENA = 49152


def build(seq, stop=None):
    nc = bass.Bass("TRN2", target_bir_lowering=False)
    pidx, npieces = piece_index()

    def DT(name, shape, kind=None):
        if kind is None:
            return nc.dram_tensor(name, shape, F32)
        return nc.dram_tensor(name, shape, F32, kind=kind)
    xin = DT("xin", [NTM, 2048], "ExternalInput")
    mem = DT("mem", [256, 2048], "ExternalInput")
    ck = DT("ck", [16 * 256, 2048], "ExternalInput")
    cv = DT("cv", [16 * 256, 2048], "ExternalInput")
    sconv = DT("sconv", [16 * 30, 1024], "ExternalInput")
    sssc = DT("sssc", [16 * 3, 1536], "ExternalInput")
    sst = DT("sst", [16 * 1024, 128], "ExternalInput")
    wall = DT("wall", [npieces * 128, 2048], "ExternalInput")
    cst_d = DT("cst", [128, NCST], "ExternalInput")
    cmat_d = DT("cmat", [128, NCM], "ExternalInput")
    postg = DT("postg", [4, 2048], "ExternalInput")
    yout = DT("yout", [NTM, 2048], "ExternalOutput")
    kout = DT("kout", [256, 2048], "ExternalOutput")
    vout = DT("vout", [256, 2048], "ExternalOutput")
    convp = DT("convp", [30, 1024], "ExternalOutput")
    sscp = DT("sscp", [3, 1536], "ExternalOutput")
    sstp = DT("sstp", [1024, 128], "ExternalOutput")
    convs = DT("convs", [16 * 30, 1024], "ExternalOutput")
    sscs = DT("sscs", [16 * 3, 1536], "ExternalOutput")
    ssts = DT("ssts", [16 * 1024, 128], "ExternalOutput")
    x1 = DT("x1", [NTM, 2048])
    x2 = DT("x2", [NTM, 2048])
    x3 = DT("x3", [NTM, 2048])
    ex_i1 = DT("ex_i1", [128, 240])
    ex_o1 = DT("ex_o1", [256, 240])
    ex_i2 = DT("ex_i2", [128, 1024])
    ex_o2 = DT("ex_o2", [256, 1024])
    PAIRS = [[0, 1], [2, 3], [4, 5], [6, 7]]

    def rows(t, r0, n, c0=0, w=None):
        a = t.ap()
        w = a.shape[1] - c0 if w is None else w
        return a[r0:r0 + n, c0:c0 + w]

    st = ExitStack()
    with st:
        P = Prog(nc, st)
        arena = st.enter_context(nc.sbuf_tensor("arena", [128, ARENA], F32))
        ps = st.enter_context(nc.psum_tensor("ps", [128, 8, 512], F32))
        psb = ps[:].rearrange("p b f -> p (b f)").bitcast(BF16)

        def V(off, dt, shape):
            n = int(np.prod(shape))
            assert off + (n if dt == F32 else (n + 1) // 2) <= ARENA, (off, n)
            if dt == F32:
                v = arena[:, off:off + n]
            else:
                v = arena[:, off:off + (n + 1) // 2].bitcast(BF16)[:, 0:n]
            if len(shape) == 2:
                v = v.rearrange("p (a b) -> p a b", a=shape[0])
            elif len(shape) == 3:
                v = v.rearrange("p (a b c) -> p a b c", a=shape[0], b=shape[1])
            return v

        def PSB(bank, nbanks, shape):
            n = int(np.prod(shape))
            assert n <= nbanks * 1024
            v = psb[:, bank * 1024:bank * 1024 + n]
            if len(shape) == 2:
                v = v.rearrange("p (a b) -> p a b", a=shape[0])
            return v

        o = 0
        wst = V(o, F32, [3, 2048]); o += 6144
        wbf = V(o, BF16, [6, 2048]); o += 6144
        cst = V(o, F32, [NCST]); o += NCST
        cm = V(o, F32, [NCM]); o += NCM
        identb = V(o, BF16, [128]); o += 64
        ssm = V(o, F32, [64]); o += 64
        Aneg = V(o, F32, [16]); o += 16
        dtp = V(o, F32, [9, 16]); o += 144
        R0 = o
        identf = cm[:, M_ID:M_ID + 128]
        onesf = cm[:, M_ONE:M_ONE + 128]
        flag = cst[:, C_FLAG:C_FLAG + 1]

        class WS:
            NS, NB, DC, DD = 3, 6, 4, 6

            def __init__(self):
                self.i = 0
                self.req = []
                self.nd = 0
                self.ncst = 0

            def _dma(self, j, pid):
                s = j % self.NS
                P.dma(('wst', s), lambda e: e.dma_start(out=wst[:, s, :], in_=rows(wall, pid * 128, 128)),
                      writes=[('wst', s)])

            def _cast(self, j):
                s, t = j % self.NS, j % self.NB
                P.op('pool', lambda e: e.tensor_copy(out=wbf[:, t, :], in_=wst[:, s, :]),
                     reads=[('wst', s)], writes=[('wbf', t)])

            def next(self, name):
                pid = pidx[name]
                i = self.i
                self.i += 1
                self.req.append(pid)
                if seq is not None:
                    assert seq[i] == pid
                    src, dc, dd = seq, self.DC, self.DD
                else:
                    src, dc, dd = self.req, 0, 0
                last = len(src) - 1

                def ensure_cast(j):
                    while self.ncst <= j:
                        ensure_dma(self.ncst)
                        self._cast(self.ncst)
                        self.ncst += 1

                def ensure_dma(j):
                    while self.nd <= j:
                        if self.nd - self.NS >= 0:
                            ensure_cast(self.nd - self.NS)
                        self._dma(self.nd, src[self.nd])
                        self.nd += 1
                ensure_dma(min(i + dd, last))
                ensure_cast(min(i + dc, last))
                return ('wbf', i % self.NB), wbf[:, i % self.NB, :]
        ws = WS()

        def mm(out, lhsT, rhs, start, stop, reads, writes, sig):
            P.op('pe', lambda e: e.matmul(out, lhsT=lhsT, rhs=rhs, start=start, stop=stop),
                 reads=reads, writes=writes, sig=sig)

        def tr(out, in_, ident, reads, writes, sig):
            P.op('pe', lambda e: e.transpose(out=out, in_=in_, identity=ident),
                 reads=reads, writes=writes, sig=sig)

        def act(out, in_, func, reads, writes, **kw):
            if func == AF.Copy and 'bias' not in kw and 'accum_out' not in kw:
                sc = kw.get('scale', None)
                eng = 'dve' if ('PSUM' in str(in_.space).upper() or 'PSUM' in str(out.space).upper() or sc is not None) else 'pool'
                if sc is None:
                    P.op(eng, lambda e: e.tensor_copy(out=out, in_=in_), reads=reads, writes=writes)
                else:
                    P.op(eng, lambda e: e.tensor_scalar(out=out, in0=in_, scalar1=sc, scalar2=None, op0=ALU.mult),
                         reads=reads, writes=writes)
                return
            P.op('act', lambda e: e.activation(out=out, in_=in_, func=func, **kw), reads=reads, writes=writes)

        def tt(out, in0, in1, op, reads, writes):
            P.op('dve', lambda e: e.tensor_tensor(out=out, in0=in0, in1=in1, op=op), reads=reads, writes=writes)

        def ts(out, in0, s1, s2, op0, op1, reads, writes):
            if op1 is None:
                P.op('dve', lambda e: e.tensor_scalar(out=out, in0=in0, scalar1=s1, scalar2=None, op0=op0), reads=reads, writes=writes)
            else:
                P.op('dve', lambda e: e.tensor_scalar(out=out, in0=in0, scalar1=s1, scalar2=s2, op0=op0, op1=op1), reads=reads, writes=writes)

        def stt(out, in0, scalar, in1, op0, op1, reads, writes):
            P.op('dve', lambda e: e.scalar_tensor_tensor(out=out, in0=in0, scalar=scalar, in1=in1, op0=op0, op1=op1),
                 reads=reads, writes=writes)

        def exchange(tag, src_sb, ib, ob, dst_sb, rkeys, wkeys):
            P.dma(('exi', tag), lambda e: e.dma_start(out=ib.ap(), in_=src_sb), reads=rkeys, writes=[('ib', tag)])
            key = ('cc', tag)
            if key not in P.dsem:
                P.dsem[key] = st.enter_context(nc.semaphore('cc_' + tag))
                P.dcnt[key] = 0
            P._wait('pool', P._deps([('ib', tag)], [('ob', tag)]))
            ins = nc.gpsimd.collective_compute("AllGather", ALU.bypass, replica_groups=PAIRS,
                                               ins=[ib.ap().opt()], outs=[ob.ap().opt()])
            ins.then_inc(P.dsem[key])
            P.dcnt[key] += 1
            P._commit(('d', key, P.dcnt[key]), [('ib', tag)], [('ob', tag)])
            P.dma(('exo', tag), lambda e: e.dma_start(out=dst_sb, in_=ob.ap()[0:128, :]), reads=[('ob', tag)], writes=wkeys)

        P.dma('c0', lambda e: [e.dma_start(out=cst, in_=cst_d.ap()), e.dma_start(out=cm, in_=cmat_d.ap())],
              writes=['cst', 'cm'])
        P.op('dve', lambda e: e.tensor_copy(out=identb, in_=identf), reads=['cm'], writes=['identb'])
        act(Aneg, cst[:, C_AL:C_AL + 16], AF.Exp, ['cst'], ['Aneg'])
        ts(Aneg, Aneg, -1.0, None, ALU.mult, None, ['Aneg'], ['Aneg'])
        P.barrier()

        def rstd_chain(n, sb, s, eps_scale):
            ts(ssm[:n, sb + 1:sb + 2], ssm[:n, sb:sb + 1], eps_scale, EPS, ALU.mult, ALU.add, [('ss', s)], [('ss', s)])
            act(ssm[:n, sb + 2:sb + 3], ssm[:n, sb + 1:sb + 2], AF.Sqrt, [('ss', s)], [('ss', s)])
            P.op('dve', lambda e: e.reciprocal(out=ssm[:n, sb + 3:sb + 4], in_=ssm[:n, sb + 2:sb + 3]),
                 reads=[('ss', s)], writes=[('ss', s)])

        def prenorm(src, tiles, gcol, xT, tmp_off):
            xt = V(tmp_off, F32, [2, 2048])
            junk = V(tmp_off + 4096, BF16, [2048])
            xs = V(tmp_off + 5120, BF16, [2, 2048])
            for ti, (r0, n) in enumerate(tiles):
                s = ti % 2
                sb = 8 * s
                P.dma(('xt', s), lambda e: e.dma_start(out=xt[:n, s, :], in_=src(r0, n)), writes=[('xt', s)])
                act(junk[:n], xt[:n, s, :], AF.Square, [('xt', s)], ['junk', ('ss', s)], accum_out=ssm[:n, sb:sb + 1])
                rstd_chain(n, sb, s, 1.0 / 2048)
                act(xs[:n, s, :], xt[:n, s, :], AF.Copy, [('xt', s), ('ss', s)], [('xs', s)], scale=ssm[:n, sb + 3:sb + 4])
                pT = PSB(2 * s, 2, [16, 128])
                for k in range(16):
                    tr(pT[:, k, :n], xs[:n, s, k * 128:(k + 1) * 128], identb[:n, :n],
                       [('xs', s), 'identb'], [('ps', 2 * s), ('ps', 2 * s + 1)], k == 15)
                tt(xT[:, :, r0:r0 + n], pT[:, :, :n], gcol.unsqueeze(2).to_broadcast([128, 16, n]), ALU.mult,
                   [('ps', 2 * s), ('ps', 2 * s + 1), 'cst'], ['xT'])

        def postnorm(yfn, tiles, prow, scale, xold, xnew, tmp_off):
            xo = V(tmp_off, F32, [2, 2048])
            junk = V(tmp_off + 4096, BF16, [2048])
            gb = V(tmp_off + 5120, F32, [2048])
            P.dma('gb', lambda e: e.dma_start(out=gb, in_=bass.AP(postg, prow * 2048, [[0, 128], [1, 2048]])),
                  writes=['gb'])
            for ti, (r0, n) in enumerate(tiles):
                s = ti % 2
                sb = 8 * s
                y = yfn(ti)
                yk = ('yacc', ti)
                P.dma(('xo', s), lambda e: e.dma_start(out=xo[:n, s, :], in_=xold(r0, n)), writes=[('xo', s)])
                act(junk[:n], y[:n], AF.Square, [yk], ['junk', ('ss', s)], accum_out=ssm[:n, sb:sb + 1])
                rstd_chain(n, sb, s, 1.0 / 2048)
                stt(y[:n], y[:n], ssm[:n, sb + 3:sb + 4], gb[:n], ALU.mult, ALU.mult, [yk, ('ss', s), 'gb'], [yk])
                stt(y[:n], y[:n], float(scale), xo[:n, s, :], ALU.mult, ALU.add, [yk, ('xo', s)], [yk])
                P.dma(('yo', ti % 4), lambda e: e.dma_start(out=xnew(r0, n), in_=y[:n]), reads=[yk], writes=[('xnew', ti)])

        def mk_tiles(ntok):
            return [(r, min(128, ntok - r)) for r in range(0, ntok, 128)]

        def mk_blocks(ntok):
            return [(c, min(512, ntok - c)) for c in range(0, ntok, 512)]

        TILES, BLOCKS = mk_tiles(NTM), mk_blocks(NTM)
        BLOCKSX = mk_blocks(NTX)

        def ffn(f, gkey, prow, src, dst):
            ntok = NTM
            tiles, blocks = TILES, BLOCKS
            nt = len(tiles)
            xT = V(R0, BF16, [16, ntok])
            o1 = R0 + 8 * ntok
            prenorm(src, tiles, cst[:, C_PRE[gkey]:C_PRE[gkey] + 16], xT, o1)
            P.barrier()
            yacc = V(o1, F32, [nt, 2048])
            o2 = o1 + nt * 2048
            hT = V(o2, BF16, [4, ntok])
            o3 = o2 + 2 * ntok
            sg = V(o3, BF16, [2, 512])
            cnt = {'t': 0, 'q': 0}

            def gate_up(j):
                kg, wg = ws.next((f, 'g', j))
                ku, wu = ws.next((f, 'u', j))
                wg = wg.rearrange("p (k c) -> p k c", k=16)
                wu = wu.rearrange("p (k c) -> p k c", k=16)
                for (c0, n) in blocks:
                    t = cnt['t'] % 2
                    cnt['t'] += 1
                    for k in range(16):
                        mm(ps[:, t, :n], wg[:, k, :], xT[:, k, c0:c0 + n], k == 0, k == 15, [kg, 'xT'], [('ps', t)], k == 15)
                    for k in range(16):
                        mm(ps[:, 2 + t, :n], wu[:, k, :], xT[:, k, c0:c0 + n], k == 0, k == 15, [ku, 'xT'], [('ps', 2 + t)], k == 15)
                    act(sg[:, t, :n], ps[:, t, :n], AF.Silu, [('ps', t)], [('sg', t)])
                    tt(hT[:, j % 4, c0:c0 + n], sg[:, t, :n], ps[:, 2 + t, :n], ALU.mult, [('sg', t), ('ps', 2 + t)], [('hT', j % 4)])

            def down(grp, first):
                wd = [ws.next((f, 'd', j)) for j in grp]
                for ti, (r0, n) in enumerate(tiles):
                    for nb in range(4):
                        q = 4 + cnt['q'] % 4
                        cnt['q'] += 1
                        for gi, j in enumerate(grp):
                            mm(ps[:n, q, :], hT[:, j % 4, r0:r0 + n], wd[gi][1][:, nb * 512:(nb + 1) * 512],
                               gi == 0, gi == len(grp) - 1, [wd[gi][0], ('hT', j % 4)], [('ps', q)], gi == len(grp) - 1)
                        ysl = yacc[:n, ti, nb * 512:(nb + 1) * 512]
                        if first:
                            act(ysl, ps[:n, q, :], AF.Copy, [('ps', q)], [('yacc', ti)])
                        else:
                            tt(ysl, ysl, ps[:n, q, :], ALU.add, [('ps', q), ('yacc', ti)], [('yacc', ti)])

            groups = [list(range(j, min(j + 2, NFC))) for j in range(0, NFC, 2)]
            for gi, grp in enumerate(groups):
                for j in grp:
                    gate_up(j)
                if gi > 0:
                    down(groups[gi - 1], gi == 1)
            down(groups[-1], False)
            P.barrier()
            postnorm(lambda ti: yacc[:, ti, :], tiles, prow, 0.5, src, dst, R0)
            P.barrier()

        def linear_tm(name, inT, tiles, acc):
            cntq = 0
            for nb in range(8):
                p0 = ws.next((name, 2 * nb))
                p1 = ws.next((name, 2 * nb + 1))
                for ti, (r0, n) in enumerate(tiles):
                    q = 4 + cntq % 4
                    cntq += 1
                    for k in range(16):
                        pk, pa = (p0, p1)[k // 8]
                        mm(ps[:n, q, 0:256], inT(k)[:, r0:r0 + n], pa[:, (k % 8) * 256:(k % 8 + 1) * 256],
                           k == 0, k == 15, [pk, 'linin'], [('ps', q)], k == 15)
                    act(acc(ti)[:n, nb * 256:(nb + 1) * 256], ps[:n, q, 0:256], AF.Copy, [('ps', q)], [('yacc', ti)])

        def win_fm(j, xT, blocks, evac, name='win'):
            kw, w = ws.next((name, j))
            w = w.rearrange("p (k c) -> p k c", k=16)
            for bi, (c0, n) in enumerate(blocks):
                t = win_fm.t % 4
                win_fm.t += 1
                for k in range(16):
                    mm(ps[:, t, :n], w[:, k, :], xT[:, k, c0:c0 + n], k == 0, k == 15, [kw, 'xT'], [('ps', t)], k == 15)
                evac(bi, c0, n, ps[:, t, :n], ('ps', t))
        win_fm.t = 0

        def dwconv(buf, acc, wcol, bcol, ntap, L, bkey='cbuf', akey='cacc'):
            ts(acc, buf[:, :, 0:L], wcol[:, 0:1], bcol, ALU.mult, ALU.add, [bkey, 'cst'], [akey])
            for k in range(1, ntap):
                stt(acc, buf[:, :, k:k + L], wcol[:, k:k + 1], acc, ALU.mult, ALU.add, [bkey, akey, 'cst'], [akey])

        if stop == 1:
            ffn('f1', 'f1', 0, lambda r0, n: rows(xin, r0, n), lambda r0, n: rows(yout, r0, n))
            return nc, ws.req
        if stop == 1.5:
            ffn('f1', 'f1', 0, lambda r0, n: rows(xin, r0, n), lambda r0, n: rows(x1, r0, n))
            ffn('f2', 'f2', 3, lambda r0, n: rows(x1, r0, n), lambda r0, n: rows(x2, r0, n))
            ffn('f1', 'f1', 0, lambda r0, n: rows(x2, r0, n), lambda r0, n: rows(x3, r0, n))
            ffn('f2', 'f2', 3, lambda r0, n: rows(x3, r0, n), lambda r0, n: rows(yout, r0, n))
            return nc, ws.req
        ffn('f1', 'f1', 0, lambda r0, n: rows(xin, r0, n), lambda r0, n: rows(x1, r0, n))

        def mixer():
            tiles, blocks = TILES, BLOCKSX
            xT = V(R0, BF16, [16, NTX]); o1 = R0 + 8 * NTX
            prenorm(lambda r0, n: rows(x1, r0, n), tiles, cst[:, C_PRE['mix']:C_PRE['mix'] + 16], xT, o1)
            hb = V(o1 + 8000, BF16, [16, 30])
            hb2 = V(o1 + 8300, BF16, [16, 30])
            P.op('dve', lambda e: e.tensor_copy(out=hb, in_=xT[:, :, 994:1024]), reads=['xT'], writes=['hb'])
            exchange('h', arena[:, o1 + 8000:o1 + 8240], ex_i1, ex_o1, arena[:, o1 + 8300:o1 + 8540], ['hb'], ['hb2'])
            ts(xT[:, :, 1088:1118], hb2, flag, None, ALU.mult, None, ['hb2', 'cst'], ['xT'])
            P.barrier()
            if stop == 2.1:
                return
            aT = V(o1, BF16, [8, NTM]); o1 += 4 * NTM
            oA = o1
            assert oA == R0 + 13296
            yc = V(oA, F32, [8, NTM]); o2 = oA + 8 * NTM
            cbp2 = [V(o2, F32, [1, 1054]), V(o2 + 1054, F32, [1, 1054])]; o2 += 2108
            cbs2 = [V(o2, F32, [16, 34]), V(o2 + 544, F32, [16, 34])]; o2 += 1088
            sgk = V(o2, F32, [3, 512]); o2 += 1536
            scT = V(o2, F32, [8, 480]); o2 += 3840
            tl = V(o2, F32, [1024]); o2 += 1024
            ld = V(o2, F32, [1024]); o2 += 1024
            for i in range(4):
                P.dma('ld', lambda e: e.dma_start(out=ld[:120], in_=rows(sconv, i * 120, 120)), writes=['ld'])
                P.dma('cp1', lambda e: [e.dma_start(out=rows(convs, (4 * i + s_) * 30, 26), in_=ld[30 * s_ + 4:30 * s_ + 30, :])
                                        for s_ in range(4)], reads=['ld'], writes=[('convs_old', i)])
                for c in range(8):
                    tr(ps[:, 4 + c % 2, 0:120], ld[:120, c * 128:(c + 1) * 128], identf[:120, :120], ['ld', 'cm'],
                       [('ps', 4 + c % 2)], True)
                    act(scT[:, c, i * 120:(i + 1) * 120], ps[:, 4 + c % 2, 0:120], AF.Copy, [('ps', 4 + c % 2)], ['scT'])
            for c in range(8):
                cbp, cbs, cbk = cbp2[c % 2], cbs2[c % 2], ('cbuf', c % 2)

                def ev_g(bi, c0, n, pa, pk):
                    act(sgk[:, bi, :n], pa, AF.Sigmoid, [pk], [('sgk', bi)])

                def ev_v(bi, c0, n, pa, pk):
                    if bi < 2:
                        tt(cbp[:, 0, 30 + c0:30 + c0 + n], sgk[:, bi, :n], pa, ALU.mult, [pk, ('sgk', bi)], [cbk])
                    else:
                        tt(cbs[:, :, 30:34], sgk[:, bi, 0:64].rearrange("p (s t) -> p s t", s=16),
                           pa[:, 0:64].rearrange("p (s t) -> p s t", s=16), ALU.mult, [pk, ('sgk', bi)], [cbk])
                        tt(cbp[:, 0, 0:30], sgk[:, bi, 64:94], pa[:, 64:94], ALU.mult, [pk, ('sgk', bi)], [cbk])
                win_fm(8 + c, xT, blocks, ev_g)
                act(cbs[:, :, 0:30], scT[:, c, :].rearrange("p (s t) -> p s t", s=16), AF.Copy, ['scT'], [cbk])
                win_fm(c, xT, blocks, ev_v)
                wcol = cst[:, C_CW + 31 * c:C_CW + 31 * c + 31]
                bcol = cst[:, C_CB + c:C_CB + c + 1]
                dwconv(cbp, yc[:, c, 0:1024].unsqueeze(1), wcol, bcol, 31, 1024, cbk, ('cacc', c))
                dwconv(cbs, yc[:, c, 1024:1088].rearrange("p (s t) -> p s t", s=16), wcol, bcol, 31, 4, cbk, ('caccs', c))
                tr(ps[:30, 6, 0:128], cbp[:, 0, 1024:1054], identf, [cbk, 'cm'], [('ps', 6)], True)
                act(tl[:30, c * 128:(c + 1) * 128], ps[:30, 6, 0:128], AF.Copy, [('ps', 6)], ['tl'])
                P.op('dve', lambda e: e.tensor_copy(out=sgk[:, 0, 0:64].rearrange("p (s t) -> p s t", s=16), in_=cbs[:, :, 30:34]),
                     reads=[cbk], writes=[('sgk', 0)])
                tr(ps[:64, 7, 0:128], sgk[:, 0, 0:64], identf, [('sgk', 0), 'cm'], [('ps', 7)], True)
                act(ld[:64, c * 128:(c + 1) * 128], ps[:64, 7, 0:128], AF.Copy, [('ps', 7)], ['ld2'])
            P.dma('o1', lambda e: e.dma_start(out=convp.ap(), in_=tl[:30, :]), reads=['tl'], writes=['convp'])
            P.dma('o2', lambda e: [e.dma_start(out=rows(convs, s_ * 30 + 26, 4), in_=ld[4 * s_:4 * s_ + 4, :]) for s_ in range(16)],
                  reads=['ld2'], writes=['convs_new'])
            P.barrier()
            sq = V(oA + 8 * NTM, F32, [512])
            st1 = V(oA + 8 * NTM + 512, F32, [3, 512])
            for (c0, n) in BLOCKS:
                for c in range(8):
                    mm(ps[:, 0, :n], onesf, yc[:, c, c0:c0 + n], c == 0, c == 7, ['yc', 'cm'], [('ps', 0)], c == 7)
                for c in range(8):
                    act(sq[:, :n], yc[:, c, c0:c0 + n], AF.Square, ['yc'], ['sq'])
                    mm(ps[:, 1, :n], onesf, sq[:, :n], c == 0, c == 7, ['sq', 'cm'], [('ps', 1)], True)
                mean, var, rstd = st1[:, 0, :n], st1[:, 1, :n], st1[:, 2, :n]
                act(mean, ps[:, 0, :n], AF.Copy, [('ps', 0)], ['st1'], scale=1.0 / 1024)
                tt(var, mean, mean, ALU.mult, ['st1'], ['st1'])
                stt(var, ps[:, 1, :n], 1.0 / 1024, var, ALU.mult, ALU.subtract, ['st1', ('ps', 1)], ['st1'])
                ts(var, var, EPS, None, ALU.add, None, ['st1'], ['st1'])
                act(var, var, AF.Sqrt, ['st1'], ['st1'])
                P.op('dve', lambda e: e.reciprocal(out=rstd, in_=var), reads=['st1'], writes=['st1'])
                for c in range(8):
                    tt(sq[:, :n], yc[:, c, c0:c0 + n], mean, ALU.subtract, ['yc', 'st1'], ['sq'])
                    tt(sq[:, :n], sq[:, :n], rstd, ALU.mult, ['sq', 'st1'], ['sq'])
                    act(aT[:, c, c0:c0 + n], sq[:, :n], AF.Silu, ['sq', 'cst'], ['aT'],
                        scale=cst[:, C_LG + c:C_LG + c + 1], bias=cst[:, C_LB + c:C_LB + c + 1])
            P.barrier()
            if stop == 2.2:
                return
            xsT = V(oA, F32, [8, NTM]); o2 = oA + 8 * NTM
            bcT = V(o2, BF16, [4, NTM]); o2 += 2 * NTM
            szT = V(o2, BF16, [8, NTM]); o2 += 4 * NTM
            oB = o2
            ynT = V(o2, BF16, [8, NTM])
            oTail = o2 + 4 * NTM
            xbp = V(o2, F32, [1, 1028]); o2 += 1028
            xbs = V(o2, F32, [16, 7]); o2 += 112
            cac = V(o2, F32, [NTM]); o2 += NTM
            s3T = V(o2, F32, [12, 48]); o2 += 576
            tl3 = V(o2, F32, [128]); o2 += 128
            ld3 = V(o2, F32, [1536]); o2 += 1536
            raw = V(o2, F32, [48]); o2 += 48
            P.dma('ld', lambda e: e.dma_start(out=ld3[:48], in_=sssc.ap()), writes=['ld'])
            for c in range(12):
                tr(ps[:, 4 + c % 2, 0:48], ld3[:48, c * 128:(c + 1) * 128], identf[:48, :48], ['ld', 'cm'], [('ps', 4 + c % 2)], True)
                act(s3T[:, c, :], ps[:, 4 + c % 2, 0:48], AF.Copy, [('ps', 4 + c % 2)], ['s3T'])
            P.barrier()
            for c in range(12):
                def ev(bi, c0, n, pa, pk):
                    if bi < 2:
                        act(xbp[:, 0, 3 + c0:3 + c0 + n], pa, AF.Copy, [pk], ['cbuf'])
                    else:
                        act(xbs[:, :, 3:7], pa[:, 0:64].rearrange("p (s t) -> p s t", s=16), AF.Copy, [pk], ['cbuf'])
                        act(xbp[:, 0, 0:3], pa[:, 91:94], AF.Copy, [pk], ['cbuf'])
                act(xbs[:, :, 0:3], s3T[:, c, :].rearrange("p (s t) -> p s t", s=16), AF.Copy, ['s3T'], ['cbuf'])
                win_fm(24 + c, xT, blocks, ev)
                wcol = cst[:, C_SW + 4 * c:C_SW + 4 * c + 4]
                bcol = cst[:, C_SB + c:C_SB + c + 1]
                dwconv(xbp, cac[:, 0:1024].unsqueeze(1), wcol, bcol, 4, 1024)
                dwconv(xbs, cac[:, 1024:1088].rearrange("p (s t) -> p s t", s=16), wcol, bcol, 4, 4)
                dst = xsT[:, c, :] if c < 8 else bcT[:, c - 8, :]
                act(dst, cac, AF.Silu, ['cacc'], ['xsT'])
                tr(ps[:3, 6, 0:128], xbp[:, 0, 1024:1027], identf, ['cbuf', 'cm'], [('ps', 6)], True)
                act(tl3[:3, :], ps[:3, 6, 0:128], AF.Copy, [('ps', 6)], ['tl'])
                P.dma('o1', lambda e: e.dma_start(out=rows(sscp, 0, 3, c * 128, 128), in_=tl3[:3, :]), reads=['tl'], writes=[('sscp', c)])
                P.op('dve', lambda e: e.tensor_copy(out=raw.rearrange("p (s t) -> p s t", s=16), in_=xbs[:, :, 4:7]), reads=['cbuf'], writes=['raw'])
                tr(ps[:48, 7, 0:128], raw, identf, ['raw', 'cm'], [('ps', 7)], True)
                act(ld3[:48, c * 128:(c + 1) * 128], ps[:48, 7, 0:128], AF.Copy, [('ps', 7)], ['ld2'])
            P.dma('o2', lambda e: e.dma_start(out=sscs.ap(), in_=ld3[:48, :]), reads=['ld2'], writes=['sscs'])
            for c in range(8):
                def evz(bi, c0, n, pa, pk):
                    nn = min(n, NTM - c0)
                    act(szT[:, c, c0:c0 + nn], pa[:, :nn], AF.Silu, [pk], ['szT'])
                win_fm(16 + c, xT, blocks, evz)
            kw, w = ws.next(('win', 36))
            w = w[:, 0:256].rearrange("p (k c) -> p k c", k=16)
            for ti, (r0, n) in enumerate(tiles):
                t = 4 + ti % 2
                for k in range(16):
                    mm(ps[:n, t, 0:16], xT[:, k, r0:r0 + n], w[:, k, :], k == 0, k == 15, [kw, 'xT'], [('ps', t)], k == 15)
                tt(dtp[:n, ti, :], ps[:n, t, 0:16], cst[:n, C_DTB:C_DTB + 16], ALU.add, [('ps', t), 'cst'], ['dtp'])
            for (a, b, n) in ((0, 8, 128), (8, 9, 64)):
                act(dtp[:n, a:b, :], dtp[:n, a:b, :], AF.Exp, ['dtp'], ['dtp'])
                act(dtp[:n, a:b, :], dtp[:n, a:b, :], AF.Ln, ['dtp'], ['dtp'], bias=1.0)
            P.barrier()
            if stop == 2.3:
                return
            o3 = R0
            xs_tm = V(o3, BF16, [1024]); o3 += 512
            b_tm = V(o3, BF16, [256]); o3 += 128
            xdt = V(o3, BF16, [1024]); o3 += 512
            xdte = V(o3, BF16, [1024]); o3 += 512
            abuf = V(o3, F32, [16]); o3 += 16
            eab = V(o3, F32, [48]); o3 += 48
            oAU = o3
            AU = V(o3, F32, [16, 128]); o3 += 2048
            oWT = o3
            WT = V(o3, BF16, [16, 128]); o3 += 1024
            xsb = V(o3, BF16, [128]); o3 += 64
            oEt = o3
            Et = V(o3, F32, [1, 512]); o3 += 512
            cbm = V(o3, F32, [2, 128]); o3 += 256
            negU = V(o3, F32, [128]); o3 += 128
            Hs = V(o3, BF16, [1024]); o3 += 512
            H = V(o3, F32, [1024]); o3 += 1024
            yg = V(o3, F32, [8, 128]); o3 += 1024
            yoS = V(o3, F32, [8, 64]); o3 += 512
            assert o3 <= R0 + 8 * NTX, (o3, R0)
            rs = V(oTail, F32, [2, 128])
            ysq = V(oTail + 256, F32, [128])
            cdc = V(oTail + 384, F32, [16])
            aex = V(oTail + 400, F32, [128])

            def to_tm(t0, n):
                pT = PSB(0, 2, [10, 128])
                for c in range(10):
                    if c < 8:
                        act(xsb[:, :n], xsT[:, c, t0:t0 + n], AF.Copy, ['xsT'], ['xsb'])
                        src = xsb[:, :n]
                    else:
                        src = bcT[:, c - 8, t0:t0 + n]
                    tr(pT[:n, c, :], src, identb, ['xsT', 'xsb'], [('ps', 0), ('ps', 1)], True)
                act(xs_tm[:n], pT[:n, 0:8, :].rearrange("p a b -> p (a b)"), AF.Copy, [('ps', 0), ('ps', 1)], ['xs_tm'])
                act(b_tm[:n], pT[:n, 8:10, :].rearrange("p a b -> p (a b)"), AF.Copy, [('ps', 0), ('ps', 1)], ['b_tm'])

            def scalars(dt_t, n, Um, Lm):
                tt(abuf[:n], dt_t, Aneg[:n], ALU.mult, ['dtp', 'Aneg'], ['abuf'])
                mm(ps[:n, 3, 0:16], Um[:n, :n], abuf[:n], True, True, ['abuf', 'cm'], [('ps', 3)], False)
                mm(ps[:n, 3, 16:32], Lm[:n, :n], abuf[:n], True, True, ['abuf', 'cm'], [('ps', 3)], False)
                mm(ps[:, 3, 32:48], onesf[:n, :], abuf[:n], True, True, ['abuf', 'cm'], [('ps', 3)], True)
                act(eab[:, 0:48], ps[:, 3, 0:48], AF.Exp, [('ps', 3)], ['eabuf'])
                tt(eab[:n, 16:32], eab[:n, 16:32], dt_t, ALU.mult, ['eabuf', 'dtp'], ['eabuf'])

            def mk_xdte(n):
                tt(xdte[:n].rearrange("p (h q) -> p h q", h=16), xs_tm[:n].rearrange("p (h q) -> p h q", h=16),
                   eab[:n, 16:32].unsqueeze(2).to_broadcast([n, 16, 64]), ALU.mult, ['xs_tm', 'eabuf'], ['xdte'])

            def state_step():
                for g in range(2):
                    mm(ps[:, 4 + g, :], b_tm[:, g * 128:(g + 1) * 128], xdte[:, g * 512:(g + 1) * 512], True, True,
                       ['b_tm', 'xdte'], [('ps', 4 + g)], True)
                H3 = H.rearrange("p (h q) -> p h q", h=16)
                tt(H3, H3, eab[:, 32:48].unsqueeze(2).to_broadcast([128, 16, 64]), ALU.mult, ['H', 'eabuf', 'Hs'], ['H'])
                for g in range(2):
                    tt(H[:, g * 512:(g + 1) * 512], H[:, g * 512:(g + 1) * 512], ps[:, 4 + g, :], ALU.add,
                       ['H', ('ps', 4 + g)], ['H'])

            Up, Lp = cm[:, M_U:M_U + 128], cm[:, M_L:M_L + 128]
            P.op('dve', lambda e: e.memset(H, 0.0), writes=['H'])
            for ci in range(8):
                to_tm(ci * 128, 128)
                scalars(dtp[:, ci, :], 128, Up, Lp)
                mk_xdte(128)
                state_step()
            Hin = V(oAU, F32, [1024])
            exchange('s', H, ex_i2, ex_o2, Hin, ['H'], ['Hin'])
            ts(H, Hin, flag, None, ALU.mult, None, ['Hin', 'cst', 'H'], ['H'])
            if stop == 2.4:
                P.barrier()
                return

            def ssd_chunk(t0, n, Um, Lm, NEGm, MSKm, yoff_fn):
                to_tm(t0, n)
                dt_t = dtp[:n, t0 // 128, :]
                scalars(dt_t, n, Um, Lm)
                tt(xdt[:n].rearrange("p (h q) -> p h q", h=16), xs_tm[:n].rearrange("p (h q) -> p h q", h=16),
                   dt_t.unsqueeze(2).to_broadcast([n, 16, 64]), ALU.mult, ['xs_tm', 'dtp'], ['xdt'])
                mk_xdte(n)
                for g in range(2):
                    mm(ps[:n, 2, g * 128:g * 128 + n], bcT[:, g, t0:t0 + n], bcT[:, 2 + g, t0:t0 + n], True, True, ['xsT'], [('ps', 2)], g == 1)
                tt(cbm[:n, :, :n], ps[:n, 2, 0:256].rearrange("p (g l) -> p g l", g=2)[:, :, :n],
                   MSKm[:n, :n].unsqueeze(1).to_broadcast([n, 2, n]), ALU.mult, [('ps', 2), 'cm'], ['cbm'])
                tt(AU[:n, :, :n], Um[:n, :n].unsqueeze(1).to_broadcast([n, 16, n]),
                   abuf[:n].unsqueeze(2).to_broadcast([n, 16, n]), ALU.mult, ['abuf', 'cm'], ['AU'])
                ts(negU[:n, :n], Um[:n, :n], -1.0, None, ALU.mult, None, ['cm'], ['negU'])
                for b4 in range(4):
                    bk = 4 + b4 % 2
                    pb = ps[:n, bk, :].rearrange("p (h l) -> p h l", h=4)[:, :, :n]
                    mm(pb, onesf[:n, :n], AU[:n, 4 * b4:4 * b4 + 4, :n], True, False, ['AU', 'cm'], [('ps', bk)], False)
                    mm(pb, negU[:n, :n], abuf[:n, 4 * b4:4 * b4 + 4].unsqueeze(2).to_broadcast([n, 4, n]), False, False,
                       ['negU', 'abuf'], [('ps', bk)], False)
                    mm(pb, identf[:n, :n], NEGm[:n, :n].unsqueeze(1).to_broadcast([n, 4, n]), False, True, ['cm'], [('ps', bk)], True)
                    Eb = Et[:n, 0, :].rearrange("p (h l) -> p h l", h=4)[:, :, :n]
                    act(Eb, pb, AF.Exp, [('ps', bk)], [('Et', 0)])
                    tt(WT[:n, 4 * b4:4 * b4 + 4, :n], Eb, cbm[:n, b4 // 2, :n].unsqueeze(1).to_broadcast([n, 4, n]), ALU.mult,
                       [('Et', 0), 'cbm'], ['WT'])
                for c in range(8):
                    bk = 6 + c % 2
                    pk = ('ps', bk)
                    pa = ps[:, bk, :]
                    for hh in range(2):
                        h = 2 * c + hh
                        mm(pa[64 * hh:64 * hh + 64, 0:n], xdt[:n, h * 64:(h + 1) * 64], WT[:n, h, :n], True, True,
                           ['xdt', 'WT'], [pk], False)
                        mm(pa[64 * hh:64 * hh + 64, 256:256 + n], onesf[:n, 0:64], AU[:n, h, :n], True, True, ['AU', 'cm'], [pk], hh == 1)
                    yo_ap, yo_keys = yoff_fn(c, pa, pk)
                    act(rs[:, 0, :n], pa[:, 256:256 + n], AF.Exp, [pk], ['rs'])
                    tt(rs[:, 0, :n], rs[:, 0, :n], yo_ap, ALU.mult, ['rs', pk] + yo_keys, ['rs'])
                    tt(rs[:, 0, :n], rs[:, 0, :n], pa[:, 0:n], ALU.add, ['rs', pk], ['rs'])
                    stt(rs[:, 0, :n], xsT[:, c, t0:t0 + n], cst[:, C_DS + c:C_DS + c + 1], rs[:, 0, :n], ALU.mult, ALU.add,
                        ['rs', 'xsT', 'cst'], ['rs'])
                    tt(yg[:, c, :n], rs[:, 0, :n], szT[:, c, t0:t0 + n], ALU.mult, ['rs', 'szT'], ['yg'])
                for g in range(2):
                    for c4 in range(4):
                        c = 4 * g + c4
                        act(ysq[:, :n], yg[:, c, :n], AF.Square, ['yg'], ['ysq'])
                        mm(ps[:, 3, 64:64 + n], onesf, ysq[:, :n], c4 == 0, c4 == 3, ['ysq', 'cm'], [('ps', 3)], True)
                    ts(rs[:, 1, :n], ps[:, 3, 64:64 + n], 1.0 / 512, EPS, ALU.mult, ALU.add, [('ps', 3)], ['rs1'])
                    act(rs[:, 1, :n], rs[:, 1, :n], AF.Sqrt, ['rs1'], ['rs1'])
                    P.op('dve', lambda e: e.reciprocal(out=rs[:, 1, :n], in_=rs[:, 1, :n]), reads=['rs1'], writes=['rs1'])
                    for c4 in range(4):
                        c = 4 * g + c4
                        stt(ynT[:, c, t0:t0 + n], yg[:, c, :n], cst[:, C_NG + c:C_NG + c + 1], rs[:, 1, :n], ALU.mult, ALU.mult,
                            ['yg', 'rs1', 'cst'], ['ynT'])

            for ci in range(8):
                t0 = ci * 128
                act(Hs, H, AF.Copy, ['H'], ['Hs'])

                def yoff_p(c, pa, pk):
                    mm(pa[:, 128:256], Hs[:, c * 128:(c + 1) * 128], bcT[:, 2 + c // 4, t0:t0 + 128], True, True, ['Hs', 'xsT'], [pk], True)
                    return pa[:, 128:256], []
                ssd_chunk(t0, 128, Up, Lp, cm[:, M_NEG:M_NEG + 128], cm[:, M_MSK:M_MSK + 128], yoff_p)
                state_step()
            P.barrier()
            stg = V(oAU, F32, [8, 128])
            for c in range(8):
                tr(ps[:, 4 + c % 2, 0:128], H[:, c * 128:(c + 1) * 128], identf, ['H', 'cm'], [('ps', 4 + c % 2)], True)
                act(stg[:, c, :], ps[:, 4 + c % 2, 0:128], AF.Copy, [('ps', 4 + c % 2)], ['stg'])
            P.dma('o3', lambda e: e.dma_start(out=sstp.ap().rearrange("(c p) n -> p c n", p=128), in_=stg), reads=['stg'], writes=['sstp'])
            P.barrier()
            if stop == 2.5:
                return
            Sin = V(oAU, F32, [2, 1024])
            STb = V(oWT, BF16, [2, 1024])
            CmT = V(oEt, BF16, [2, 1024])
            xdm = V(oWT, BF16, [1024])
            T0 = 1024
            bmv = cm[:, M_BM:M_BM + 1024].rearrange("p (s t) -> p s t", s=16)
            for g in range(2):
                tt(CmT[:, g, :].rearrange("p (s t) -> p s t", s=16), bcT[:, 2 + g, T0:T0 + 64].unsqueeze(1).to_broadcast([128, 16, 64]),
                   bmv, ALU.mult, ['xsT', 'cm'], ['CmT'])

            def load_state(s_):
                sl = s_ % 2
                P.dma(('sin', sl), lambda e: e.dma_start(out=Sin[:, sl, :], in_=bass.AP(sst, s_ * 1024 * 128, [[1024, 128], [1, 1024]])),
                      writes=[('sin', sl)])
                return sl
            yacc_ps = ps[:, 2, :].rearrange("p (c l) -> p c l", c=8)
            for s_ in range(16):
                sl = load_state(s_)
                for r in range(8):
                    tr(ps[:, 4 + r // 4, (r % 4) * 128:(r % 4 + 1) * 128], Sin[:, sl, r * 128:(r + 1) * 128], identf,
                       [('sin', sl), 'cm'], [('ps', 4 + r // 4)], r % 4 == 3)
                for hb_ in range(2):
                    act(STb[:, sl, :].rearrange("p (q r) -> p r q", r=8)[:, 4 * hb_:4 * hb_ + 4, :],
                        ps[:, 4 + hb_, :].rearrange("p (r q) -> p r q", r=4), AF.Copy, [('ps', 4 + hb_)], [('STb', sl)])
                for c in range(8):
                    mm(yacc_ps[:, c, :], STb[:, sl, c * 128:(c + 1) * 128], CmT[:, c // 4, s_ * 64:(s_ + 1) * 64],
                       s_ == 0 and c == 0, s_ == 15 and c == 7, [('STb', sl), 'CmT'], [('ps', 2)], c == 7)
            act(yoS, yacc_ps, AF.Copy, [('ps', 2)], ['yoS'])
            P.barrier()

            def yoff_s(c, pa, pk):
                return yoS[:, c, :], ['yoS']
            ssd_chunk(T0, 64, cm[:, M_UB:M_UB + 128], cm[:, M_LB:M_LB + 128], cm[:, M_NEGB:M_NEGB + 128],
                      cm[:, M_MSKB:M_MSKB + 128], yoff_s)
            P.barrier()
            P.op('dve', lambda e: e.tensor_copy(out=aex[:64].rearrange("p (h r) -> p h r", h=16),
                                                in_=abuf[:64].unsqueeze(2).to_broadcast([64, 16, 8])), reads=['abuf'], writes=['aex'])
            mm(ps[:, 3, 0:16], aex[:64, :], cm[:64, M_BMC:M_BMC + 16], True, True, ['aex', 'cm'], [('ps', 3)], True)
            act(cdc, ps[:, 3, 0:16], AF.Exp, [('ps', 3)], ['cdc'])
            for s_ in range(16):
                sl = load_state(s_)
                ts(xdm[:64], xdte[:64], cm[:64, M_BMC + s_:M_BMC + s_ + 1], None, ALU.mult, None, ['xdte', 'cm'], ['xdm'])
                xv = xdm[:64].rearrange("p (q r) -> p r q", r=8)
                for r in range(8):
                    bk = 4 + r // 4
                    for g in range(2):
                        mm(ps[64 * g:64 * g + 64, bk, (r % 4) * 128:(r % 4 + 1) * 128], xv[:, r, 64 * g:64 * g + 64],
                           b_tm[:64, g * 128:(g + 1) * 128], True, True, ['xdm', 'b_tm'], [('ps', bk)], (r % 4 == 3) and g == 1)
                for hb_ in range(2):
                    stt(Sin[:, sl, hb_ * 512:(hb_ + 1) * 512], Sin[:, sl, hb_ * 512:(hb_ + 1) * 512], cdc[:, s_:s_ + 1],
                        ps[:, 4 + hb_, :], ALU.mult, ALU.add, [('sin', sl), ('ps', 4 + hb_), 'cdc'], [('sin', sl)])
                P.dma(('sout', sl), lambda e: e.dma_start(out=bass.AP(ssts, s_ * 1024 * 128, [[1024, 128], [1, 1024]]), in_=Sin[:, sl, :]),
                      reads=[('sin', sl)], writes=[('ssts', s_)])
            P.barrier()
            if stop == 2.6:
                return
            mA = V(R0, F32, [4, 2048])
            mB = V(oA, F32, [5, 2048])
            macc = lambda ti: (mA[:, ti, :] if ti < 4 else mB[:, ti - 4, :])
            linear_tm('wout', lambda k: (aT[:, k, :] if k < 8 else ynT[:, k - 8, :]), TILES, macc)
            P.barrier()
            postnorm(macc, TILES, 1, 1.0, lambda r0, n: rows(x1, r0, n), lambda r0, n: rows(yout if stop == 2 else x2, r0, n), oA + 10240)
            P.barrier()

        mixer()
        if stop is not None and 2 <= stop < 3:
            return nc, ws.req

        def attention():
            SC = 512 ** -0.5
            xT = V(R0, BF16, [16, NTM])
            qT = V(R0 + 8704, BF16, [16, NTM])
            oX = R0 + 18432
            prenorm(lambda r0, n: rows(x2, r0, n), TILES, cst[:, C_PRE['xa']:C_PRE['xa'] + 16], xT, oX)
            P.barrier()
            mT = V(oX, BF16, [16, 256]); o2 = oX + 2048
            prenorm(lambda r0, n: rows(mem, r0, n), [(0, 128), (128, 128)], cst[:, C_PRE['mem']:C_PRE['mem'] + 16], mT, o2)
            P.barrier()
            Kn = V(o2, F32, [2, 2048]); o2 += 4096
            Vn = V(o2, F32, [2, 2048]); o2 += 4096
            for j in range(16):
                def evq(bi, c0, n, pa, pk):
                    act(qT[:, j, c0:c0 + n], pa, AF.Copy, [pk], ['qT'])
                win_fm(j, xT, BLOCKS, evq, name='xq')
            for name, dstt, outd in (('xk', Kn, kout), ('xv', Vn, vout)):
                linear_tm(name, lambda k: mT[:, k, :], [(0, 128), (128, 128)], lambda ti, dstt=dstt: dstt[:, ti, :])
                P.dma(('okv', name), lambda e: e.dma_start(out=outd.ap().rearrange("(t p) d -> p t d", p=128), in_=dstt),
                      reads=[('yacc', 0), ('yacc', 1)], writes=[('okv', name)])
            P.barrier()
            o3 = R0
            KT = V(o3, BF16, [16, 256]); o3 += 2048
            Vb = V(o3, BF16, [2, 2048]); o3 += 2048
            Pn = V(o3, BF16, [4, 256]); o3 += 512
            Pf = V(o3, F32, [4, 256]); o3 += 1024
            PT = V(o3, BF16, [8, 128]); o3 += 512
            Pms = V(o3, BF16, [8, 64]); o3 += 256
            qm = V(o3, BF16, [16, 64]); o3 += 512
            sm4 = V(o3, F32, [16]); o3 += 16
            assert o3 <= R0 + 8704
            otm = V(R0 + 17408, BF16, [2048])
            oT = V(oX, BF16, [16, NTM])
            stg = V(oX + 8704, F32, [2, 2048])

            def make_KT(src_fn, key_fn):
                for mt in range(2):
                    src = src_fn(mt)
                    for half in range(2):
                        for dc in range(8):
                            d = half * 8 + dc
                            bk = 4 + dc // 2
                            tr(ps[:, bk, (dc % 2) * 128:(dc % 2) * 128 + 128], src[:, d * 128:(d + 1) * 128],
                               identf, key_fn(mt) + ['cm'], [('ps', bk)], dc % 2 == 1)
                        for b2 in range(4):
                            act(KT[:, half * 8 + 2 * b2:half * 8 + 2 * b2 + 2, mt * 128:(mt + 1) * 128],
                                ps[:, 4 + b2, 0:256].rearrange("p (a m) -> p a m", a=2), AF.Copy, [('ps', 4 + b2)], ['KT'])

            def softmax_rows(n, nh, sp_views, spk):
                for h in range(nh):
                    sv = sp_views[h]
                    P.op('dve', lambda e: e.tensor_reduce(out=sm4[:n, h:h + 1], in_=sv, axis=AX.X, op=ALU.max), reads=spk, writes=['sm4'])
                    ts(sm4[:n, 4 + h:5 + h], sm4[:n, h:h + 1], -SC, None, ALU.mult, None, ['sm4'], ['sm4'])
                    act(Pf[:n, h, :], sv, AF.Exp, spk + ['sm4'], ['Pf'], scale=SC, bias=sm4[:n, 4 + h:5 + h], accum_out=sm4[:n, 8 + h:9 + h])
                    P.op('dve', lambda e: e.reciprocal(out=sm4[:n, 12 + h:13 + h], in_=sm4[:n, 8 + h:9 + h]), reads=['sm4'], writes=['sm4'])
                    ts(Pn[:n, h, :], Pf[:n, h, :], sm4[:n, 12 + h:13 + h], None, ALU.mult, None, ['Pf', 'sm4'], ['Pn'])

            make_KT(lambda mt: Kn[:, mt, :], lambda mt: [('yacc', 0), ('yacc', 1)])
            act(Vb, Vn, AF.Copy, [('yacc', 0), ('yacc', 1)], [('Vb', 0), ('Vb', 1)])
            P.barrier()
            for ti in range(8):
                t0 = ti * 128
                for hp in range(2):
                    views = []
                    for hh in range(2):
                        h = 2 * hp + hh
                        for dc in range(4):
                            mm(ps[:, hp, hh * 256:(hh + 1) * 256], qT[:, 4 * h + dc, t0:t0 + 128], KT[:, 4 * h + dc, :], dc == 0, dc == 3,
                               ['qT', 'KT'], [('ps', hp)], dc == 3)
                        views.append(ps[:, hp, hh * 256:(hh + 1) * 256])
                    for hh in range(2):
                        h = 2 * hp + hh
                        sv = views[hh]
                        P.op('dve', lambda e: e.tensor_reduce(out=sm4[:, h:h + 1], in_=sv, axis=AX.X, op=ALU.max), reads=[('ps', hp)], writes=['sm4'])
                        ts(sm4[:, 4 + h:5 + h], sm4[:, h:h + 1], -SC, None, ALU.mult, None, ['sm4'], ['sm4'])
                        act(Pf[:, h, :], sv, AF.Exp, [('ps', hp), 'sm4'], ['Pf'], scale=SC, bias=sm4[:, 4 + h:5 + h], accum_out=sm4[:, 8 + h:9 + h])
                        P.op('dve', lambda e: e.reciprocal(out=sm4[:, 12 + h:13 + h], in_=sm4[:, 8 + h:9 + h]), reads=['sm4'], writes=['sm4'])
                        ts(Pn[:, h, :], Pf[:, h, :], sm4[:, 12 + h:13 + h], None, ALU.mult, None, ['Pf', 'sm4'], ['Pn'])
                pT = PSB(2, 1, [8, 128])
                for h in range(4):
                    for mt in range(2):
                        tr(pT[:, 2 * h + mt, :], Pn[:, h, mt * 128:(mt + 1) * 128], identb, ['Pn'], [('ps', 2)], h == 3 and mt == 1)
                act(PT, pT, AF.Copy, [('ps', 2)], ['PT'])
                for h in range(4):
                    bk = 4 + h
                    for dc in range(4):
                        for mt in range(2):
                            mm(ps[:, bk, dc * 128:(dc + 1) * 128], Vb[:, mt, (4 * h + dc) * 128:(4 * h + dc + 1) * 128], PT[:, 2 * h + mt, :],
                               mt == 0, mt == 1, [('Vb', mt), 'PT'], [('ps', bk)], dc == 3 and mt == 1)
                    act(oT[:, 4 * h:4 * h + 4, t0:t0 + 128], ps[:, bk, :].rearrange("p (a t) -> p a t", a=4), AF.Copy, [('ps', bk)], ['oT'])
            P.barrier()
            T0 = 1024
            bmv = cm[:, M_BM:M_BM + 1024].rearrange("p (s t) -> p s t", s=16)
            sp = [ps[:64, 0, 0:256], ps[:64, 0, 256:512], ps[:64, 1, 0:256], ps[:64, 1, 256:512]]
            for s_ in range(16):
                for mt in range(2):
                    P.dma(('kst', mt), lambda e: e.dma_start(out=stg[:, mt, :], in_=rows(ck, s_ * 256 + mt * 128, 128)),
                          writes=[('stg', mt)])
                make_KT(lambda mt: stg[:, mt, :], lambda mt: [('stg', mt)])
                tt(qm, qT[:, :, T0:T0 + 64], bmv[:, s_, :].unsqueeze(1).to_broadcast([128, 16, 64]), ALU.mult, ['qT', 'cm'], ['qm'])
                for h in range(4):
                    for dc in range(4):
                        first = (s_ == 0 and dc == 0 and h % 2 == 0)
                        last = (s_ == 15 and dc == 3 and h % 2 == 1)
                        mm(sp[h], qm[:, 4 * h + dc, :], KT[:, 4 * h + dc, :], first, last, ['qm', 'KT'], [('ps', h // 2)], dc == 3)
            softmax_rows(64, 4, sp, [('ps', 0), ('ps', 1)])
            pT = PSB(2, 1, [8, 64])
            for h in range(4):
                for mt in range(2):
                    tr(pT[:, 2 * h + mt, :], Pn[:64, h, mt * 128:(mt + 1) * 128], identb[:64, :64], ['Pn'], [('ps', 2)], h == 3 and mt == 1)
            act(PT[:, :, 0:64], pT, AF.Copy, [('ps', 2)], ['PT'])
            for s_ in range(16):
                for mt in range(2):
                    P.dma(('kst', mt), lambda e: e.dma_start(out=stg[:, mt, :], in_=rows(cv, s_ * 256 + mt * 128, 128)),
                          writes=[('stg', mt)])
                    act(Vb[:, mt, :], stg[:, mt, :], AF.Copy, [('stg', mt)], [('Vb', mt)])
                tt(Pms, PT[:, :, 0:64], bmv[:, s_, :].unsqueeze(1).to_broadcast([128, 8, 64]), ALU.mult, ['PT', 'cm'], ['Pms'])
                for h in range(4):
                    for mt in range(2):
                        mm(ps[:64, 4 + h, :], Pms[:, 2 * h + mt, :], Vb[:, mt, h * 512:(h + 1) * 512],
                           s_ == 0 and mt == 0, s_ == 15 and mt == 1, ['Pms', ('Vb', mt)], [('ps', 4 + h)], mt == 1)
            for h in range(4):
                act(otm[:64, h * 512:(h + 1) * 512], ps[:64, 4 + h, :], AF.Copy, [('ps', 4 + h)], ['otm'])
            pT2 = PSB(0, 2, [16, 64])
            for k in range(16):
                tr(pT2[:, k, :], otm[:64, k * 128:(k + 1) * 128], identb[:64, :64], ['otm'], [('ps', 0), ('ps', 1)], k == 15)
            act(oT[:, :, T0:T0 + 64], pT2, AF.Copy, [('ps', 0), ('ps', 1)], ['oT'])
            P.barrier()
            aacc = V(R0, F32, [9, 2048])
            assert R0 + 9 * 2048 <= oX
            linear_tm('xo', lambda k: oT[:, k, :], TILES, lambda ti: aacc[:, ti, :])
            P.barrier()
            postnorm(lambda ti: aacc[:, ti, :], TILES, 2, 1.0, lambda r0, n: rows(x2, r0, n), lambda r0, n: rows(yout if stop == 3 else x3, r0, n), oX)
            P.barrier()

        attention()
        if stop == 3:
            return nc, ws.req

        ffn('f2', 'f2', 3, lambda r0, n: rows(x3, r0, n), lambda r0, n: rows(yout, r0, n))
        P.barrier()
    return nc, ws.req


_CACHE = {}


def kernel(**inp):
    inp = {k: np.asarray(v) for k, v in inp.items()}
    if 'prog' not in _CACHE:
        _, seq = build(None)
        nc, _ = build(seq)
        _CACHE['prog'] = nc
    nc = _CACHE['prog']
    wall = build_wall(inp).reshape(-1, 2048)
    cmat = build_cmat()
    postg = np.stack([inp['ffn1_post_g'][0], inp['mix_post_g'][0], inp['xattn_post_g'][0], inp['ffn2_post_g'][0]]).astype(np.float32)
    in_maps = []
    for cid in range(8):
        b, half = cid // 2, cid % 2
        xin = np.concatenate([inp['x_prompt'][b, half * 1024:(half + 1) * 1024],
                              inp['x_sample'][cid * 16:(cid + 1) * 16].reshape(64, 2048)], axis=0)
        in_maps.append({
            "xin": np.ascontiguousarray(xin, np.float32),
            "mem": np.ascontiguousarray(inp['mem_prompt'][b]),
            "ck": np.ascontiguousarray(inp['cache_mem_k'][0, cid * 16:(cid + 1) * 16].reshape(16 * 256, 2048)),
            "cv": np.ascontiguousarray(inp['cache_mem_v'][0, cid * 16:(cid + 1) * 16].reshape(16 * 256, 2048)),
            "sconv": np.ascontiguousarray(inp['state_conv'][0, cid * 16:(cid + 1) * 16].reshape(16 * 30, 1024)),
            "sssc": np.ascontiguousarray(inp['state_ssm_conv'][0, cid * 16:(cid + 1) * 16].reshape(16 * 3, 1536)),
            "sst": np.ascontiguousarray(inp['state_ssm'][0, cid * 16:(cid + 1) * 16].reshape(16 * 1024, 128)),
            "wall": wall,
            "cst": build_consts(inp, half),
            "cmat": cmat,
            "postg": postg,
        })
    res = run_bass_kernel_spmd(nc, in_maps, core_ids=list(range(8))).results
    yp = np.empty((4, 2048, 2048), np.float32)
    ys = np.empty((128, 4, 2048), np.float32)
    nk = np.empty((1, 4, 256, 4, 512), np.float32)
    nv = np.empty((1, 4, 256, 4, 512), np.float32)
    ncp = np.empty((1, 4, 30, 1024), np.float32)
    nscp = np.empty((1, 4, 3, 1536), np.float32)
    nsp = np.empty((1, 4, 16, 64, 128), np.float32)
    ncs = np.empty((1, 128, 30, 1024), np.float32)
    nscs = np.empty((1, 128, 3, 1536), np.float32)
    nss = np.empty((1, 128, 16, 64, 128), np.float32)
    for cid in range(8):
        b, half = cid // 2, cid % 2
        r = res[cid]
        yp[b, half * 1024:(half + 1) * 1024] = r["yout"][0:1024]
        ys[cid * 16:(cid + 1) * 16] = r["yout"][1024:1088].reshape(16, 4, 2048)
        ncs[0, cid * 16:(cid + 1) * 16] = r["convs"].reshape(16, 30, 1024)
        nscs[0, cid * 16:(cid + 1) * 16] = r["sscs"].reshape(16, 3, 1536)
        nss[0, cid * 16:(cid + 1) * 16] = r["ssts"].reshape(16, 16, 64, 128)
        if half == 0:
            nk[0, b] = r["kout"].reshape(256, 4, 512)
            nv[0, b] = r["vout"].reshape(256, 4, 512)
        else:
            ncp[0, b] = r["convp"]
            nscp[0, b] = r["sscp"]
            nsp[0, b] = r["sstp"].reshape(16, 64, 128)
    return (yp, ys, nk, nv, ncp, nscp, nsp, ncs, nscs, nss)
```

```python
from contextlib import ExitStack
import numpy as np
import concourse.bass as bass
import concourse.mybir as mybir
from concourse.bass_utils import run_bass_kernel_spmd

F32 = mybir.dt.float32
BF16 = mybir.dt.bfloat16
AF = mybir.ActivationFunctionType
ALU = mybir.AluOpType
AX = mybir.AxisListType

SAME_ENG_SYNC = True
EPS = 1e-6
DFF = 5504
NFC = 43


class Prog:
    CE = ('pe', 'act', 'dve', 'pool')

    def __init__(self, nc, stack):
        self.nc = nc
        self.stack = stack
        self.eng = {'pe': nc.tensor, 'act': nc.scalar, 'dve': nc.vector,
                    'pool': nc.gpsimd, 'sp': nc.sync}
        self.sem = {e: stack.enter_context(nc.semaphore('s_' + e)) for e in self.CE}
        self.cnt = {e: 0 for e in self.CE}
        self.pend = {e: False for e in self.CE}
        self.dsem = {}
        self.dcnt = {}
        self.last_w = {}
        self.readers = {}
        self.seen = {e: {} for e in self.eng}

    def _tok_sem(self, tok):
        if tok[0] == 'e':
            return self.sem[tok[1]], tok[2]
        return self.dsem[tok[1]], tok[2]

    def _wait(self, eng, toks):
        for tok in toks:
            if tok[0] == 'e' and tok[1] == eng:
                if eng == 'pe' or not SAME_ENG_SYNC:
                    continue
            key = (tok[0], tok[1])
            if self.seen[eng].get(key, 0) >= tok[2]:
                continue
            self.seen[eng][key] = tok[2]
            sem, val = self._tok_sem(tok)
            self.eng[eng].wait_ge(sem, val)

    def _deps(self, reads, writes):
        deps = []
        for r in reads:
            t = self.last_w.get(r)
            if t is not None:
                deps.append(t)
        for w in writes:
            t = self.last_w.get(w)
            if t is not None:
                deps.append(t)
            deps.extend(self.readers.get(w, ()))
        return deps

    def _commit(self, tok, reads, writes):
        for w in writes:
            self.last_w[w] = tok
            self.readers[w] = []
        for r in reads:
            if r in writes:
                continue
            self.readers.setdefault(r, []).append(tok)

    def op(self, eng, fn, reads=(), writes=(), sig=True):
        self._wait(eng, self._deps(reads, writes))
        inst = fn(self.eng[eng])
        if sig:
            self.cnt[eng] += 1
            inst.then_inc(self.sem[eng], 1)
            tok = ('e', eng, self.cnt[eng])
            self.pend[eng] = False
        else:
            assert eng == 'pe'
            tok = ('e', eng, self.cnt[eng] + 1)
            self.pend[eng] = True
        self._commit(tok, reads, writes)
        return tok

    def dma(self, key, fn, reads=(), writes=(), q='sp'):
        if key not in self.dsem:
            self.dsem[key] = self.stack.enter_context(
                self.nc.semaphore('d_%d' % len(self.dsem)))
            self.dcnt[key] = 0
        self._wait(q, self._deps(reads, writes))
        insts = fn(self.eng[q])
        if not isinstance(insts, (list, tuple)):
            insts = [insts]
        for i in insts:
            i.then_inc(self.dsem[key], 16)
            self.dcnt[key] += 16
        tok = ('d', key, self.dcnt[key])
        self._commit(tok, reads, writes)
        return tok

    def barrier(self):
        for e in self.CE:
            assert not self.pend[e], e
        toks = [('e', e, self.cnt[e]) for e in self.CE if self.cnt[e] > 0]
        toks += [('d', k, v) for k, v in self.dcnt.items() if v > 0]
        for e in self.eng:
            self._wait(e, [t for t in toks if not (t[0] == 'e' and t[1] == e)])
        self.last_w = {}
        self.readers = {}


def piece_index():
    idx = {}
    n = 0
    for f in ('f1', 'f2'):
        for kind in ('g', 'u', 'd'):
            for j in range(NFC):
                idx[(f, kind, j)] = n
                n += 1
    for j in range(37):
        idx[('win', j)] = n
        n += 1
    for nm in ('wout', 'xq', 'xk', 'xv', 'xo'):
        for j in range(16):
            idx[(nm, j)] = n
            n += 1
    return idx, n


def fm_piece(W, j, width=128):
    blk = W[:, j * width:(j + 1) * width].reshape(16, 128, width).transpose(1, 0, 2)
    out = np.zeros((128, 2048), np.float32)
    out[:, :16 * width] = blk.reshape(128, 16 * width)
    return out


def tm_piece(W, j):
    nb, half = j // 2, j % 2
    blk = W[half * 1024:(half + 1) * 1024, nb * 256:(nb + 1) * 256].reshape(8, 128, 256).transpose(1, 0, 2)
    return np.ascontiguousarray(blk).reshape(128, 2048)


def build_wall(inp):
    idx, n = piece_index()
    wall = np.empty((n, 128, 2048), np.float32)
    for f, pre in (('f1', 'ffn1'), ('f2', 'ffn2')):
        Wg, Wu, Wd = inp[pre + '_w_gate'][0], inp[pre + '_w_up'][0], inp[pre + '_w_down'][0]
        for j in range(NFC):
            wall[idx[(f, 'g', j)]] = fm_piece(Wg, j)
            wall[idx[(f, 'u', j)]] = fm_piece(Wu, j)
            wall[idx[(f, 'd', j)]] = Wd[j * 128:(j + 1) * 128, :]
    Win = inp['w_in'][0]
    for j in range(36):
        wall[idx[('win', j)]] = fm_piece(Win, j)
    wall[idx[('win', 36)]] = fm_piece(Win[:, 4608:4624], 0, 16)
    for j in range(16):
        wall[idx[('wout', j)]] = tm_piece(inp['w_out'][0], j)
        wall[idx[('xq', j)]] = fm_piece(inp['w_xq'][0], j)
        wall[idx[('xk', j)]] = tm_piece(inp['w_xk'][0], j)
        wall[idx[('xv', j)]] = tm_piece(inp['w_xv'][0], j)
        wall[idx[('xo', j)]] = tm_piece(inp['w_xo'][0], j)
    return wall


C_PRE = {'f1': 0, 'mix': 16, 'xa': 32, 'f2': 48, 'mem': 64}
C_CW, C_CB, C_LG, C_LB = 80, 328, 336, 344
C_SW, C_SB, C_NG, C_DS, C_DTB, C_AL, C_FLAG = 352, 400, 412, 420, 428, 444, 460
NCST = 464
M_ID, M_U, M_L, M_ONE, M_NEG, M_MSK = 0, 128, 256, 384, 512, 640
M_UB, M_LB, M_NEGB, M_MSKB, M_SAME, M_BM = 768, 896, 1024, 1152, 1280, 1408
M_BMC = 2432
NCM = 2448


def build_consts(inp, half):
    c = np.zeros((128, NCST), np.float32)

    def col16(v):
        return v.reshape(16, 128).T
    for nm, key in (('f1', 'ffn1_pre_g'), ('mix', 'mix_pre_g'), ('xa', 'xattn_pre_g'),
                    ('f2', 'ffn2_pre_g'), ('mem', 'mem_norm_g')):
        c[:, C_PRE[nm]:C_PRE[nm] + 16] = col16(inp[key][0])
    cw = inp['conv_w'][0]
    c[:, C_CW:C_CW + 248] = cw.T.reshape(8, 128, 31).transpose(1, 0, 2).reshape(128, 248)
    c[:, C_CB:C_CB + 8] = inp['conv_b'][0].reshape(8, 128).T
    c[:, C_LG:C_LG + 8] = inp['conv_ln_g'][0].reshape(8, 128).T
    c[:, C_LB:C_LB + 8] = inp['conv_ln_b'][0].reshape(8, 128).T
    sw = inp['ssm_conv_w'][0]
    c[:, C_SW:C_SW + 48] = sw.T.reshape(12, 128, 4).transpose(1, 0, 2).reshape(128, 48)
    c[:, C_SB:C_SB + 12] = inp['ssm_conv_b'][0].reshape(12, 128).T
    c[:, C_NG:C_NG + 8] = inp['ssm_norm_g'][0].reshape(8, 128).T
    c[:, C_DS:C_DS + 8] = np.repeat(inp['d_skip'][0], 64).reshape(8, 128).T
    c[:, C_DTB:C_DTB + 16] = inp['dt_bias'][0][None, :]
    c[:, C_AL:C_AL + 16] = inp['a_log'][0][None, :]
    c[:, C_FLAG] = float(half)
    return c


def build_cmat():
    m = np.zeros((128, NCM), np.float32)
    j = np.arange(128)[:, None]
    l = np.arange(128)[None, :]
    m[:, M_ID:M_ID + 128] = (j == l)
    m[:, M_U:M_U + 128] = (j <= l)
    m[:, M_L:M_L + 128] = (j > l)
    m[:, M_ONE:M_ONE + 128] = 1.0
    m[:, M_NEG:M_NEG + 128] = np.where(l < j, -30000.0, 0.0)
    m[:, M_MSK:M_MSK + 128] = (l >= j)
    same = (j // 4 == l // 4) & (j < 64) & (l < 64)
    m[:, M_UB:M_UB + 128] = same & (j <= l)
    m[:, M_LB:M_LB + 128] = same & (j > l)
    m[:, M_NEGB:M_NEGB + 128] = np.where(same & (l >= j), 0.0, -30000.0)
    m[:, M_MSKB:M_MSKB + 128] = same & (l >= j)
    m[:, M_SAME:M_SAME + 128] = same
    bm = (np.arange(64)[None, :] // 4 == np.arange(16)[:, None]).astype(np.float32)
    m[:, M_BM:M_BM + 1024] = bm.reshape(1, 1024)
    m[:64, M_BMC:M_BMC + 16] = bm.T
    return m


NTM = 1088
NTX = 1118
ARENA = 49152


def build(seq, stop=None):
    nc = bass.Bass("TRN2", target_bir_lowering=False)
    pidx, npieces = piece_index()

    def DT(name, shape, kind=None):
        if kind is None:
            return nc.dram_tensor(name, shape, F32)
        return nc.dram_tensor(name, shape, F32, kind=kind)
    xin = DT("xin", [NTM, 2048], "ExternalInput")
    mem = DT("mem", [256, 2048], "ExternalInput")
    ck = DT("ck", [16 * 256, 2048], "ExternalInput")
    cv = DT("cv", [16 * 256, 2048], "ExternalInput")
    sconv = DT("sconv", [16 * 30, 1024], "ExternalInput")
    sssc = DT("sssc", [16 * 3, 1536], "ExternalInput")
    sst = DT("sst", [16 * 1024, 128], "ExternalInput")
    wall = DT("wall", [npieces * 128, 2048], "ExternalInput")
    cst_d = DT("cst", [128, NCST], "ExternalInput")
    cmat_d = DT("cmat", [128, NCM], "ExternalInput")
    postg = DT("postg", [4, 2048], "ExternalInput")
    yout = DT("yout", [NTM, 2048], "ExternalOutput")
    kout = DT("kout", [256, 2048], "ExternalOutput")
    vout = DT("vout", [256, 2048], "ExternalOutput")
    convp = DT("convp", [30, 1024], "ExternalOutput")
    sscp = DT("sscp", [3, 1536], "ExternalOutput")
    sstp = DT("sstp", [1024, 128], "ExternalOutput")
    convs = DT("convs", [16 * 30, 1024], "ExternalOutput")
    sscs = DT("sscs", [16 * 3, 1536], "ExternalOutput")
    ssts = DT("ssts", [16 * 1024, 128], "ExternalOutput")
    x1 = DT("x1", [NTM, 2048])
    x2 = DT("x2", [NTM, 2048])
    x3 = DT("x3", [NTM, 2048])
    ex_i1 = DT("ex_i1", [128, 240])
    ex_o1 = DT("ex_o1", [256, 240])
    ex_i2 = DT("ex_i2", [128, 1024])
    ex_o2 = DT("ex_o2", [256, 1024])
    PAIRS = [[0, 1], [2, 3], [4, 5], [6, 7]]

    def rows(t, r0, n, c0=0, w=None):
        a = t.ap()
        w = a.shape[1] - c0 if w is None else w
        return a[r0:r0 + n, c0:c0 + w]

    st = ExitStack()
    with st:
        P = Prog(nc, st)
        arena = st.enter_context(nc.sbuf_tensor("arena", [128, ARENA], F32))
        ps = st.enter_context(nc.psum_tensor("ps", [128, 8, 512], F32))
        psb = ps[:].rearrange("p b f -> p (b f)").bitcast(BF16)

        def V(off, dt, shape):
            n = int(np.prod(shape))
            assert off + (n if dt == F32 else (n + 1) // 2) <= ARENA, (off, n)
            if dt == F32:
                v = arena[:, off:off + n]
            else:
                v = arena[:, off:off + (n + 1) // 2].bitcast(BF16)[:, 0:n]
            if len(shape) == 2:
                v = v.rearrange("p (a b) -> p a b", a=shape[0])
            elif len(shape) == 3:
                v = v.rearrange("p (a b c) -> p a b c", a=shape[0], b=shape[1])
            return v

        def PSB(bank, nbanks, shape):
            n = int(np.prod(shape))
            assert n <= nbanks * 1024
            v = psb[:, bank * 1024:bank * 1024 + n]
            if len(shape) == 2:
                v = v.rearrange("p (a b) -> p a b", a=shape[0])
            return v

        o = 0
        wst = V(o, F32, [3, 2048]); o += 6144
        wbf = V(o, BF16, [6, 2048]); o += 6144
        cst = V(o, F32, [NCST]); o += NCST
        cm = V(o, F32, [NCM]); o += NCM
        identb = V(o, BF16, [128]); o += 64
        ssm = V(o, F32, [64]); o += 64
        Aneg = V(o, F32, [16]); o += 16
        dtp = V(o, F32, [9, 16]); o += 144
        R0 = o
        identf = cm[:, M_ID:M_ID + 128]
        onesf = cm[:, M_ONE:M_ONE + 128]
        flag = cst[:, C_FLAG:C_FLAG + 1]

        class WS:
            NS, NB, DC, DD = 3, 6, 4, 6

            def __init__(self):
                self.i = 0
                self.req = []
                self.nd = 0
                self.ncst = 0

            def _dma(self, j, pid):
                s = j % self.NS
                P.dma(('wst', s), lambda e: e.dma_start(out=wst[:, s, :], in_=rows(wall, pid * 128, 128)),
                      writes=[('wst', s)])

            def _cast(self, j):
                s, t = j % self.NS, j % self.NB
                P.op('pool', lambda e: e.tensor_copy(out=wbf[:, t, :], in_=wst[:, s, :]),
                     reads=[('wst', s)], writes=[('wbf', t)])

            def next(self, name):
                pid = pidx[name]
                i = self.i
                self.i += 1
                self.req.append(pid)
                if seq is not None:
                    assert seq[i] == pid
                    src, dc, dd = seq, self.DC, self.DD
                else:
                    src, dc, dd = self.req, 0, 0
                last = len(src) - 1

                def ensure_cast(j):
                    while self.ncst <= j:
                        ensure_dma(self.ncst)
                        self._cast(self.ncst)
                        self.ncst += 1

                def ensure_dma(j):
                    while self.nd <= j:
                        if self.nd - self.NS >= 0:
                            ensure_cast(self.nd - self.NS)
                        self._dma(self.nd, src[self.nd])
                        self.nd += 1
                ensure_dma(min(i + dd, last))
                ensure_cast(min(i + dc, last))
                return ('wbf', i % self.NB), wbf[:, i % self.NB, :]
        ws = WS()

        def mm(out, lhsT, rhs, start, stop, reads, writes, sig):
            P.op('pe', lambda e: e.matmul(out, lhsT=lhsT, rhs=rhs, start=start, stop=stop),
                 reads=reads, writes=writes, sig=sig)

        def tr(out, in_, ident, reads, writes, sig):
            P.op('pe', lambda e: e.transpose(out=out, in_=in_, identity=ident),
                 reads=reads, writes=writes, sig=sig)

        def act(out, in_, func, reads, writes, **kw):
            if func == AF.Copy and 'bias' not in kw and 'accum_out' not in kw:
                sc = kw.get('scale', None)
                eng = 'dve' if ('PSUM' in str(in_.space).upper() or 'PSUM' in str(out.space).upper() or sc is not None) else 'pool'
                if sc is None:
                    P.op(eng, lambda e: e.tensor_copy(out=out, in_=in_), reads=reads, writes=writes)
                else:
                    P.op(eng, lambda e: e.tensor_scalar(out=out, in0=in_, scalar1=sc, scalar2=None, op0=ALU.mult),
                         reads=reads, writes=writes)
                return
            P.op('act', lambda e: e.activation(out=out, in_=in_, func=func, **kw), reads=reads, writes=writes)

        def tt(out, in0, in1, op, reads, writes):
            P.op('dve', lambda e: e.tensor_tensor(out=out, in0=in0, in1=in1, op=op), reads=reads, writes=writes)

        def ts(out, in0, s1, s2, op0, op1, reads, writes):
            if op1 is None:
                P.op('dve', lambda e: e.tensor_scalar(out=out, in0=in0, scalar1=s1, scalar2=None, op0=op0), reads=reads, writes=writes)
            else:
                P.op('dve', lambda e: e.tensor_scalar(out=out, in0=in0, scalar1=s1, scalar2=s2, op0=op0, op1=op1), reads=reads, writes=writes)

        def stt(out, in0, scalar, in1, op0, op1, reads, writes):
            P.op('dve', lambda e: e.scalar_tensor_tensor(out=out, in0=in0, scalar=scalar, in1=in1, op0=op0, op1=op1),
                 reads=reads, writes=writes)

        def exchange(tag, src_sb, ib, ob, dst_sb, rkeys, wkeys):
            P.dma(('exi', tag), lambda e: e.dma_start(out=ib.ap(), in_=src_sb), reads=rkeys, writes=[('ib', tag)])
            key = ('cc', tag)
            if key not in P.dsem:
                P.dsem[key] = st.enter_context(nc.semaphore('cc_' + tag))
                P.dcnt[key] = 0
            P._wait('pool', P._deps([('ib', tag)], [('ob', tag)]))
            ins = nc.gpsimd.collective_compute("AllGather", ALU.bypass, replica_groups=PAIRS,
                                               ins=[ib.ap().opt()], outs=[ob.ap().opt()])
            ins.then_inc(P.dsem[key])
            P.dcnt[key] += 1
            P._commit(('d', key, P.dcnt[key]), [('ib', tag)], [('ob', tag)])
            P.dma(('exo', tag), lambda e: e.dma_start(out=dst_sb, in_=ob.ap()[0:128, :]), reads=[('ob', tag)], writes=wkeys)

        P.dma('c0', lambda e: [e.dma_start(out=cst, in_=cst_d.ap()), e.dma_start(out=cm, in_=cmat_d.ap())],
              writes=['cst', 'cm'])
        P.op('dve', lambda e: e.tensor_copy(out=identb, in_=identf), reads=['cm'], writes=['identb'])
        act(Aneg, cst[:, C_AL:C_AL + 16], AF.Exp, ['cst'], ['Aneg'])
        ts(Aneg, Aneg, -1.0, None, ALU.mult, None, ['Aneg'], ['Aneg'])
        P.barrier()

        def rstd_chain(n, sb, s, eps_scale):
            ts(ssm[:n, sb + 1:sb + 2], ssm[:n, sb:sb + 1], eps_scale, EPS, ALU.mult, ALU.add, [('ss', s)], [('ss', s)])
            act(ssm[:n, sb + 2:sb + 3], ssm[:n, sb + 1:sb + 2], AF.Sqrt, [('ss', s)], [('ss', s)])
            P.op('dve', lambda e: e.reciprocal(out=ssm[:n, sb + 3:sb + 4], in_=ssm[:n, sb + 2:sb + 3]),
                 reads=[('ss', s)], writes=[('ss', s)])

        def prenorm(src, tiles, gcol, xT, tmp_off):
            xt = V(tmp_off, F32, [2, 2048])
            junk = V(tmp_off + 4096, BF16, [2048])
            xs = V(tmp_off + 5120, BF16, [2, 2048])
            for ti, (r0, n) in enumerate(tiles):
                s = ti % 2
                sb = 8 * s
                P.dma(('xt', s), lambda e: e.dma_start(out=xt[:n, s, :], in_=src(r0, n)), writes=[('xt', s)])
                act(junk[:n], xt[:n, s, :], AF.Square, [('xt', s)], ['junk', ('ss', s)], accum_out=ssm[:n, sb:sb + 1])
                rstd_chain(n, sb, s, 1.0 / 2048)
                act(xs[:n, s, :], xt[:n, s, :], AF.Copy, [('xt', s), ('ss', s)], [('xs', s)], scale=ssm[:n, sb + 3:sb + 4])
                pT = PSB(2 * s, 2, [16, 128])
                for k in range(16):
                    tr(pT[:, k, :n], xs[:n, s, k * 128:(k + 1) * 128], identb[:n, :n],
                       [('xs', s), 'identb'], [('ps', 2 * s), ('ps', 2 * s + 1)], k == 15)
                tt(xT[:, :, r0:r0 + n], pT[:, :, :n], gcol.unsqueeze(2).to_broadcast([128, 16, n]), ALU.mult,
                   [('ps', 2 * s), ('ps', 2 * s + 1), 'cst'], ['xT'])

        def postnorm(yfn, tiles, prow, scale, xold, xnew, tmp_off):
            xo = V(tmp_off, F32, [2, 2048])
            junk = V(tmp_off + 4096, BF16, [2048])
            gb = V(tmp_off + 5120, F32, [2048])
            P.dma('gb', lambda e: e.dma_start(out=gb, in_=bass.AP(postg, prow * 2048, [[0, 128], [1, 2048]])),
                  writes=['gb'])
            for ti, (r0, n) in enumerate(tiles):
                s = ti % 2
                sb = 8 * s
                y = yfn(ti)
                yk = ('yacc', ti)
                P.dma(('xo', s), lambda e: e.dma_start(out=xo[:n, s, :], in_=xold(r0, n)), writes=[('xo', s)])
                act(junk[:n], y[:n], AF.Square, [yk], ['junk', ('ss', s)], accum_out=ssm[:n, sb:sb + 1])
                rstd_chain(n, sb, s, 1.0 / 2048)
                stt(y[:n], y[:n], ssm[:n, sb + 3:sb + 4], gb[:n], ALU.mult, ALU.mult, [yk, ('ss', s), 'gb'], [yk])
                stt(y[:n], y[:n], float(scale), xo[:n, s, :], ALU.mult, ALU.add, [yk, ('xo', s)], [yk])
                P.dma(('yo', ti % 4), lambda e: e.dma_start(out=xnew(r0, n), in_=y[:n]), reads=[yk], writes=[('xnew', ti)])

        def mk_tiles(ntok):
            return [(r, min(128, ntok - r)) for r in range(0, ntok, 128)]

        def mk_blocks(ntok):
            return [(c, min(512, ntok - c)) for c in range(0, ntok, 512)]

        TILES, BLOCKS = mk_tiles(NTM), mk_blocks(NTM)
        BLOCKSX = mk_blocks(NTX)

        def ffn(f, gkey, prow, src, dst):
            ntok = NTM
            tiles, blocks = TILES, BLOCKS
            nt = len(tiles)
            xT = V(R0, BF16, [16, ntok])
            o1 = R0 + 8 * ntok
            prenorm(src, tiles, cst[:, C_PRE[gkey]:C_PRE[gkey] + 16], xT, o1)
            P.barrier()
            yacc = V(o1, F32, [nt, 2048])
            o2 = o1 + nt * 2048
            hT = V(o2, BF16, [4, ntok])
            o3 = o2 + 2 * ntok
            sg = V(o3, BF16, [2, 512])
            cnt = {'t': 0, 'q': 0}

            def gate_up(j):
                kg, wg = ws.next((f, 'g', j))
                ku, wu = ws.next((f, 'u', j))
                wg = wg.rearrange("p (k c) -> p k c", k=16)
                wu = wu.rearrange("p (k c) -> p k c", k=16)
                for (c0, n) in blocks:
                    t = cnt['t'] % 2
                    cnt['t'] += 1
                    for k in range(16):
                        mm(ps[:, t, :n], wg[:, k, :], xT[:, k, c0:c0 + n], k == 0, k == 15, [kg, 'xT'], [('ps', t)], k == 15)
                    for k in range(16):
                        mm(ps[:, 2 + t, :n], wu[:, k, :], xT[:, k, c0:c0 + n], k == 0, k == 15, [ku, 'xT'], [('ps', 2 + t)], k == 15)
                    act(sg[:, t, :n], ps[:, t, :n], AF.Silu, [('ps', t)], [('sg', t)])
                    tt(hT[:, j % 4, c0:c0 + n], sg[:, t, :n], ps[:, 2 + t, :n], ALU.mult, [('sg', t), ('ps', 2 + t)], [('hT', j % 4)])

            def down(grp, first):
                wd = [ws.next((f, 'd', j)) for j in grp]
                for ti, (r0, n) in enumerate(tiles):
                    for nb in range(4):
                        q = 4 + cnt['q'] % 4
                        cnt['q'] += 1
                        for gi, j in enumerate(grp):
                            mm(ps[:n, q, :], hT[:, j % 4, r0:r0 + n], wd[gi][1][:, nb * 512:(nb + 1) * 512],
                               gi == 0, gi == len(grp) - 1, [wd[gi][0], ('hT', j % 4)], [('ps', q)], gi == len(grp) - 1)
                        ysl = yacc[:n, ti, nb * 512:(nb + 1) * 512]
                        if first:
                            act(ysl, ps[:n, q, :], AF.Copy, [('ps', q)], [('yacc', ti)])
                        else:
                            tt(ysl, ysl, ps[:n, q, :], ALU.add, [('ps', q), ('yacc', ti)], [('yacc', ti)])

            groups = [list(range(j, min(j + 2, NFC))) for j in range(0, NFC, 2)]
            for gi, grp in enumerate(groups):
                for j in grp:
                    gate_up(j)
                if gi > 0:
                    down(groups[gi - 1], gi == 1)
            down(groups[-1], False)
            P.barrier()
            postnorm(lambda ti: yacc[:, ti, :], tiles, prow, 0.5, src, dst, R0)
            P.barrier()

        def linear_tm(name, inT, tiles, acc):
            cntq = 0
            for nb in range(8):
                p0 = ws.next((name, 2 * nb))
                p1 = ws.next((name, 2 * nb + 1))
                for ti, (r0, n) in enumerate(tiles):
                    q = 4 + cntq % 4
                    cntq += 1
                    for k in range(16):
                        pk, pa = (p0, p1)[k // 8]
                        mm(ps[:n, q, 0:256], inT(k)[:, r0:r0 + n], pa[:, (k % 8) * 256:(k % 8 + 1) * 256],
                           k == 0, k == 15, [pk, 'linin'], [('ps', q)], k == 15)
                    act(acc(ti)[:n, nb * 256:(nb + 1) * 256], ps[:n, q, 0:256], AF.Copy, [('ps', q)], [('yacc', ti)])

        def win_fm(j, xT, blocks, evac, name='win'):
            kw, w = ws.next((name, j))
            w = w.rearrange("p (k c) -> p k c", k=16)
            for bi, (c0, n) in enumerate(blocks):
                t = win_fm.t % 4
                win_fm.t += 1
                for k in range(16):
                    mm(ps[:, t, :n], w[:, k, :], xT[:, k, c0:c0 + n], k == 0, k == 15, [kw, 'xT'], [('ps', t)], k == 15)
                evac(bi, c0, n, ps[:, t, :n], ('ps', t))
        win_fm.t = 0

        def dwconv(buf, acc, wcol, bcol, ntap, L, bkey='cbuf', akey='cacc'):
            ts(acc, buf[:, :, 0:L], wcol[:, 0:1], bcol, ALU.mult, ALU.add, [bkey, 'cst'], [akey])
            for k in range(1, ntap):
                stt(acc, buf[:, :, k:k + L], wcol[:, k:k + 1], acc, ALU.mult, ALU.add, [bkey, akey, 'cst'], [akey])

        if stop == 1:
            ffn('f1', 'f1', 0, lambda r0, n: rows(xin, r0, n), lambda r0, n: rows(yout, r0, n))
            return nc, ws.req
        if stop == 1.5:
            ffn('f1', 'f1', 0, lambda r0, n: rows(xin, r0, n), lambda r0, n: rows(x1, r0, n))
            ffn('f2', 'f2', 3, lambda r0, n: rows(x1, r0, n), lambda r0, n: rows(x2, r0, n))
            ffn('f1', 'f1', 0, lambda r0, n: rows(x2, r0, n), lambda r0, n: rows(x3, r0, n))
            ffn('f2', 'f2', 3, lambda r0, n: rows(x3, r0, n), lambda r0, n: rows(yout, r0, n))
            return nc, ws.req
        ffn('f1', 'f1', 0, lambda r0, n: rows(xin, r0, n), lambda r0, n: rows(x1, r0, n))

        def mixer():
            tiles, blocks = TILES, BLOCKSX
            xT = V(R0, BF16, [16, NTX]); o1 = R0 + 8 * NTX
            prenorm(lambda r0, n: rows(x1, r0, n), tiles, cst[:, C_PRE['mix']:C_PRE['mix'] + 16], xT, o1)
            hb = V(o1 + 8000, BF16, [16, 30])
            hb2 = V(o1 + 8300, BF16, [16, 30])
            P.op('dve', lambda e: e.tensor_copy(out=hb, in_=xT[:, :, 994:1024]), reads=['xT'], writes=['hb'])
            exchange('h', arena[:, o1 + 8000:o1 + 8240], ex_i1, ex_o1, arena[:, o1 + 8300:o1 + 8540], ['hb'], ['hb2'])
            ts(xT[:, :, 1088:1118], hb2, flag, None, ALU.mult, None, ['hb2', 'cst'], ['xT'])
            P.barrier()
            if stop == 2.1:
                return
            aT = V(o1, BF16, [8, NTM]); o1 += 4 * NTM
            oA = o1
            assert oA == R0 + 13296
            yc = V(oA, F32, [8, NTM]); o2 = oA + 8 * NTM
            cbp2 = [V(o2, F32, [1, 1054])] * 2; o2 += 1054
            cbs2 = [V(o2, F32, [16, 34])] * 2; o2 += 544
            ubp2 = [V(o2, BF16, [1, 1054]), V(o2 + 527, BF16, [1, 1054])]; o2 += 1054
            ubs2 = [V(o2, BF16, [16, 34]), V(o2 + 272, BF16, [16, 34])]; o2 += 544
            Dg = V(o2, BF16, [31, 128]); o2 += 1984
            sgf = V(o2, F32, [1120]); o2 += 1120
            sgks = [sgf[:, 0:512], sgf[:, 512:1024], sgf[:, 1024:1120]]
            scT = V(o2, F32, [8, 480]); o2 += 3840
            tl = V(o2, F32, [128]); o2 += 128
            ld = V(o2, F32, [1024]); o2 += 1024
            for i in range(4):
                P.dma('ld', lambda e: e.dma_start(out=ld[:120], in_=rows(sconv, i * 120, 120)), writes=['ld'])
                P.dma('cp1', lambda e: [e.dma_start(out=rows(convs, (4 * i + s_) * 30, 26), in_=ld[30 * s_ + 4:30 * s_ + 30, :])
                                        for s_ in range(4)], reads=['ld'], writes=[('convs_old', i)])
                for c in range(8):
                    tr(ps[:, 4 + c % 2, 0:120], ld[:120, c * 128:(c + 1) * 128], identf[:120, :120], ['ld', 'cm'],
                       [('ps', 4 + c % 2)], True)
                    act(scT[:, c, i * 120:(i + 1) * 120], ps[:, 4 + c % 2, 0:120], AF.Copy, [('ps', 4 + c % 2)], ['scT'])
            for c in range(8):
                cbp, cbs, cbk = cbp2[0], cbs2[0], 'cbuf'
                ubp, ubs, ubk = ubp2[c % 2], ubs2[c % 2], ('ubuf', c % 2)

                def ev_g(bi, c0, n, pa, pk):
                    act(sgks[bi][:, :n], pa, AF.Sigmoid, [pk], [('sgk', bi)])

                def ev_v(bi, c0, n, pa, pk):
                    if bi < 2:
                        tt(cbp[:, 0, 30 + c0:30 + c0 + n], sgks[bi][:, :n], pa, ALU.mult, [pk, ('sgk', bi)], [cbk])
                    else:
                        tt(cbs[:, :, 30:34], sgks[bi][:, 0:64].rearrange("p (s t) -> p s t", s=16),
                           pa[:, 0:64].rearrange("p (s t) -> p s t", s=16), ALU.mult, [pk, ('sgk', bi)], [cbk])
                        tt(cbp[:, 0, 0:30], sgks[bi][:, 64:94], pa[:, 64:94], ALU.mult, [pk, ('sgk', bi)], [cbk])
                win_fm(8 + c, xT, blocks, ev_g)
                act(cbs[:, :, 0:30], scT[:, c, :].rearrange("p (s t) -> p s t", s=16), AF.Copy, ['scT'], [cbk])
                win_fm(c, xT, blocks, ev_v)
                wcol = cst[:, C_CW + 31 * c:C_CW + 31 * c + 31]
                bcol = cst[:, C_CB + c:C_CB + c + 1]
                P.op('pool', lambda e: e.tensor_copy(out=ubp, in_=cbp), reads=[cbk], writes=[ubk])
                P.op('pool', lambda e: e.tensor_copy(out=ubs, in_=cbs), reads=[cbk], writes=[ubk])
                for k in range(31):
                    ts(Dg[:, k, :], identb, wcol[:, k:k + 1], None, ALU.mult, None, ['identb', 'cst'], ['Dg'])
                for bi, (c0, n) in enumerate(((0, 512), (512, 512), (1024, 64))):
                    bk = 4 + (3 * c + bi) % 2
                    for k in range(31):
                        rhs = ubp[:, 0, c0 + k:c0 + k + n] if bi < 2 else ubs[:, :, k:k + 4]
                        outp = ps[:, bk, :n] if bi < 2 else ps[:, bk, 0:64].rearrange("p (s t) -> p s t", s=16)
                        mm(outp, Dg[:, k, :], rhs, k == 0, k == 30, ['Dg', ubk], [('ps', bk)], k == 30)
                    ts(yc[:, c, c0:c0 + n], ps[:, bk, :n], bcol, None, ALU.add, None, [('ps', bk), 'cst'], [('cacc', c)])
                tr(ps[:30, 6, 0:128], cbp[:, 0, 1024:1054], identf, [cbk, 'cm'], [('ps', 6)], True)
                act(tl[:30, :], ps[:30, 6, 0:128], AF.Copy, [('ps', 6)], ['tl'])
                P.dma('o1', lambda e: e.dma_start(out=rows(convp, 0, 30, c * 128, 128), in_=tl[:30, :]), reads=['tl'], writes=[('convp', c)])
                P.op('dve', lambda e: e.tensor_copy(out=sgks[0][:, 0:64].rearrange("p (s t) -> p s t", s=16), in_=cbs[:, :, 30:34]),
                     reads=[cbk], writes=[('sgk', 0)])
                tr(ps[:64, 7, 0:128], sgks[0][:, 0:64], identf, [('sgk', 0), 'cm'], [('ps', 7)], True)
                act(ld[:64, c * 128:(c + 1) * 128], ps[:64, 7, 0:128], AF.Copy, [('ps', 7)], ['ld2'])
            P.dma('o2', lambda e: [e.dma_start(out=rows(convs, s_ * 30 + 26, 4), in_=ld[4 * s_:4 * s_ + 4, :]) for s_ in range(16)],
                  reads=['ld2'], writes=['convs_new'])
            P.barrier()
            sq = V(oA + 8 * NTM, F32, [512])
            st1 = V(oA + 8 * NTM + 512, F32, [3, 512])
            for (c0, n) in BLOCKS:
                for c in range(8):
                    mm(ps[:, 0, :n], onesf, yc[:, c, c0:c0 + n], c == 0, c == 7, ['yc', 'cm'], [('ps', 0)], c == 7)
                for c in range(8):
                    act(sq[:, :n], yc[:, c, c0:c0 + n], AF.Square, ['yc'], ['sq'])
                    mm(ps[:, 1, :n], onesf, sq[:, :n], c == 0, c == 7, ['sq', 'cm'], [('ps', 1)], True)
                mean, var, rstd = st1[:, 0, :n], st1[:, 1, :n], st1[:, 2, :n]
                act(mean, ps[:, 0, :n], AF.Copy, [('ps', 0)], ['st1'], scale=1.0 / 1024)
                tt(var, mean, mean, ALU.mult, ['st1'], ['st1'])
                stt(var, ps[:, 1, :n], 1.0 / 1024, var, ALU.mult, ALU.subtract, ['st1', ('ps', 1)], ['st1'])
                ts(var, var, EPS, None, ALU.add, None, ['st1'], ['st1'])
                act(var, var, AF.Sqrt, ['st1'], ['st1'])
                P.op('dve', lambda e: e.reciprocal(out=rstd, in_=var), reads=['st1'], writes=['st1'])
                for c in range(8):
                    tt(sq[:, :n], yc[:, c, c0:c0 + n], mean, ALU.subtract, ['yc', 'st1'], ['sq'])
                    tt(sq[:, :n], sq[:, :n], rstd, ALU.mult, ['sq', 'st1'], ['sq'])
                    act(aT[:, c, c0:c0 + n], sq[:, :n], AF.Silu, ['sq', 'cst'], ['aT'],
                        scale=cst[:, C_LG + c:C_LG + c + 1], bias=cst[:, C_LB + c:C_LB + c + 1])
            P.barrier()
            if stop == 2.2:
                return
            xsT = V(oA, F32, [8, NTM]); o2 = oA + 8 * NTM
            bcT = V(o2, BF16, [4, NTM]); o2 += 2 * NTM
            szT = V(o2, BF16, [8, NTM]); o2 += 4 * NTM
            oB = o2
            ynT = V(o2, BF16, [8, NTM])
            oTail = o2 + 4 * NTM
            xbp = V(o2, F32, [1, 1028]); o2 += 1028
            xbs = V(o2, F32, [16, 7]); o2 += 112
            cac = V(o2, F32, [NTM]); o2 += NTM
            s3T = V(o2, F32, [12, 48]); o2 += 576
            tl3 = V(o2, F32, [128]); o2 += 128
            ld3 = V(o2, F32, [1536]); o2 += 1536
            raw = V(o2, F32, [48]); o2 += 48
            P.dma('ld', lambda e: e.dma_start(out=ld3[:48], in_=sssc.ap()), writes=['ld'])
            for c in range(12):
                tr(ps[:, 4 + c % 2, 0:48], ld3[:48, c * 128:(c + 1) * 128], identf[:48, :48], ['ld', 'cm'], [('ps', 4 + c % 2)], True)
                act(s3T[:, c, :], ps[:, 4 + c % 2, 0:48], AF.Copy, [('ps', 4 + c % 2)], ['s3T'])
            P.barrier()
            for c in range(12):
                def ev(bi, c0, n, pa, pk):
                    if bi < 2:
                        act(xbp[:, 0, 3 + c0:3 + c0 + n], pa, AF.Copy, [pk], ['cbuf'])
                    else:
                        act(xbs[:, :, 3:7], pa[:, 0:64].rearrange("p (s t) -> p s t", s=16), AF.Copy, [pk], ['cbuf'])
                        act(xbp[:, 0, 0:3], pa[:, 91:94], AF.Copy, [pk], ['cbuf'])
                act(xbs[:, :, 0:3], s3T[:, c, :].rearrange("p (s t) -> p s t", s=16), AF.Copy, ['s3T'], ['cbuf'])
                win_fm(24 + c, xT, blocks, ev)
                wcol = cst[:, C_SW + 4 * c:C_SW + 4 * c + 4]
                bcol = cst[:, C_SB + c:C_SB + c + 1]
                dwconv(xbp, cac[:, 0:1024].unsqueeze(1), wcol, bcol, 4, 1024)
                dwconv(xbs, cac[:, 1024:1088].rearrange("p (s t) -> p s t", s=16), wcol, bcol, 4, 4)
                dst = xsT[:, c, :] if c < 8 else bcT[:, c - 8, :]
                act(dst, cac, AF.Silu, ['cacc'], ['xsT'])
                tr(ps[:3, 6, 0:128], xbp[:, 0, 1024:1027], identf, ['cbuf', 'cm'], [('ps', 6)], True)
                act(tl3[:3, :], ps[:3, 6, 0:128], AF.Copy, [('ps', 6)], ['tl'])
                P.dma('o1', lambda e: e.dma_start(out=rows(sscp, 0, 3, c * 128, 128), in_=tl3[:3, :]), reads=['tl'], writes=[('sscp', c)])
                P.op('dve', lambda e: e.tensor_copy(out=raw.rearrange("p (s t) -> p s t", s=16), in_=xbs[:, :, 4:7]), reads=['cbuf'], writes=['raw'])
                tr(ps[:48, 7, 0:128], raw, identf, ['raw', 'cm'], [('ps', 7)], True)
                act(ld3[:48, c * 128:(c + 1) * 128], ps[:48, 7, 0:128], AF.Copy, [('ps', 7)], ['ld2'])
            P.dma('o2', lambda e: e.dma_start(out=sscs.ap(), in_=ld3[:48, :]), reads=['ld2'], writes=['sscs'])
            for c in range(8):
                def evz(bi, c0, n, pa, pk):
                    nn = min(n, NTM - c0)
                    act(szT[:, c, c0:c0 + nn], pa[:, :nn], AF.Silu, [pk], ['szT'])
                win_fm(16 + c, xT, blocks, evz)
            kw, w = ws.next(('win', 36))
            w = w[:, 0:256].rearrange("p (k c) -> p k c", k=16)
            for ti, (r0, n) in enumerate(tiles):
                t = 4 + ti % 2
                for k in range(16):
                    mm(ps[:n, t, 0:16], xT[:, k, r0:r0 + n], w[:, k, :], k == 0, k == 15, [kw, 'xT'], [('ps', t)], k == 15)
                tt(dtp[:n, ti, :], ps[:n, t, 0:16], cst[:n, C_DTB:C_DTB + 16], ALU.add, [('ps', t), 'cst'], ['dtp'])
            for (a, b, n) in ((0, 8, 128), (8, 9, 64)):
                act(dtp[:n, a:b, :], dtp[:n, a:b, :], AF.Exp, ['dtp'], ['dtp'])
                act(dtp[:n, a:b, :], dtp[:n, a:b, :], AF.Ln, ['dtp'], ['dtp'], bias=1.0)
            P.barrier()
            if stop == 2.3:
                return
            o3 = R0
            xs_tm = V(o3, BF16, [1024]); o3 += 512
            b_tm = V(o3, BF16, [256]); o3 += 128
            xdt = V(o3, BF16, [1024]); o3 += 512
            xdte = V(o3, BF16, [1024]); o3 += 512
            abuf = V(o3, F32, [16]); o3 += 16
            eab = V(o3, F32, [48]); o3 += 48
            oAU = o3
            AU = V(o3, F32, [16, 128]); o3 += 2048
            oWT = o3
            WT = V(o3, BF16, [16, 128]); o3 += 1024
            xsb = V(o3, BF16, [128]); o3 += 64
            oEt = o3
            Et = V(o3, F32, [1, 512]); o3 += 512
            cbm = V(o3, F32, [2, 128]); o3 += 256
            negU = V(o3, F32, [128]); o3 += 128
            Hs = V(o3, BF16, [1024]); o3 += 512
            H = V(o3, F32, [1024]); o3 += 1024
            yg = V(o3, F32, [8, 128]); o3 += 1024
            yoS = V(o3, F32, [8, 64]); o3 += 512
            assert o3 <= R0 + 8 * NTX, (o3, R0)
            rs = V(oTail, F32, [2, 128])
            ysq = V(oTail + 256, F32, [128])
            cdc = V(oTail + 384, F32, [16])
            aex = V(oTail + 400, F32, [128])

            def to_tm(t0, n):
                pT = PSB(0, 2, [10, 128])
                for c in range(10):
                    if c < 8:
                        act(xsb[:, :n], xsT[:, c, t0:t0 + n], AF.Copy, ['xsT'], ['xsb'])
                        src = xsb[:, :n]
                    else:
                        src = bcT[:, c - 8, t0:t0 + n]
                    tr(pT[:n, c, :], src, identb, ['xsT', 'xsb'], [('ps', 0), ('ps', 1)], True)
                act(xs_tm[:n], pT[:n, 0:8, :].rearrange("p a b -> p (a b)"), AF.Copy, [('ps', 0), ('ps', 1)], ['xs_tm'])
                act(b_tm[:n], pT[:n, 8:10, :].rearrange("p a b -> p (a b)"), AF.Copy, [('ps', 0), ('ps', 1)], ['b_tm'])

            def scalars(dt_t, n, Um, Lm):
                tt(abuf[:n], dt_t, Aneg[:n], ALU.mult, ['dtp', 'Aneg'], ['abuf'])
                mm(ps[:n, 3, 0:16], Um[:n, :n], abuf[:n], True, True, ['abuf', 'cm'], [('ps', 3)], False)
                mm(ps[:n, 3, 16:32], Lm[:n, :n], abuf[:n], True, True, ['abuf', 'cm'], [('ps', 3)], False)
                mm(ps[:, 3, 32:48], onesf[:n, :], abuf[:n], True, True, ['abuf', 'cm'], [('ps', 3)], True)
                act(eab[:, 0:48], ps[:, 3, 0:48], AF.Exp, [('ps', 3)], ['eabuf'])
                tt(eab[:n, 16:32], eab[:n, 16:32], dt_t, ALU.mult, ['eabuf', 'dtp'], ['eabuf'])

            def mk_xdte(n):
                tt(xdte[:n].rearrange("p (h q) -> p h q", h=16), xs_tm[:n].rearrange("p (h q) -> p h q", h=16),
                   eab[:n, 16:32].unsqueeze(2).to_broadcast([n, 16, 64]), ALU.mult, ['xs_tm', 'eabuf'], ['xdte'])

            def state_step():
                for g in range(2):
                    mm(ps[:, 4 + g, :], b_tm[:, g * 128:(g + 1) * 128], xdte[:, g * 512:(g + 1) * 512], True, True,
                       ['b_tm', 'xdte'], [('ps', 4 + g)], True)
                H3 = H.rearrange("p (h q) -> p h q", h=16)
                tt(H3, H3, eab[:, 32:48].unsqueeze(2).to_broadcast([128, 16, 64]), ALU.mult, ['H', 'eabuf', 'Hs'], ['H'])
                for g in range(2):
                    tt(H[:, g * 512:(g + 1) * 512], H[:, g * 512:(g + 1) * 512], ps[:, 4 + g, :], ALU.add,
                       ['H', ('ps', 4 + g)], ['H'])

            Up, Lp = cm[:, M_U:M_U + 128], cm[:, M_L:M_L + 128]
            P.op('dve', lambda e: e.memset(H, 0.0), writes=['H'])
            for ci in range(8):
                to_tm(ci * 128, 128)
                scalars(dtp[:, ci, :], 128, Up, Lp)
                mk_xdte(128)
                state_step()
            Hin = V(oAU, F32, [1024])
            exchange('s', H, ex_i2, ex_o2, Hin, ['H'], ['Hin'])
            ts(H, Hin, flag, None, ALU.mult, None, ['Hin', 'cst', 'H'], ['H'])
            if stop == 2.4:
                P.barrier()
                return

            def ssd_chunk(t0, n, Um, Lm, NEGm, MSKm, yoff_fn):
                to_tm(t0, n)
                dt_t = dtp[:n, t0 // 128, :]
                scalars(dt_t, n, Um, Lm)
                tt(xdt[:n].rearrange("p (h q) -> p h q", h=16), xs_tm[:n].rearrange("p (h q) -> p h q", h=16),
                   dt_t.unsqueeze(2).to_broadcast([n, 16, 64]), ALU.mult, ['xs_tm', 'dtp'], ['xdt'])
                mk_xdte(n)
                for g in range(2):
                    mm(ps[:n, 2, g * 128:g * 128 + n], bcT[:, g, t0:t0 + n], bcT[:, 2 + g, t0:t0 + n], True, True, ['xsT'], [('ps', 2)], g == 1)
                tt(cbm[:n, :, :n], ps[:n, 2, 0:256].rearrange("p (g l) -> p g l", g=2)[:, :, :n],
                   MSKm[:n, :n].unsqueeze(1).to_broadcast([n, 2, n]), ALU.mult, [('ps', 2), 'cm'], ['cbm'])
                tt(AU[:n, :, :n], Um[:n, :n].unsqueeze(1).to_broadcast([n, 16, n]),
                   abuf[:n].unsqueeze(2).to_broadcast([n, 16, n]), ALU.mult, ['abuf', 'cm'], ['AU'])
                ts(negU[:n, :n], Um[:n, :n], -1.0, None, ALU.mult, None, ['cm'], ['negU'])
                for b4 in range(4):
                    bk = 4 + b4 % 2
                    pb = ps[:n, bk, :].rearrange("p (h l) -> p h l", h=4)[:, :, :n]
                    mm(pb, onesf[:n, :n], AU[:n, 4 * b4:4 * b4 + 4, :n], True, False, ['AU', 'cm'], [('ps', bk)], False)
                    mm(pb, negU[:n, :n], abuf[:n, 4 * b4:4 * b4 + 4].unsqueeze(2).to_broadcast([n, 4, n]), False, False,
                       ['negU', 'abuf'], [('ps', bk)], False)
                    mm(pb, identf[:n, :n], NEGm[:n, :n].unsqueeze(1).to_broadcast([n, 4, n]), False, True, ['cm'], [('ps', bk)], True)
                    Eb = Et[:n, 0, :].rearrange("p (h l) -> p h l", h=4)[:, :, :n]
                    act(Eb, pb, AF.Exp, [('ps', bk)], [('Et', 0)])
                    tt(WT[:n, 4 * b4:4 * b4 + 4, :n], Eb, cbm[:n, b4 // 2, :n].unsqueeze(1).to_broadcast([n, 4, n]), ALU.mult,
                       [('Et', 0), 'cbm'], ['WT'])
                for c in range(8):
                    bk = 6 + c % 2
                    pk = ('ps', bk)
                    pa = ps[:, bk, :]
                    for hh in range(2):
                        h = 2 * c + hh
                        mm(pa[64 * hh:64 * hh + 64, 0:n], xdt[:n, h * 64:(h + 1) * 64], WT[:n, h, :n], True, True,
                           ['xdt', 'WT'], [pk], False)
                        mm(pa[64 * hh:64 * hh + 64, 256:256 + n], onesf[:n, 0:64], AU[:n, h, :n], True, True, ['AU', 'cm'], [pk], hh == 1)
                    yo_ap, yo_keys = yoff_fn(c, pa, pk)
                    act(rs[:, 0, :n], pa[:, 256:256 + n], AF.Exp, [pk], ['rs'])
                    tt(rs[:, 0, :n], rs[:, 0, :n], yo_ap, ALU.mult, ['rs', pk] + yo_keys, ['rs'])
                    tt(rs[:, 0, :n], rs[:, 0, :n], pa[:, 0:n], ALU.add, ['rs', pk], ['rs'])
                    stt(rs[:, 0, :n], xsT[:, c, t0:t0 + n], cst[:, C_DS + c:C_DS + c + 1], rs[:, 0, :n], ALU.mult, ALU.add,
                        ['rs', 'xsT', 'cst'], ['rs'])
                    tt(yg[:, c, :n], rs[:, 0, :n], szT[:, c, t0:t0 + n], ALU.mult, ['rs', 'szT'], ['yg'])
                for g in range(2):
                    for c4 in range(4):
                        c = 4 * g + c4
                        act(ysq[:, :n], yg[:, c, :n], AF.Square, ['yg'], ['ysq'])
                        mm(ps[:, 3, 64:64 + n], onesf, ysq[:, :n], c4 == 0, c4 == 3, ['ysq', 'cm'], [('ps', 3)], True)
                    ts(rs[:, 1, :n], ps[:, 3, 64:64 + n], 1.0 / 512, EPS, ALU.mult, ALU.add, [('ps', 3)], ['rs1'])
                    act(rs[:, 1, :n], rs[:, 1, :n], AF.Sqrt, ['rs1'], ['rs1'])
                    P.op('dve', lambda e: e.reciprocal(out=rs[:, 1, :n], in_=rs[:, 1, :n]), reads=['rs1'], writes=['rs1'])
                    for c4 in range(4):
                        c = 4 * g + c4
                        stt(ynT[:, c, t0:t0 + n], yg[:, c, :n], cst[:, C_NG + c:C_NG + c + 1], rs[:, 1, :n], ALU.mult, ALU.mult,
                            ['yg', 'rs1', 'cst'], ['ynT'])

            for ci in range(8):
                t0 = ci * 128
                act(Hs, H, AF.Copy, ['H'], ['Hs'])

                def yoff_p(c, pa, pk):
                    mm(pa[:, 128:256], Hs[:, c * 128:(c + 1) * 128], bcT[:, 2 + c // 4, t0:t0 + 128], True, True, ['Hs', 'xsT'], [pk], True)
                    return pa[:, 128:256], []
                ssd_chunk(t0, 128, Up, Lp, cm[:, M_NEG:M_NEG + 128], cm[:, M_MSK:M_MSK + 128], yoff_p)
                state_step()
            P.barrier()
            stg = V(oAU, F32, [8, 128])
            for c in range(8):
                tr(ps[:, 4 + c % 2, 0:128], H[:, c * 128:(c + 1) * 128], identf, ['H', 'cm'], [('ps', 4 + c % 2)], True)
                act(stg[:, c, :], ps[:, 4 + c % 2, 0:128], AF.Copy, [('ps', 4 + c % 2)], ['stg'])
            P.dma('o3', lambda e: e.dma_start(out=sstp.ap().rearrange("(c p) n -> p c n", p=128), in_=stg), reads=['stg'], writes=['sstp'])
            P.barrier()
            if stop == 2.5:
                return
            Sin = V(oAU, F32, [2, 1024])
            STb = V(oWT, BF16, [2, 1024])
            CmT = V(oEt, BF16, [2, 1024])
            xdm = V(oWT, BF16, [1024])
            T0 = 1024
            bmv = cm[:, M_BM:M_BM + 1024].rearrange("p (s t) -> p s t", s=16)
            for g in range(2):
                tt(CmT[:, g, :].rearrange("p (s t) -> p s t", s=16), bcT[:, 2 + g, T0:T0 + 64].unsqueeze(1).to_broadcast([128, 16, 64]),
                   bmv, ALU.mult, ['xsT', 'cm'], ['CmT'])

            def load_state(s_):
                sl = s_ % 2
                P.dma(('sin', sl), lambda e: e.dma_start(out=Sin[:, sl, :], in_=bass.AP(sst, s_ * 1024 * 128, [[1024, 128], [1, 1024]])),
                      writes=[('sin', sl)])
                return sl
            yacc_ps = ps[:, 2, :].rearrange("p (c l) -> p c l", c=8)
            for s_ in range(16):
                sl = load_state(s_)
                for r in range(8):
                    tr(ps[:, 4 + r // 4, (r % 4) * 128:(r % 4 + 1) * 128], Sin[:, sl, r * 128:(r + 1) * 128], identf,
                       [('sin', sl), 'cm'], [('ps', 4 + r // 4)], r % 4 == 3)
                for hb_ in range(2):
                    act(STb[:, sl, :].rearrange("p (q r) -> p r q", r=8)[:, 4 * hb_:4 * hb_ + 4, :],
                        ps[:, 4 + hb_, :].rearrange("p (r q) -> p r q", r=4), AF.Copy, [('ps', 4 + hb_)], [('STb', sl)])
                for c in range(8):
                    mm(yacc_ps[:, c, :], STb[:, sl, c * 128:(c + 1) * 128], CmT[:, c // 4, s_ * 64:(s_ + 1) * 64],
                       s_ == 0 and c == 0, s_ == 15 and c == 7, [('STb', sl), 'CmT'], [('ps', 2)], c == 7)
            act(yoS, yacc_ps, AF.Copy, [('ps', 2)], ['yoS'])
            P.barrier()

            def yoff_s(c, pa, pk):
                return yoS[:, c, :], ['yoS']
            ssd_chunk(T0, 64, cm[:, M_UB:M_UB + 128], cm[:, M_LB:M_LB + 128], cm[:, M_NEGB:M_NEGB + 128],
                      cm[:, M_MSKB:M_MSKB + 128], yoff_s)
            P.barrier()
            P.op('dve', lambda e: e.tensor_copy(out=aex[:64].rearrange("p (h r) -> p h r", h=16),
                                                in_=abuf[:64].unsqueeze(2).to_broadcast([64, 16, 8])), reads=['abuf'], writes=['aex'])
            mm(ps[:, 3, 0:16], aex[:64, :], cm[:64, M_BMC:M_BMC + 16], True, True, ['aex', 'cm'], [('ps', 3)], True)
            act(cdc, ps[:, 3, 0:16], AF.Exp, [('ps', 3)], ['cdc'])
            for s_ in range(16):
                sl = load_state(s_)
                ts(xdm[:64], xdte[:64], cm[:64, M_BMC + s_:M_BMC + s_ + 1], None, ALU.mult, None, ['xdte', 'cm'], ['xdm'])
                xv = xdm[:64].rearrange("p (q r) -> p r q", r=8)
                for r in range(8):
                    bk = 4 + r // 4
                    for g in range(2):
                        mm(ps[64 * g:64 * g + 64, bk, (r % 4) * 128:(r % 4 + 1) * 128], xv[:, r, 64 * g:64 * g + 64],
                           b_tm[:64, g * 128:(g + 1) * 128], True, True, ['xdm', 'b_tm'], [('ps', bk)], (r % 4 == 3) and g == 1)
                for hb_ in range(2):
                    stt(Sin[:, sl, hb_ * 512:(hb_ + 1) * 512], Sin[:, sl, hb_ * 512:(hb_ + 1) * 512], cdc[:, s_:s_ + 1],
                        ps[:, 4 + hb_, :], ALU.mult, ALU.add, [('sin', sl), ('ps', 4 + hb_), 'cdc'], [('sin', sl)])
                P.dma(('sout', sl), lambda e: e.dma_start(out=bass.AP(ssts, s_ * 1024 * 128, [[1024, 128], [1, 1024]]), in_=Sin[:, sl, :]),
                      reads=[('sin', sl)], writes=[('ssts', s_)])
            P.barrier()
            if stop == 2.6:
                return
            mA = V(R0, F32, [4, 2048])
            mB = V(oA, F32, [5, 2048])
            macc = lambda ti: (mA[:, ti, :] if ti < 4 else mB[:, ti - 4, :])
            linear_tm('wout', lambda k: (aT[:, k, :] if k < 8 else ynT[:, k - 8, :]), TILES, macc)
            P.barrier()
            postnorm(macc, TILES, 1, 1.0, lambda r0, n: rows(x1, r0, n), lambda r0, n: rows(yout if stop == 2 else x2, r0, n), oA + 10240)
            P.barrier()

        mixer()
        if stop is not None and 2 <= stop < 3:
            return nc, ws.req

        def attention():
            SC = 512 ** -0.5
            xT = V(R0, BF16, [16, NTM])
            qT = V(R0 + 8704, BF16, [16, NTM])
            oX = R0 + 18432
            prenorm(lambda r0, n: rows(x2, r0, n), TILES, cst[:, C_PRE['xa']:C_PRE['xa'] + 16], xT, oX)
            P.barrier()
            mT = V(oX, BF16, [16, 256]); o2 = oX + 2048
            prenorm(lambda r0, n: rows(mem, r0, n), [(0, 128), (128, 128)], cst[:, C_PRE['mem']:C_PRE['mem'] + 16], mT, o2)
            P.barrier()
            Kn = V(o2, F32, [2, 2048]); o2 += 4096
            Vn = V(o2, F32, [2, 2048]); o2 += 4096
            for j in range(16):
                def evq(bi, c0, n, pa, pk):
                    act(qT[:, j, c0:c0 + n], pa, AF.Copy, [pk], ['qT'])
                win_fm(j, xT, BLOCKS, evq, name='xq')
            for name, dstt, outd in (('xk', Kn, kout), ('xv', Vn, vout)):
                linear_tm(name, lambda k: mT[:, k, :], [(0, 128), (128, 128)], lambda ti, dstt=dstt: dstt[:, ti, :])
                P.dma(('okv', name), lambda e: e.dma_start(out=outd.ap().rearrange("(t p) d -> p t d", p=128), in_=dstt),
                      reads=[('yacc', 0), ('yacc', 1)], writes=[('okv', name)])
            P.barrier()
            o3 = R0
            KT = V(o3, BF16, [16, 256]); o3 += 2048
            Vb = V(o3, BF16, [2, 2048]); o3 += 2048
            Pn = V(o3, BF16, [4, 256]); o3 += 512
            Pf = V(o3, F32, [4, 256]); o3 += 1024
            PT = V(o3, BF16, [8, 128]); o3 += 512
            Pms = V(o3, BF16, [8, 64]); o3 += 256
            qm = V(o3, BF16, [16, 64]); o3 += 512
            sm4 = V(o3, F32, [16]); o3 += 16
            assert o3 <= R0 + 8704
            otm = V(R0 + 17408, BF16, [2048])
            oT = V(oX, BF16, [16, NTM])
            stg = V(oX + 8704, F32, [2, 2048])

            def make_KT(src_fn, key_fn):
                for mt in range(2):
                    src = src_fn(mt)
                    for half in range(2):
                        for dc in range(8):
                            d = half * 8 + dc
                            bk = 4 + dc // 2
                            tr(ps[:, bk, (dc % 2) * 128:(dc % 2) * 128 + 128], src[:, d * 128:(d + 1) * 128],
                               identf, key_fn(mt) + ['cm'], [('ps', bk)], dc % 2 == 1)
                        for b2 in range(4):
                            act(KT[:, half * 8 + 2 * b2:half * 8 + 2 * b2 + 2, mt * 128:(mt + 1) * 128],
                                ps[:, 4 + b2, 0:256].rearrange("p (a m) -> p a m", a=2), AF.Copy, [('ps', 4 + b2)], ['KT'])

            def softmax_rows(n, nh, sp_views, spk):
                for h in range(nh):
                    sv = sp_views[h]
                    P.op('dve', lambda e: e.tensor_reduce(out=sm4[:n, h:h + 1], in_=sv, axis=AX.X, op=ALU.max), reads=spk, writes=['sm4'])
                    ts(sm4[:n, 4 + h:5 + h], sm4[:n, h:h + 1], -SC, None, ALU.mult, None, ['sm4'], ['sm4'])
                    act(Pf[:n, h, :], sv, AF.Exp, spk + ['sm4'], ['Pf'], scale=SC, bias=sm4[:n, 4 + h:5 + h], accum_out=sm4[:n, 8 + h:9 + h])
                    P.op('dve', lambda e: e.reciprocal(out=sm4[:n, 12 + h:13 + h], in_=sm4[:n, 8 + h:9 + h]), reads=['sm4'], writes=['sm4'])
                    ts(Pn[:n, h, :], Pf[:n, h, :], sm4[:n, 12 + h:13 + h], None, ALU.mult, None, ['Pf', 'sm4'], ['Pn'])

            make_KT(lambda mt: Kn[:, mt, :], lambda mt: [('yacc', 0), ('yacc', 1)])
            act(Vb, Vn, AF.Copy, [('yacc', 0), ('yacc', 1)], [('Vb', 0), ('Vb', 1)])
            P.barrier()
            for ti in range(8):
                t0 = ti * 128
                for hp in range(2):
                    views = []
                    for hh in range(2):
                        h = 2 * hp + hh
                        for dc in range(4):
                            mm(ps[:, hp, hh * 256:(hh + 1) * 256], qT[:, 4 * h + dc, t0:t0 + 128], KT[:, 4 * h + dc, :], dc == 0, dc == 3,
                               ['qT', 'KT'], [('ps', hp)], dc == 3)
                        views.append(ps[:, hp, hh * 256:(hh + 1) * 256])
                    for hh in range(2):
                        h = 2 * hp + hh
                        sv = views[hh]
                        P.op('dve', lambda e: e.tensor_reduce(out=sm4[:, h:h + 1], in_=sv, axis=AX.X, op=ALU.max), reads=[('ps', hp)], writes=['sm4'])
                        ts(sm4[:, 4 + h:5 + h], sm4[:, h:h + 1], -SC, None, ALU.mult, None, ['sm4'], ['sm4'])
                        act(Pf[:, h, :], sv, AF.Exp, [('ps', hp), 'sm4'], ['Pf'], scale=SC, bias=sm4[:, 4 + h:5 + h], accum_out=sm4[:, 8 + h:9 + h])
                        P.op('dve', lambda e: e.reciprocal(out=sm4[:, 12 + h:13 + h], in_=sm4[:, 8 + h:9 + h]), reads=['sm4'], writes=['sm4'])
                        ts(Pn[:, h, :], Pf[:, h, :], sm4[:, 12 + h:13 + h], None, ALU.mult, None, ['Pf', 'sm4'], ['Pn'])
                pT = PSB(2, 1, [8, 128])
                for h in range(4):
                    for mt in range(2):
                        tr(pT[:, 2 * h + mt, :], Pn[:, h, mt * 128:(mt + 1) * 128], identb, ['Pn'], [('ps', 2)], h == 3 and mt == 1)
                act(PT, pT, AF.Copy, [('ps', 2)], ['PT'])
                for h in range(4):
                    bk = 4 + h
                    for dc in range(4):
                        for mt in range(2):
                            mm(ps[:, bk, dc * 128:(dc + 1) * 128], Vb[:, mt, (4 * h + dc) * 128:(4 * h + dc + 1) * 128], PT[:, 2 * h + mt, :],
                               mt == 0, mt == 1, [('Vb', mt), 'PT'], [('ps', bk)], dc == 3 and mt == 1)
                    act(oT[:, 4 * h:4 * h + 4, t0:t0 + 128], ps[:, bk, :].rearrange("p (a t) -> p a t", a=4), AF.Copy, [('ps', bk)], ['oT'])
            P.barrier()
            T0 = 1024
            bmv = cm[:, M_BM:M_BM + 1024].rearrange("p (s t) -> p s t", s=16)
            sp = [ps[:64, 0, 0:256], ps[:64, 0, 256:512], ps[:64, 1, 0:256], ps[:64, 1, 256:512]]
            for s_ in range(16):
                for mt in range(2):
                    P.dma(('kst', mt), lambda e: e.dma_start(out=stg[:, mt, :], in_=rows(ck, s_ * 256 + mt * 128, 128)),
                          writes=[('stg', mt)])
                make_KT(lambda mt: stg[:, mt, :], lambda mt: [('stg', mt)])
                tt(qm, qT[:, :, T0:T0 + 64], bmv[:, s_, :].unsqueeze(1).to_broadcast([128, 16, 64]), ALU.mult, ['qT', 'cm'], ['qm'])
                for h in range(4):
                    for dc in range(4):
                        first = (s_ == 0 and dc == 0 and h % 2 == 0)
                        last = (s_ == 15 and dc == 3 and h % 2 == 1)
                        mm(sp[h], qm[:, 4 * h + dc, :], KT[:, 4 * h + dc, :], first, last, ['qm', 'KT'], [('ps', h // 2)], dc == 3)
            softmax_rows(64, 4, sp, [('ps', 0), ('ps', 1)])
            pT = PSB(2, 1, [8, 64])
            for h in range(4):
                for mt in range(2):
                    tr(pT[:, 2 * h + mt, :], Pn[:64, h, mt * 128:(mt + 1) * 128], identb[:64, :64], ['Pn'], [('ps', 2)], h == 3 and mt == 1)
            act(PT[:, :, 0:64], pT, AF.Copy, [('ps', 2)], ['PT'])
            for s_ in range(16):
                for mt in range(2):
                    P.dma(('kst', mt), lambda e: e.dma_start(out=stg[:, mt, :], in_=rows(cv, s_ * 256 + mt * 128, 128)),
                          writes=[('stg', mt)])
                    act(Vb[:, mt, :], stg[:, mt, :], AF.Copy, [('stg', mt)], [('Vb', mt)])
                tt(Pms, PT[:, :, 0:64], bmv[:, s_, :].unsqueeze(1).to_broadcast([128, 8, 64]), ALU.mult, ['PT', 'cm'], ['Pms'])
                for h in range(4):
                    for mt in range(2):
                        mm(ps[:64, 4 + h, :], Pms[:, 2 * h + mt, :], Vb[:, mt, h * 512:(h + 1) * 512],
                           s_ == 0 and mt == 0, s_ == 15 and mt == 1, ['Pms', ('Vb', mt)], [('ps', 4 + h)], mt == 1)
            for h in range(4):
                act(otm[:64, h * 512:(h + 1) * 512], ps[:64, 4 + h, :], AF.Copy, [('ps', 4 + h)], ['otm'])
            pT2 = PSB(0, 2, [16, 64])
            for k in range(16):
                tr(pT2[:, k, :], otm[:64, k * 128:(k + 1) * 128], identb[:64, :64], ['otm'], [('ps', 0), ('ps', 1)], k == 15)
            act(oT[:, :, T0:T0 + 64], pT2, AF.Copy, [('ps', 0), ('ps', 1)], ['oT'])
            P.barrier()
            aacc = V(R0, F32, [9, 2048])
            assert R0 + 9 * 2048 <= oX
            linear_tm('xo', lambda k: oT[:, k, :], TILES, lambda ti: aacc[:, ti, :])
            P.barrier()
            postnorm(lambda ti: aacc[:, ti, :], TILES, 2, 1.0, lambda r0, n: rows(x2, r0, n), lambda r0, n: rows(yout if stop == 3 else x3, r0, n), oX)
            P.barrier()

        attention()
        if stop == 3:
            return nc, ws.req

        ffn('f2', 'f2', 3, lambda r0, n: rows(x3, r0, n), lambda r0, n: rows(yout, r0, n))
        P.barrier()
    return nc, ws.req


_CACHE = {}


def kernel(**inp):
    inp = {k: np.asarray(v) for k, v in inp.items()}
    if 'prog' not in _CACHE:
        _, seq = build(None)
        nc, _ = build(seq)
        _CACHE['prog'] = nc
    nc = _CACHE['prog']
    wall = build_wall(inp).reshape(-1, 2048)
    cmat = build_cmat()
    postg = np.stack([inp['ffn1_post_g'][0], inp['mix_post_g'][0], inp['xattn_post_g'][0], inp['ffn2_post_g'][0]]).astype(np.float32)
    in_maps = []
    for cid in range(8):
        b, half = cid // 2, cid % 2
        xin = np.concatenate([inp['x_prompt'][b, half * 1024:(half + 1) * 1024],
                              inp['x_sample'][cid * 16:(cid + 1) * 16].reshape(64, 2048)], axis=0)
        in_maps.append({
            "xin": np.ascontiguousarray(xin, np.float32),
            "mem": np.ascontiguousarray(inp['mem_prompt'][b]),
            "ck": np.ascontiguousarray(inp['cache_mem_k'][0, cid * 16:(cid + 1) * 16].reshape(16 * 256, 2048)),
            "cv": np.ascontiguousarray(inp['cache_mem_v'][0, cid * 16:(cid + 1) * 16].reshape(16 * 256, 2048)),
            "sconv": np.ascontiguousarray(inp['state_conv'][0, cid * 16:(cid + 1) * 16].reshape(16 * 30, 1024)),
            "sssc": np.ascontiguousarray(inp['state_ssm_conv'][0, cid * 16:(cid + 1) * 16].reshape(16 * 3, 1536)),
            "sst": np.ascontiguousarray(inp['state_ssm'][0, cid * 16:(cid + 1) * 16].reshape(16 * 1024, 128)),
            "wall": wall,
            "cst": build_consts(inp, half),
            "cmat": cmat,
            "postg": postg,
        })
    res = run_bass_kernel_spmd(nc, in_maps, core_ids=list(range(8))).results
    yp = np.empty((4, 2048, 2048), np.float32)
    ys = np.empty((128, 4, 2048), np.float32)
    nk = np.empty((1, 4, 256, 4, 512), np.float32)
    nv = np.empty((1, 4, 256, 4, 512), np.float32)
    ncp = np.empty((1, 4, 30, 1024), np.float32)
    nscp = np.empty((1, 4, 3, 1536), np.float32)
    nsp = np.empty((1, 4, 16, 64, 128), np.float32)
    ncs = np.empty((1, 128, 30, 1024), np.float32)
    nscs = np.empty((1, 128, 3, 1536), np.float32)
    nss = np.empty((1, 128, 16, 64, 128), np.float32)
    for cid in range(8):
        b, half = cid // 2, cid % 2
        r = res[cid]
        yp[b, half * 1024:(half + 1) * 1024] = r["yout"][0:1024]
        ys[cid * 16:(cid + 1) * 16] = r["yout"][1024:1088].reshape(16, 4, 2048)
        ncs[0, cid * 16:(cid + 1) * 16] = r["convs"].reshape(16, 30, 1024)
        nscs[0, cid * 16:(cid + 1) * 16] = r["sscs"].reshape(16, 3, 1536)
        nss[0, cid * 16:(cid + 1) * 16] = r["ssts"].reshape(16, 16, 64, 128)
        if half == 0:
            nk[0, b] = r["kout"].reshape(256, 4, 512)
            nv[0, b] = r["vout"].reshape(256, 4, 512)
        else:
            ncp[0, b] = r["convp"]
            nscp[0, b] = r["sscp"]
            nsp[0, b] = r["sstp"].reshape(16, 64, 128)
    return (yp, ys, nk, nv, ncp, nscp, nsp, ncs, nscs, nss)
```

```python
from contextlib import ExitStack
import numpy as np
import concourse.bass as bass
import concourse.mybir as mybir
from concourse.bass_utils import run_bass_kernel_spmd

F32 = mybir.dt.float32
BF16 = mybir.dt.bfloat16
AF = mybir.ActivationFunctionType
ALU = mybir.AluOpType
AX = mybir.AxisListType

SAME_ENG_SYNC = True
EPS = 1e-6
DFF = 5504
NFC = 43


class Prog:
    CE = ('pe', 'act', 'dve', 'pool')

    def __init__(self, nc, stack):
        self.nc = nc
        self.stack = stack
        self.eng = {'pe': nc.tensor, 'act': nc.scalar, 'dve': nc.vector,
                    'pool': nc.gpsimd, 'sp': nc.sync}
        self.sem = {e: stack.enter_context(nc.semaphore('s_' + e)) for e in self.CE}
        self.cnt = {e: 0 for e in self.CE}
        self.pend = {e: False for e in self.CE}
        self.dsem = {}
        self.dcnt = {}
        self.last_w = {}
        self.readers = {}
        self.seen = {e: {} for e in self.eng}

    def _tok_sem(self, tok):
        if tok[0] == 'e':
            return self.sem[tok[1]], tok[2]
        return self.dsem[tok[1]], tok[2]

    def _wait(self, eng, toks):
        for tok in toks:
            if tok[0] == 'e' and tok[1] == eng:
                if eng == 'pe' or not SAME_ENG_SYNC:
                    continue
            key = (tok[0], tok[1])
            if self.seen[eng].get(key, 0) >= tok[2]:
                continue
            self.seen[eng][key] = tok[2]
            sem, val = self._tok_sem(tok)
            self.eng[eng].wait_ge(sem, val)

    def _deps(self, reads, writes):
        deps = []
        for r in reads:
            t = self.last_w.get(r)
            if t is not None:
                deps.append(t)
        for w in writes:
            t = self.last_w.get(w)
            if t is not None:
                deps.append(t)
            deps.extend(self.readers.get(w, ()))
        return deps

    def _commit(self, tok, reads, writes):
        for w in writes:
            self.last_w[w] = tok
            self.readers[w] = []
        for r in reads:
            if r in writes:
                continue
            self.readers.setdefault(r, []).append(tok)

    def op(self, eng, fn, reads=(), writes=(), sig=True):
        self._wait(eng, self._deps(reads, writes))
        inst = fn(self.eng[eng])
        if sig:
            self.cnt[eng] += 1
            inst.then_inc(self.sem[eng], 1)
            tok = ('e', eng, self.cnt[eng])
            self.pend[eng] = False
        else:
            assert eng == 'pe'
            tok = ('e', eng, self.cnt[eng] + 1)
            self.pend[eng] = True
        self._commit(tok, reads, writes)
        return tok

    def dma(self, key, fn, reads=(), writes=(), q='sp'):
        if key not in self.dsem:
            self.dsem[key] = self.stack.enter_context(
                self.nc.semaphore('d_%d' % len(self.dsem)))
            self.dcnt[key] = 0
        self._wait(q, self._deps(reads, writes))
        insts = fn(self.eng[q])
        if not isinstance(insts, (list, tuple)):
            insts = [insts]
        for i in insts:
            i.then_inc(self.dsem[key], 16)
            self.dcnt[key] += 16
        tok = ('d', key, self.dcnt[key])
        self._commit(tok, reads, writes)
        return tok

    def barrier(self):
        for e in self.CE:
            assert not self.pend[e], e
        toks = [('e', e, self.cnt[e]) for e in self.CE if self.cnt[e] > 0]
        toks += [('d', k, v) for k, v in self.dcnt.items() if v > 0]
        for e in self.eng:
            self._wait(e, [t for t in toks if not (t[0] == 'e' and t[1] == e)])
        self.last_w = {}
        self.readers = {}


def piece_index():
    idx = {}
    n = 0
    for f in ('f1', 'f2'):
        for kind in ('g', 'u', 'd'):
            for j in range(NFC):
                idx[(f, kind, j)] = n
                n += 1
    for j in range(37):
        idx[('win', j)] = n
        n += 1
    for nm in ('wout', 'xq', 'xk', 'xv', 'xo'):
        for j in range(16):
            idx[(nm, j)] = n
            n += 1
    return idx, n


def fm_piece(W, j, width=128):
    blk = W[:, j * width:(j + 1) * width].reshape(16, 128, width).transpose(1, 0, 2)
    out = np.zeros((128, 2048), np.float32)
    out[:, :16 * width] = blk.reshape(128, 16 * width)
    return out


def tm_piece(W, j):
    nb, half = j // 2, j % 2
    blk = W[half * 1024:(half + 1) * 1024, nb * 256:(nb + 1) * 256].reshape(8, 128, 256).transpose(1, 0, 2)
    return np.ascontiguousarray(blk).reshape(128, 2048)


def build_wall(inp):
    idx, n = piece_index()
    wall = np.empty((n, 128, 2048), np.float32)
    for f, pre in (('f1', 'ffn1'), ('f2', 'ffn2')):
        Wg, Wu, Wd = inp[pre + '_w_gate'][0], inp[pre + '_w_up'][0], inp[pre + '_w_down'][0]
        for j in range(NFC):
            wall[idx[(f, 'g', j)]] = fm_piece(Wg, j)
            wall[idx[(f, 'u', j)]] = fm_piece(Wu, j)
            wall[idx[(f, 'd', j)]] = Wd[j * 128:(j + 1) * 128, :]
    Win = inp['w_in'][0]
    for j in range(36):
        wall[idx[('win', j)]] = fm_piece(Win, j)
    wall[idx[('win', 36)]] = fm_piece(Win[:, 4608:4624], 0, 16)
    for j in range(16):
        wall[idx[('wout', j)]] = tm_piece(inp['w_out'][0], j)
        wall[idx[('xq', j)]] = fm_piece(inp['w_xq'][0], j)
        wall[idx[('xk', j)]] = tm_piece(inp['w_xk'][0], j)
        wall[idx[('xv', j)]] = tm_piece(inp['w_xv'][0], j)
        wall[idx[('xo', j)]] = tm_piece(inp['w_xo'][0], j)
    return wall


C_PRE = {'f1': 0, 'mix': 16, 'xa': 32, 'f2': 48, 'mem': 64}
C_CW, C_CB, C_LG, C_LB = 80, 328, 336, 344
C_SW, C_SB, C_NG, C_DS, C_DTB, C_AL, C_FLAG = 352, 400, 412, 420, 428, 444, 460
NCST = 464
M_ID, M_U, M_L, M_ONE, M_NEG, M_MSK = 0, 128, 256, 384, 512, 640
M_UB, M_LB, M_NEGB, M_MSKB, M_SAME, M_BM = 768, 896, 1024, 1152, 1280, 1408
M_BMC = 2432
NCM = 2448


def build_consts(inp, half):
    c = np.zeros((128, NCST), np.float32)

    def col16(v):
        return v.reshape(16, 128).T
    for nm, key in (('f1', 'ffn1_pre_g'), ('mix', 'mix_pre_g'), ('xa', 'xattn_pre_g'),
                    ('f2', 'ffn2_pre_g'), ('mem', 'mem_norm_g')):
        c[:, C_PRE[nm]:C_PRE[nm] + 16] = col16(inp[key][0])
    cw = inp['conv_w'][0]
    c[:, C_CW:C_CW + 248] = cw.T.reshape(8, 128, 31).transpose(1, 0, 2).reshape(128, 248)
    c[:, C_CB:C_CB + 8] = inp['conv_b'][0].reshape(8, 128).T
    c[:, C_LG:C_LG + 8] = inp['conv_ln_g'][0].reshape(8, 128).T
    c[:, C_LB:C_LB + 8] = inp['conv_ln_b'][0].reshape(8, 128).T
    sw = inp['ssm_conv_w'][0]
    c[:, C_SW:C_SW + 48] = sw.T.reshape(12, 128, 4).transpose(1, 0, 2).reshape(128, 48)
    c[:, C_SB:C_SB + 12] = inp['ssm_conv_b'][0].reshape(12, 128).T
    c[:, C_NG:C_NG + 8] = inp['ssm_norm_g'][0].reshape(8, 128).T
    c[:, C_DS:C_DS + 8] = np.repeat(inp['d_skip'][0], 64).reshape(8, 128).T
    c[:, C_DTB:C_DTB + 16] = inp['dt_bias'][0][None, :]
    c[:, C_AL:C_AL + 16] = inp['a_log'][0][None, :]
    c[:, C_FLAG] = float(half)
    return c


def build_cmat():
    m = np.zeros((128, NCM), np.float32)
    j = np.arange(128)[:, None]
    l = np.arange(128)[None, :]
    m[:, M_ID:M_ID + 128] = (j == l)
    m[:, M_U:M_U + 128] = (j <= l)
    m[:, M_L:M_L + 128] = (j > l)
    m[:, M_ONE:M_ONE + 128] = 1.0
    m[:, M_NEG:M_NEG + 128] = np.where(l < j, -30000.0, 0.0)
    m[:, M_MSK:M_MSK + 128] = (l >= j)
    same = (j // 4 == l // 4) & (j < 64) & (l < 64)
    m[:, M_UB:M_UB + 128] = same & (j <= l)
    m[:, M_LB:M_LB + 128] = same & (j > l)
    m[:, M_NEGB:M_NEGB + 128] = np.where(same & (l >= j), 0.0, -30000.0)
    m[:, M_MSKB:M_MSKB + 128] = same & (l >= j)
    m[:, M_SAME:M_SAME + 128] = same
    bm = (np.arange(64)[None, :] // 4 == np.arange(16)[:, None]).astype(np.float32)
    m[:, M_BM:M_BM + 1024] = bm.reshape(1, 1024)
    m[:64, M_BMC:M_BMC + 16] = bm.T
    return m


NTM = 1088
NTX = 1118
ARENA = 49152


def build(seq, stop=None):
    nc = bass.Bass("TRN2", target_bir_lowering=False)
    pidx, npieces = piece_index()

    def DT(name, shape, kind=None):
        if kind is None:
            return nc.dram_tensor(name, shape, F32)
        return nc.dram_tensor(name, shape, F32, kind=kind)
    xin = DT("xin", [NTM, 2048], "ExternalInput")
    mem = DT("mem", [256, 2048], "ExternalInput")
    ck = DT("ck", [16 * 256, 2048], "ExternalInput")
    cv = DT("cv", [16 * 256, 2048], "ExternalInput")
    sconv = DT("sconv", [16 * 30, 1024], "ExternalInput")
    sssc = DT("sssc", [16 * 3, 1536], "ExternalInput")
    sst = DT("sst", [16 * 1024, 128], "ExternalInput")
    wall = DT("wall", [npieces * 128, 2048], "ExternalInput")
    cst_d = DT("cst", [128, NCST], "ExternalInput")
    cmat_d = DT("cmat", [128, NCM], "ExternalInput")
    postg = DT("postg", [4, 2048], "ExternalInput")
    yout = DT("yout", [NTM, 2048], "ExternalOutput")
    kout = DT("kout", [256, 2048], "ExternalOutput")
    vout = DT("vout", [256, 2048], "ExternalOutput")
    convp = DT("convp", [30, 1024], "ExternalOutput")
    sscp = DT("sscp", [3, 1536], "ExternalOutput")
    sstp = DT("sstp", [1024, 128], "ExternalOutput")
    convs = DT("convs", [16 * 30, 1024], "ExternalOutput")
    sscs = DT("sscs", [16 * 3, 1536], "ExternalOutput")
    ssts = DT("ssts", [16 * 1024, 128], "ExternalOutput")
    x1 = DT("x1", [NTM, 2048])
    x2 = DT("x2", [NTM, 2048])
    x3 = DT("x3", [NTM, 2048])
    ex_i1 = DT("ex_i1", [128, 240])
    ex_o1 = DT("ex_o1", [256, 240])
    ex_i2 = DT("ex_i2", [128, 1024])
    ex_o2 = DT("ex_o2", [256, 1024])
    PAIRS = [[0, 1], [2, 3], [4, 5], [6, 7]]

    def rows(t, r0, n, c0=0, w=None):
        a = t.ap()
        w = a.shape[1] - c0 if w is None else w
        return a[r0:r0 + n, c0:c0 + w]

    st = ExitStack()
    with st:
        P = Prog(nc, st)
        arena = st.enter_context(nc.sbuf_tensor("arena", [128, ARENA], F32))
        ps = st.enter_context(nc.psum_tensor("ps", [128, 8, 512], F32))
        psb = ps[:].rearrange("p b f -> p (b f)").bitcast(BF16)

        def V(off, dt, shape):
            n = int(np.prod(shape))
            assert off + (n if dt == F32 else (n + 1) // 2) <= ARENA, (off, n)
            if dt == F32:
                v = arena[:, off:off + n]
            else:
                v = arena[:, off:off + (n + 1) // 2].bitcast(BF16)[:, 0:n]
            if len(shape) == 2:
                v = v.rearrange("p (a b) -> p a b", a=shape[0])
            elif len(shape) == 3:
                v = v.rearrange("p (a b c) -> p a b c", a=shape[0], b=shape[1])
            return v

        def PSB(bank, nbanks, shape):
            n = int(np.prod(shape))
            assert n <= nbanks * 1024
            v = psb[:, bank * 1024:bank * 1024 + n]
            if len(shape) == 2:
                v = v.rearrange("p (a b) -> p a b", a=shape[0])
            return v

        o = 0
        wst = V(o, F32, [3, 2048]); o += 6144
        wbf = V(o, BF16, [6, 2048]); o += 6144
        cst = V(o, F32, [NCST]); o += NCST
        cm = V(o, F32, [NCM]); o += NCM
        identb = V(o, BF16, [128]); o += 64
        ssm = V(o, F32, [64]); o += 64
        Aneg = V(o, F32, [16]); o += 16
        dtp = V(o, F32, [9, 16]); o += 144
        R0 = o
        identf = cm[:, M_ID:M_ID + 128]
        onesf = cm[:, M_ONE:M_ONE + 128]
        flag = cst[:, C_FLAG:C_FLAG + 1]

        class WS:
            NS, NB, DC, DD = 3, 6, 4, 6

            def __init__(self):
                self.i = 0
                self.req = []
                self.nd = 0
                self.ncst = 0

            def _dma(self, j, pid):
                s = j % self.NS
                P.dma(('wst', s), lambda e: e.dma_start(out=wst[:, s, :], in_=rows(wall, pid * 128, 128)),
                      writes=[('wst', s)])

            def _cast(self, j):
                s, t = j % self.NS, j % self.NB
                P.op('pool', lambda e: e.tensor_copy(out=wbf[:, t, :], in_=wst[:, s, :]),
                     reads=[('wst', s)], writes=[('wbf', t)])

            def next(self, name):
                pid = pidx[name]
                i = self.i
                self.i += 1
                self.req.append(pid)
                if seq is not None:
                    assert seq[i] == pid
                    src, dc, dd = seq, self.DC, self.DD
                else:
                    src, dc, dd = self.req, 0, 0
                last = len(src) - 1

                def ensure_cast(j):
                    while self.ncst <= j:
                        ensure_dma(self.ncst)
                        self._cast(self.ncst)
                        self.ncst += 1

                def ensure_dma(j):
                    while self.nd <= j:
                        if self.nd - self.NS >= 0:
                            ensure_cast(self.nd - self.NS)
                        self._dma(self.nd, src[self.nd])
                        self.nd += 1
                ensure_dma(min(i + dd, last))
                ensure_cast(min(i + dc, last))
                return ('wbf', i % self.NB), wbf[:, i % self.NB, :]
        ws = WS()

        def mm(out, lhsT, rhs, start, stop, reads, writes, sig):
            P.op('pe', lambda e: e.matmul(out, lhsT=lhsT, rhs=rhs, start=start, stop=stop),
                 reads=reads, writes=writes, sig=sig)

        def tr(out, in_, ident, reads, writes, sig):
            P.op('pe', lambda e: e.transpose(out=out, in_=in_, identity=ident),
                 reads=reads, writes=writes, sig=sig)

        def act(out, in_, func, reads, writes, **kw):
            if func == AF.Copy and 'bias' not in kw and 'accum_out' not in kw:
                sc = kw.get('scale', None)
                eng = 'dve' if ('PSUM' in str(in_.space).upper() or 'PSUM' in str(out.space).upper() or sc is not None) else 'pool'
                if sc is None:
                    P.op(eng, lambda e: e.tensor_copy(out=out, in_=in_), reads=reads, writes=writes)
                else:
                    P.op(eng, lambda e: e.tensor_scalar(out=out, in0=in_, scalar1=sc, scalar2=None, op0=ALU.mult),
                         reads=reads, writes=writes)
                return
            P.op('act', lambda e: e.activation(out=out, in_=in_, func=func, **kw), reads=reads, writes=writes)

        def tt(out, in0, in1, op, reads, writes):
            P.op('dve', lambda e: e.tensor_tensor(out=out, in0=in0, in1=in1, op=op), reads=reads, writes=writes)

        def ts(out, in0, s1, s2, op0, op1, reads, writes):
            if op1 is None:
                P.op('dve', lambda e: e.tensor_scalar(out=out, in0=in0, scalar1=s1, scalar2=None, op0=op0), reads=reads, writes=writes)
            else:
                P.op('dve', lambda e: e.tensor_scalar(out=out, in0=in0, scalar1=s1, scalar2=s2, op0=op0, op1=op1), reads=reads, writes=writes)

        def stt(out, in0, scalar, in1, op0, op1, reads, writes):
            P.op('dve', lambda e: e.scalar_tensor_tensor(out=out, in0=in0, scalar=scalar, in1=in1, op0=op0, op1=op1),
                 reads=reads, writes=writes)

        def exchange(tag, src_sb, ib, ob, dst_sb, rkeys, wkeys):
            P.dma(('exi', tag), lambda e: e.dma_start(out=ib.ap(), in_=src_sb), reads=rkeys, writes=[('ib', tag)])
            key = ('cc', tag)
            if key not in P.dsem:
                P.dsem[key] = st.enter_context(nc.semaphore('cc_' + tag))
                P.dcnt[key] = 0
            P._wait('pool', P._deps([('ib', tag)], [('ob', tag)]))
            ins = nc.gpsimd.collective_compute("AllGather", ALU.bypass, replica_groups=PAIRS,
                                               ins=[ib.ap().opt()], outs=[ob.ap().opt()])
            ins.then_inc(P.dsem[key])
            P.dcnt[key] += 1
            P._commit(('d', key, P.dcnt[key]), [('ib', tag)], [('ob', tag)])
            P.dma(('exo', tag), lambda e: e.dma_start(out=dst_sb, in_=ob.ap()[0:128, :]), reads=[('ob', tag)], writes=wkeys)

        P.dma('c0', lambda e: [e.dma_start(out=cst, in_=cst_d.ap()), e.dma_start(out=cm, in_=cmat_d.ap())],
              writes=['cst', 'cm'])
        P.op('dve', lambda e: e.tensor_copy(out=identb, in_=identf), reads=['cm'], writes=['identb'])
        act(Aneg, cst[:, C_AL:C_AL + 16], AF.Exp, ['cst'], ['Aneg'])
        ts(Aneg, Aneg, -1.0, None, ALU.mult, None, ['Aneg'], ['Aneg'])
        P.barrier()

        def rstd_chain(n, sb, s, eps_scale):
            ts(ssm[:n, sb + 1:sb + 2], ssm[:n, sb:sb + 1], eps_scale, EPS, ALU.mult, ALU.add, [('ss', s)], [('ss', s)])
            act(ssm[:n, sb + 2:sb + 3], ssm[:n, sb + 1:sb + 2], AF.Sqrt, [('ss', s)], [('ss', s)])
            P.op('dve', lambda e: e.reciprocal(out=ssm[:n, sb + 3:sb + 4], in_=ssm[:n, sb + 2:sb + 3]),
                 reads=[('ss', s)], writes=[('ss', s)])

        def prenorm(src, tiles, gcol, xT, tmp_off):
            xt = V(tmp_off, F32, [2, 2048])
            junk = V(tmp_off + 4096, BF16, [2048])
            xs = V(tmp_off + 5120, BF16, [2, 2048])
            for ti, (r0, n) in enumerate(tiles):
                s = ti % 2
                sb = 8 * s
                P.dma(('xt', s), lambda e: e.dma_start(out=xt[:n, s, :], in_=src(r0, n)), writes=[('xt', s)])
                act(junk[:n], xt[:n, s, :], AF.Square, [('xt', s)], ['junk', ('ss', s)], accum_out=ssm[:n, sb:sb + 1])
                rstd_chain(n, sb, s, 1.0 / 2048)
                act(xs[:n, s, :], xt[:n, s, :], AF.Copy, [('xt', s), ('ss', s)], [('xs', s)], scale=ssm[:n, sb + 3:sb + 4])
                pT = PSB(2 * s, 2, [16, 128])
                for k in range(16):
                    tr(pT[:, k, :n], xs[:n, s, k * 128:(k + 1) * 128], identb[:n, :n],
                       [('xs', s), 'identb'], [('ps', 2 * s), ('ps', 2 * s + 1)], k == 15)
                tt(xT[:, :, r0:r0 + n], pT[:, :, :n], gcol.unsqueeze(2).to_broadcast([128, 16, n]), ALU.mult,
                   [('ps', 2 * s), ('ps', 2 * s + 1), 'cst'], ['xT'])

        def postnorm(yfn, tiles, prow, scale, xold, xnew, tmp_off):
            xo = V(tmp_off, F32, [2, 2048])
            junk = V(tmp_off + 4096, BF16, [2048])
            gb = V(tmp_off + 5120, F32, [2048])
            P.dma('gb', lambda e: e.dma_start(out=gb, in_=bass.AP(postg, prow * 2048, [[0, 128], [1, 2048]])),
                  writes=['gb'])
            for ti, (r0, n) in enumerate(tiles):
                s = ti % 2
                sb = 8 * s
                y = yfn(ti)
                yk = ('yacc', ti)
                P.dma(('xo', s), lambda e: e.dma_start(out=xo[:n, s, :], in_=xold(r0, n)), writes=[('xo', s)])
                act(junk[:n], y[:n], AF.Square, [yk], ['junk', ('ss', s)], accum_out=ssm[:n, sb:sb + 1])
                rstd_chain(n, sb, s, 1.0 / 2048)
                stt(y[:n], y[:n], ssm[:n, sb + 3:sb + 4], gb[:n], ALU.mult, ALU.mult, [yk, ('ss', s), 'gb'], [yk])
                stt(y[:n], y[:n], float(scale), xo[:n, s, :], ALU.mult, ALU.add, [yk, ('xo', s)], [yk])
                P.dma(('yo', ti % 4), lambda e: e.dma_start(out=xnew(r0, n), in_=y[:n]), reads=[yk], writes=[('xnew', ti)])

        def mk_tiles(ntok):
            return [(r, min(128, ntok - r)) for r in range(0, ntok, 128)]

        def mk_blocks(ntok):
            return [(c, min(512, ntok - c)) for c in range(0, ntok, 512)]

        TILES, BLOCKS = mk_tiles(NTM), mk_blocks(NTM)
        BLOCKSX = mk_blocks(NTX)

        def ffn(f, gkey, prow, src, dst):
            ntok = NTM
            tiles, blocks = TILES, BLOCKS
            nt = len(tiles)
            xT = V(R0, BF16, [16, ntok])
            o1 = R0 + 8 * ntok
            prenorm(src, tiles, cst[:, C_PRE[gkey]:C_PRE[gkey] + 16], xT, o1)
            P.barrier()
            yacc = V(o1, F32, [nt, 2048])
            o2 = o1 + nt * 2048
            hT = V(o2, BF16, [4, ntok])
            o3 = o2 + 2 * ntok
            sg = V(o3, BF16, [2, 512])
            cnt = {'t': 0, 'q': 0}

            def gate_up(j):
                kg, wg = ws.next((f, 'g', j))
                ku, wu = ws.next((f, 'u', j))
                wg = wg.rearrange("p (k c) -> p k c", k=16)
                wu = wu.rearrange("p (k c) -> p k c", k=16)
                for (c0, n) in blocks:
                    t = cnt['t'] % 2
                    cnt['t'] += 1
                    for k in range(16):
                        mm(ps[:, t, :n], wg[:, k, :], xT[:, k, c0:c0 + n], k == 0, k == 15, [kg, 'xT'], [('ps', t)], k == 15)
                    for k in range(16):
                        mm(ps[:, 2 + t, :n], wu[:, k, :], xT[:, k, c0:c0 + n], k == 0, k == 15, [ku, 'xT'], [('ps', 2 + t)], k == 15)
                    act(sg[:, t, :n], ps[:, t, :n], AF.Silu, [('ps', t)], [('sg', t)])
                    tt(hT[:, j % 4, c0:c0 + n], sg[:, t, :n], ps[:, 2 + t, :n], ALU.mult, [('sg', t), ('ps', 2 + t)], [('hT', j % 4)])

            def down(grp, first):
                wd = [ws.next((f, 'd', j)) for j in grp]
                for ti, (r0, n) in enumerate(tiles):
                    for nb in range(4):
                        q = 4 + cnt['q'] % 4
                        cnt['q'] += 1
                        for gi, j in enumerate(grp):
                            mm(ps[:n, q, :], hT[:, j % 4, r0:r0 + n], wd[gi][1][:, nb * 512:(nb + 1) * 512],
                               gi == 0, gi == len(grp) - 1, [wd[gi][0], ('hT', j % 4)], [('ps', q)], gi == len(grp) - 1)
                        ysl = yacc[:n, ti, nb * 512:(nb + 1) * 512]
                        if first:
                            act(ysl, ps[:n, q, :], AF.Copy, [('ps', q)], [('yacc', ti)])
                        else:
                            tt(ysl, ysl, ps[:n, q, :], ALU.add, [('ps', q), ('yacc', ti)], [('yacc', ti)])

            groups = [list(range(j, min(j + 2, NFC))) for j in range(0, NFC, 2)]
            for gi, grp in enumerate(groups):
                for j in grp:
                    gate_up(j)
                if gi > 0:
                    down(groups[gi - 1], gi == 1)
            down(groups[-1], False)
            P.barrier()
            postnorm(lambda ti: yacc[:, ti, :], tiles, prow, 0.5, src, dst, R0)
            P.barrier()

        def linear_tm(name, inT, tiles, acc):
            cntq = 0
            for nb in range(8):
                p0 = ws.next((name, 2 * nb))
                p1 = ws.next((name, 2 * nb + 1))
                for ti, (r0, n) in enumerate(tiles):
                    q = 4 + cntq % 4
                    cntq += 1
                    for k in range(16):
                        pk, pa = (p0, p1)[k // 8]
                        mm(ps[:n, q, 0:256], inT(k)[:, r0:r0 + n], pa[:, (k % 8) * 256:(k % 8 + 1) * 256],
                           k == 0, k == 15, [pk, 'linin'], [('ps', q)], k == 15)
                    act(acc(ti)[:n, nb * 256:(nb + 1) * 256], ps[:n, q, 0:256], AF.Copy, [('ps', q)], [('yacc', ti)])

        def win_fm(j, xT, blocks, evac, name='win'):
            kw, w = ws.next((name, j))
            w = w.rearrange("p (k c) -> p k c", k=16)
            for bi, (c0, n) in enumerate(blocks):
                t = win_fm.t % 4
                win_fm.t += 1
                for k in range(16):
                    mm(ps[:, t, :n], w[:, k, :], xT[:, k, c0:c0 + n], k == 0, k == 15, [kw, 'xT'], [('ps', t)], k == 15)
                evac(bi, c0, n, ps[:, t, :n], ('ps', t))
        win_fm.t = 0

        def dwconv(buf, acc, wcol, bcol, ntap, L, bkey='cbuf', akey='cacc'):
            ts(acc, buf[:, :, 0:L], wcol[:, 0:1], bcol, ALU.mult, ALU.add, [bkey, 'cst'], [akey])
            for k in range(1, ntap):
                stt(acc, buf[:, :, k:k + L], wcol[:, k:k + 1], acc, ALU.mult, ALU.add, [bkey, akey, 'cst'], [akey])

        if stop == 1:
            ffn('f1', 'f1', 0, lambda r0, n: rows(xin, r0, n), lambda r0, n: rows(yout, r0, n))
            return nc, ws.req
        if stop == 1.5:
            ffn('f1', 'f1', 0, lambda r0, n: rows(xin, r0, n), lambda r0, n: rows(x1, r0, n))
            ffn('f2', 'f2', 3, lambda r0, n: rows(x1, r0, n), lambda r0, n: rows(x2, r0, n))
            ffn('f1', 'f1', 0, lambda r0, n: rows(x2, r0, n), lambda r0, n: rows(x3, r0, n))
            ffn('f2', 'f2', 3, lambda r0, n: rows(x3, r0, n), lambda r0, n: rows(yout, r0, n))
            return nc, ws.req
        ffn('f1', 'f1', 0, lambda r0, n: rows(xin, r0, n), lambda r0, n: rows(x1, r0, n))

        def mixer():
            tiles, blocks = TILES, BLOCKSX
            xT = V(R0, BF16, [16, NTX]); o1 = R0 + 8 * NTX
            prenorm(lambda r0, n: rows(x1, r0, n), tiles, cst[:, C_PRE['mix']:C_PRE['mix'] + 16], xT, o1)
            hb = V(o1 + 8000, BF16, [16, 30])
            hb2 = V(o1 + 8300, BF16, [16, 30])
            P.op('dve', lambda e: e.tensor_copy(out=hb, in_=xT[:, :, 994:1024]), reads=['xT'], writes=['hb'])
            exchange('h', arena[:, o1 + 8000:o1 + 8240], ex_i1, ex_o1, arena[:, o1 + 8300:o1 + 8540], ['hb'], ['hb2'])
            ts(xT[:, :, 1088:1118], hb2, flag, None, ALU.mult, None, ['hb2', 'cst'], ['xT'])
            P.barrier()
            if stop == 2.1:
                return
            aT = V(o1, BF16, [8, NTM]); o1 += 4 * NTM
            oA = o1
            assert oA == R0 + 13296
            yc = V(oA, F32, [8, NTM]); o2 = oA + 8 * NTM
            cbp2 = [V(o2, F32, [1, 1054])] * 2; o2 += 1054
            cbs2 = [V(o2, F32, [16, 34])] * 2; o2 += 544
            ubp2 = [V(o2, BF16, [1, 1054]), V(o2 + 527, BF16, [1, 1054])]; o2 += 1054
            ubs2 = [V(o2, BF16, [16, 34]), V(o2 + 272, BF16, [16, 34])]; o2 += 544
            Dg = V(o2, BF16, [31, 128]); o2 += 1984
            sgf = V(o2, F32, [1120]); o2 += 1120
            sgks = [sgf[:, 0:512], sgf[:, 512:1024], sgf[:, 1024:1120]]
            scT = V(o2, F32, [8, 480]); o2 += 3840
            tl = V(o2, F32, [128]); o2 += 128
            ld = V(o2, F32, [1024]); o2 += 1024
            for i in range(4):
                P.dma('ld', lambda e: e.dma_start(out=ld[:120], in_=rows(sconv, i * 120, 120)), writes=['ld'])
                P.dma('cp1', lambda e: [e.dma_start(out=rows(convs, (4 * i + s_) * 30, 26), in_=ld[30 * s_ + 4:30 * s_ + 30, :])
                                        for s_ in range(4)], reads=['ld'], writes=[('convs_old', i)])
                for c in range(8):
                    tr(ps[:, 4 + c % 2, 0:120], ld[:120, c * 128:(c + 1) * 128], identf[:120, :120], ['ld', 'cm'],
                       [('ps', 4 + c % 2)], True)
                    act(scT[:, c, i * 120:(i + 1) * 120], ps[:, 4 + c % 2, 0:120], AF.Copy, [('ps', 4 + c % 2)], ['scT'])
            for c in range(8):
                cbp, cbs, cbk = cbp2[0], cbs2[0], 'cbuf'
                ubp, ubs, ubk = ubp2[c % 2], ubs2[c % 2], ('ubuf', c % 2)

                def ev_g(bi, c0, n, pa, pk):
                    act(sgks[bi][:, :n], pa, AF.Sigmoid, [pk], [('sgk', bi)])

                def ev_v(bi, c0, n, pa, pk):
                    if bi < 2:
                        tt(cbp[:, 0, 30 + c0:30 + c0 + n], sgks[bi][:, :n], pa, ALU.mult, [pk, ('sgk', bi)], [cbk])
                    else:
                        tt(cbs[:, :, 30:34], sgks[bi][:, 0:64].rearrange("p (s t) -> p s t", s=16),
                           pa[:, 0:64].rearrange("p (s t) -> p s t", s=16), ALU.mult, [pk, ('sgk', bi)], [cbk])
                        tt(cbp[:, 0, 0:30], sgks[bi][:, 64:94], pa[:, 64:94], ALU.mult, [pk, ('sgk', bi)], [cbk])
                win_fm(8 + c, xT, blocks, ev_g)
                act(cbs[:, :, 0:30], scT[:, c, :].rearrange("p (s t) -> p s t", s=16), AF.Copy, ['scT'], [cbk])
                win_fm(c, xT, blocks, ev_v)
                wcol = cst[:, C_CW + 31 * c:C_CW + 31 * c + 31]
                bcol = cst[:, C_CB + c:C_CB + c + 1]
                P.op('pool', lambda e: e.tensor_copy(out=ubp, in_=cbp), reads=[cbk], writes=[ubk])
                P.op('pool', lambda e: e.tensor_copy(out=ubs, in_=cbs), reads=[cbk], writes=[ubk])
                for k in range(31):
                    ts(Dg[:, k, :], identb, wcol[:, k:k + 1], None, ALU.mult, None, ['identb', 'cst'], ['Dg'])
                for bi, (c0, n) in enumerate(((0, 512), (512, 512), (1024, 64))):
                    bk = 4 + (3 * c + bi) % 2
                    for k in range(31):
                        rhs = ubp[:, 0, c0 + k:c0 + k + n] if bi < 2 else ubs[:, :, k:k + 4]
                        outp = ps[:, bk, :n] if bi < 2 else ps[:, bk, 0:64].rearrange("p (s t) -> p s t", s=16)
                        mm(outp, Dg[:, k, :], rhs, k == 0, k == 30, ['Dg', ubk], [('ps', bk)], k == 30)
                    ts(yc[:, c, c0:c0 + n], ps[:, bk, :n], bcol, None, ALU.add, None, [('ps', bk), 'cst'], [('cacc', c)])
                tr(ps[:30, 6, 0:128], cbp[:, 0, 1024:1054], identf, [cbk, 'cm'], [('ps', 6)], True)
                act(tl[:30, :], ps[:30, 6, 0:128], AF.Copy, [('ps', 6)], ['tl'])
                P.dma('o1', lambda e: e.dma_start(out=rows(convp, 0, 30, c * 128, 128), in_=tl[:30, :]), reads=['tl'], writes=[('convp', c)])
                P.op('dve', lambda e: e.tensor_copy(out=sgks[0][:, 0:64].rearrange("p (s t) -> p s t", s=16), in_=cbs[:, :, 30:34]),
                     reads=[cbk], writes=[('sgk', 0)])
                tr(ps[:64, 7, 0:128], sgks[0][:, 0:64], identf, [('sgk', 0), 'cm'], [('ps', 7)], True)
                act(ld[:64, c * 128:(c + 1) * 128], ps[:64, 7, 0:128], AF.Copy, [('ps', 7)], ['ld2'])
            P.dma('o2', lambda e: [e.dma_start(out=rows(convs, s_ * 30 + 26, 4), in_=ld[4 * s_:4 * s_ + 4, :]) for s_ in range(16)],
                  reads=['ld2'], writes=['convs_new'])
            P.barrier()
            sq = V(oA + 8 * NTM, F32, [512])
            st1 = V(oA + 8 * NTM + 512, F32, [3, 512])
            for (c0, n) in BLOCKS:
                for c in range(8):
                    mm(ps[:, 0, :n], onesf, yc[:, c, c0:c0 + n], c == 0, c == 7, ['yc', 'cm'], [('ps', 0)], c == 7)
                for c in range(8):
                    act(sq[:, :n], yc[:, c, c0:c0 + n], AF.Square, ['yc'], ['sq'])
                    mm(ps[:, 1, :n], onesf, sq[:, :n], c == 0, c == 7, ['sq', 'cm'], [('ps', 1)], True)
                mean, var, rstd = st1[:, 0, :n], st1[:, 1, :n], st1[:, 2, :n]
                act(mean, ps[:, 0, :n], AF.Copy, [('ps', 0)], ['st1'], scale=1.0 / 1024)
                tt(var, mean, mean, ALU.mult, ['st1'], ['st1'])
                stt(var, ps[:, 1, :n], 1.0 / 1024, var, ALU.mult, ALU.subtract, ['st1', ('ps', 1)], ['st1'])
                ts(var, var, EPS, None, ALU.add, None, ['st1'], ['st1'])
                act(var, var, AF.Sqrt, ['st1'], ['st1'])
                P.op('dve', lambda e: e.reciprocal(out=rstd, in_=var), reads=['st1'], writes=['st1'])
                for c in range(8):
                    tt(sq[:, :n], yc[:, c, c0:c0 + n], mean, ALU.subtract, ['yc', 'st1'], ['sq'])
                    tt(sq[:, :n], sq[:, :n], rstd, ALU.mult, ['sq', 'st1'], ['sq'])
                    act(aT[:, c, c0:c0 + n], sq[:, :n], AF.Silu, ['sq', 'cst'], ['aT'],
                        scale=cst[:, C_LG + c:C_LG + c + 1], bias=cst[:, C_LB + c:C_LB + c + 1])
            P.barrier()
            if stop == 2.2:
                return
            xsT = V(oA, F32, [8, NTM]); o2 = oA + 8 * NTM
            bcT = V(o2, BF16, [4, NTM]); o2 += 2 * NTM
            szT = V(o2, BF16, [8, NTM]); o2 += 4 * NTM
            oB = o2
            ynT = V(o2, BF16, [8, NTM])
            oTail = o2 + 4 * NTM
            ld3 = V(o2, F32, [1536])
            xbp2 = [V(o2, F32, [1, 1028]), V(o2 + 1028, F32, [1, 1028])]; o2 += 2056
            xbs2 = [V(o2, F32, [16, 7]), V(o2 + 112, F32, [16, 7])]; o2 += 224
            cac = V(o2, F32, [NTM]); o2 += NTM
            s3T = V(o2, F32, [12, 48]); o2 += 576
            tl3 = V(o2, F32, [128]); o2 += 128
            ldc = V(o2, F32, [128]); o2 += 128
            raw = V(o2, F32, [48]); o2 += 48
            P.dma('ld', lambda e: e.dma_start(out=ld3[:48], in_=sssc.ap()), writes=['ld'])
            for c in range(12):
                tr(ps[:, 4 + c % 2, 0:48], ld3[:48, c * 128:(c + 1) * 128], identf[:48, :48], ['ld', 'cm'], [('ps', 4 + c % 2)], True)
                act(s3T[:, c, :], ps[:, 4 + c % 2, 0:48], AF.Copy, [('ps', 4 + c % 2)], ['s3T'])
            P.barrier()
            for c in range(12):
                xbp, xbs, xbk = xbp2[c % 2], xbs2[c % 2], ('xbuf', c % 2)

                def ev(bi, c0, n, pa, pk):
                    if bi < 2:
                        act(xbp[:, 0, 3 + c0:3 + c0 + n], pa, AF.Copy, [pk], [xbk])
                    else:
                        act(xbs[:, :, 3:7], pa[:, 0:64].rearrange("p (s t) -> p s t", s=16), AF.Copy, [pk], [xbk])
                        act(xbp[:, 0, 0:3], pa[:, 91:94], AF.Copy, [pk], [xbk])
                act(xbs[:, :, 0:3], s3T[:, c, :].rearrange("p (s t) -> p s t", s=16), AF.Copy, ['s3T'], [xbk])
                win_fm(24 + c, xT, blocks, ev)
                wcol = cst[:, C_SW + 4 * c:C_SW + 4 * c + 4]
                bcol = cst[:, C_SB + c:C_SB + c + 1]
                dwconv(xbp, cac[:, 0:1024].unsqueeze(1), wcol, bcol, 4, 1024, xbk, 'cacc')
                dwconv(xbs, cac[:, 1024:1088].rearrange("p (s t) -> p s t", s=16), wcol, bcol, 4, 4, xbk, 'cacc')
                dst = xsT[:, c, :] if c < 8 else bcT[:, c - 8, :]
                act(dst, cac, AF.Silu, ['cacc'], ['xsT'])
                tr(ps[:3, 6, 0:128], xbp[:, 0, 1024:1027], identf, [xbk, 'cm'], [('ps', 6)], True)
                act(tl3[:3, :], ps[:3, 6, 0:128], AF.Copy, [('ps', 6)], ['tl'])
                P.dma('o1', lambda e: e.dma_start(out=rows(sscp, 0, 3, c * 128, 128), in_=tl3[:3, :]), reads=['tl'], writes=[('sscp', c)])
                P.op('dve', lambda e: e.tensor_copy(out=raw.rearrange("p (s t) -> p s t", s=16), in_=xbs[:, :, 4:7]), reads=[xbk], writes=['raw'])
                tr(ps[:48, 7, 0:128], raw, identf, ['raw', 'cm'], [('ps', 7)], True)
                act(ldc[:48, :], ps[:48, 7, 0:128], AF.Copy, [('ps', 7)], ['ld2'])
                P.dma('o2', lambda e: e.dma_start(out=rows(sscs, 0, 48, c * 128, 128), in_=ldc[:48, :]), reads=['ld2'], writes=[('sscs', c)])
            for c in range(8):
                def evz(bi, c0, n, pa, pk):
                    nn = min(n, NTM - c0)
                    act(szT[:, c, c0:c0 + nn], pa[:, :nn], AF.Silu, [pk], ['szT'])
                win_fm(16 + c, xT, blocks, evz)
            kw, w = ws.next(('win', 36))
            w = w[:, 0:256].rearrange("p (k c) -> p k c", k=16)
            for ti, (r0, n) in enumerate(tiles):
                t = 4 + ti % 2
                for k in range(16):
                    mm(ps[:n, t, 0:16], xT[:, k, r0:r0 + n], w[:, k, :], k == 0, k == 15, [kw, 'xT'], [('ps', t)], k == 15)
                tt(dtp[:n, ti, :], ps[:n, t, 0:16], cst[:n, C_DTB:C_DTB + 16], ALU.add, [('ps', t), 'cst'], ['dtp'])
            for (a, b, n) in ((0, 8, 128), (8, 9, 64)):
                act(dtp[:n, a:b, :], dtp[:n, a:b, :], AF.Exp, ['dtp'], ['dtp'])
                act(dtp[:n, a:b, :], dtp[:n, a:b, :], AF.Ln, ['dtp'], ['dtp'], bias=1.0)
            P.barrier()
            if stop == 2.3:
                return
            o3 = R0
            xs_tm = V(o3, BF16, [1024]); o3 += 512
            b_tm = V(o3, BF16, [256]); o3 += 128
            xdt = V(o3, BF16, [1024]); o3 += 512
            xdte = V(o3, BF16, [1024]); o3 += 512
            abuf = V(o3, F32, [16]); o3 += 16
            eab = V(o3, F32, [48]); o3 += 48
            oAU = o3
            AU = V(o3, F32, [16, 128]); o3 += 2048
            oWT = o3
            WT = V(o3, BF16, [16, 128]); o3 += 1024
            xsb = V(o3, BF16, [128]); o3 += 64
            oEt = o3
            Et = V(o3, F32, [1, 512]); o3 += 512
            cbm = V(o3, F32, [2, 128]); o3 += 256
            negU = V(o3, F32, [128]); o3 += 128
            Hs = V(o3, BF16, [1024]); o3 += 512
            H = V(o3, F32, [1024]); o3 += 1024
            yg = V(o3, F32, [8, 128]); o3 += 1024
            yoS = V(o3, F32, [8, 64]); o3 += 512
            assert o3 <= R0 + 8 * NTX, (o3, R0)
            rs = V(oTail, F32, [2, 128])
            ysq = V(oTail + 256, F32, [128])
            cdc = V(oTail + 384, F32, [16])
            aex = V(oTail + 400, F32, [128])

            def to_tm(t0, n):
                pT = PSB(0, 2, [10, 128])
                for c in range(10):
                    if c < 8:
                        act(xsb[:, :n], xsT[:, c, t0:t0 + n], AF.Copy, ['xsT'], ['xsb'])
                        src = xsb[:, :n]
                    else:
                        src = bcT[:, c - 8, t0:t0 + n]
                    tr(pT[:n, c, :], src, identb, ['xsT', 'xsb'], [('ps', 0), ('ps', 1)], True)
                act(xs_tm[:n], pT[:n, 0:8, :].rearrange("p a b -> p (a b)"), AF.Copy, [('ps', 0), ('ps', 1)], ['xs_tm'])
                act(b_tm[:n], pT[:n, 8:10, :].rearrange("p a b -> p (a b)"), AF.Copy, [('ps', 0), ('ps', 1)], ['b_tm'])

            def scalars(dt_t, n, Um, Lm):
                tt(abuf[:n], dt_t, Aneg[:n], ALU.mult, ['dtp', 'Aneg'], ['abuf'])
                mm(ps[:n, 3, 0:16], Um[:n, :n], abuf[:n], True, True, ['abuf', 'cm'], [('ps', 3)], False)
                mm(ps[:n, 3, 16:32], Lm[:n, :n], abuf[:n], True, True, ['abuf', 'cm'], [('ps', 3)], False)
                mm(ps[:, 3, 32:48], onesf[:n, :], abuf[:n], True, True, ['abuf', 'cm'], [('ps', 3)], True)
                act(eab[:, 0:48], ps[:, 3, 0:48], AF.Exp, [('ps', 3)], ['eabuf'])
                tt(eab[:n, 16:32], eab[:n, 16:32], dt_t, ALU.mult, ['eabuf', 'dtp'], ['eabuf'])

            def mk_xdte(n):
                tt(xdte[:n].rearrange("p (h q) -> p h q", h=16), xs_tm[:n].rearrange("p (h q) -> p h q", h=16),
                   eab[:n, 16:32].unsqueeze(2).to_broadcast([n, 16, 64]), ALU.mult, ['xs_tm', 'eabuf'], ['xdte'])

            def state_step():
                for g in range(2):
                    mm(ps[:, 4 + g, :], b_tm[:, g * 128:(g + 1) * 128], xdte[:, g * 512:(g + 1) * 512], True, True,
                       ['b_tm', 'xdte'], [('ps', 4 + g)], True)
                H3 = H.rearrange("p (h q) -> p h q", h=16)
                tt(H3, H3, eab[:, 32:48].unsqueeze(2).to_broadcast([128, 16, 64]), ALU.mult, ['H', 'eabuf', 'Hs'], ['H'])
                for g in range(2):
                    tt(H[:, g * 512:(g + 1) * 512], H[:, g * 512:(g + 1) * 512], ps[:, 4 + g, :], ALU.add,
                       ['H', ('ps', 4 + g)], ['H'])

            Up, Lp = cm[:, M_U:M_U + 128], cm[:, M_L:M_L + 128]
            P.op('dve', lambda e: e.memset(H, 0.0), writes=['H'])
            for ci in range(8):
                to_tm(ci * 128, 128)
                scalars(dtp[:, ci, :], 128, Up, Lp)
                mk_xdte(128)
                state_step()
            Hin = V(oAU, F32, [1024])
            exchange('s', H, ex_i2, ex_o2, Hin, ['H'], ['Hin'])
            ts(H, Hin, flag, None, ALU.mult, None, ['Hin', 'cst', 'H'], ['H'])
            if stop == 2.4:
                P.barrier()
                return

            def ssd_chunk(t0, n, Um, Lm, NEGm, MSKm, yoff_fn):
                to_tm(t0, n)
                dt_t = dtp[:n, t0 // 128, :]
                scalars(dt_t, n, Um, Lm)
                tt(xdt[:n].rearrange("p (h q) -> p h q", h=16), xs_tm[:n].rearrange("p (h q) -> p h q", h=16),
                   dt_t.unsqueeze(2).to_broadcast([n, 16, 64]), ALU.mult, ['xs_tm', 'dtp'], ['xdt'])
                mk_xdte(n)
                for g in range(2):
                    mm(ps[:n, 2, g * 128:g * 128 + n], bcT[:, g, t0:t0 + n], bcT[:, 2 + g, t0:t0 + n], True, True, ['xsT'], [('ps', 2)], g == 1)
                tt(cbm[:n, :, :n], ps[:n, 2, 0:256].rearrange("p (g l) -> p g l", g=2)[:, :, :n],
                   MSKm[:n, :n].unsqueeze(1).to_broadcast([n, 2, n]), ALU.mult, [('ps', 2), 'cm'], ['cbm'])
                tt(AU[:n, :, :n], Um[:n, :n].unsqueeze(1).to_broadcast([n, 16, n]),
                   abuf[:n].unsqueeze(2).to_broadcast([n, 16, n]), ALU.mult, ['abuf', 'cm'], ['AU'])
                ts(negU[:n, :n], Um[:n, :n], -1.0, None, ALU.mult, None, ['cm'], ['negU'])
                for b4 in range(4):
                    bk = 4 + b4 % 2
                    pb = ps[:n, bk, :].rearrange("p (h l) -> p h l", h=4)[:, :, :n]
                    mm(pb, onesf[:n, :n], AU[:n, 4 * b4:4 * b4 + 4, :n], True, False, ['AU', 'cm'], [('ps', bk)], False)
                    mm(pb, negU[:n, :n], abuf[:n, 4 * b4:4 * b4 + 4].unsqueeze(2).to_broadcast([n, 4, n]), False, False,
                       ['negU', 'abuf'], [('ps', bk)], False)
                    mm(pb, identf[:n, :n], NEGm[:n, :n].unsqueeze(1).to_broadcast([n, 4, n]), False, True, ['cm'], [('ps', bk)], True)
                    Eb = Et[:n, 0, :].rearrange("p (h l) -> p h l", h=4)[:, :, :n]
                    act(Eb, pb, AF.Exp, [('ps', bk)], [('Et', 0)])
                    tt(WT[:n, 4 * b4:4 * b4 + 4, :n], Eb, cbm[:n, b4 // 2, :n].unsqueeze(1).to_broadcast([n, 4, n]), ALU.mult,
                       [('Et', 0), 'cbm'], ['WT'])
                for c in range(8):
                    bk = 6 + c % 2
                    pk = ('ps', bk)
                    pa = ps[:, bk, :]
                    for hh in range(2):
                        h = 2 * c + hh
                        mm(pa[64 * hh:64 * hh + 64, 0:n], xdt[:n, h * 64:(h + 1) * 64], WT[:n, h, :n], True, True,
                           ['xdt', 'WT'], [pk], False)
                        mm(pa[64 * hh:64 * hh + 64, 256:256 + n], onesf[:n, 0:64], AU[:n, h, :n], True, True, ['AU', 'cm'], [pk], hh == 1)
                    yo_ap, yo_keys = yoff_fn(c, pa, pk)
                    act(rs[:, 0, :n], pa[:, 256:256 + n], AF.Exp, [pk], ['rs'])
                    tt(rs[:, 0, :n], rs[:, 0, :n], yo_ap, ALU.mult, ['rs', pk] + yo_keys, ['rs'])
                    tt(rs[:, 0, :n], rs[:, 0, :n], pa[:, 0:n], ALU.add, ['rs', pk], ['rs'])
                    stt(rs[:, 0, :n], xsT[:, c, t0:t0 + n], cst[:, C_DS + c:C_DS + c + 1], rs[:, 0, :n], ALU.mult, ALU.add,
                        ['rs', 'xsT', 'cst'], ['rs'])
                    tt(yg[:, c, :n], rs[:, 0, :n], szT[:, c, t0:t0 + n], ALU.mult, ['rs', 'szT'], ['yg'])
                for g in range(2):
                    for c4 in range(4):
                        c = 4 * g + c4
                        act(ysq[:, :n], yg[:, c, :n], AF.Square, ['yg'], ['ysq'])
                        mm(ps[:, 3, 64:64 + n], onesf, ysq[:, :n], c4 == 0, c4 == 3, ['ysq', 'cm'], [('ps', 3)], True)
                    ts(rs[:, 1, :n], ps[:, 3, 64:64 + n], 1.0 / 512, EPS, ALU.mult, ALU.add, [('ps', 3)], ['rs1'])
                    act(rs[:, 1, :n], rs[:, 1, :n], AF.Sqrt, ['rs1'], ['rs1'])
                    P.op('dve', lambda e: e.reciprocal(out=rs[:, 1, :n], in_=rs[:, 1, :n]), reads=['rs1'], writes=['rs1'])
                    for c4 in range(4):
                        c = 4 * g + c4
                        stt(ynT[:, c, t0:t0 + n], yg[:, c, :n], cst[:, C_NG + c:C_NG + c + 1], rs[:, 1, :n], ALU.mult, ALU.mult,
                            ['yg', 'rs1', 'cst'], ['ynT'])

            for ci in range(8):
                t0 = ci * 128
                act(Hs, H, AF.Copy, ['H'], ['Hs'])

                def yoff_p(c, pa, pk):
                    mm(pa[:, 128:256], Hs[:, c * 128:(c + 1) * 128], bcT[:, 2 + c // 4, t0:t0 + 128], True, True, ['Hs', 'xsT'], [pk], True)
                    return pa[:, 128:256], []
                ssd_chunk(t0, 128, Up, Lp, cm[:, M_NEG:M_NEG + 128], cm[:, M_MSK:M_MSK + 128], yoff_p)
                state_step()
            P.barrier()
            stg = V(oAU, F32, [8, 128])
            for c in range(8):
                tr(ps[:, 4 + c % 2, 0:128], H[:, c * 128:(c + 1) * 128], identf, ['H', 'cm'], [('ps', 4 + c % 2)], True)
                act(stg[:, c, :], ps[:, 4 + c % 2, 0:128], AF.Copy, [('ps', 4 + c % 2)], ['stg'])
            P.dma('o3', lambda e: e.dma_start(out=sstp.ap().rearrange("(c p) n -> p c n", p=128), in_=stg), reads=['stg'], writes=['sstp'])
            P.barrier()
            if stop == 2.5:
                return
            Sin = V(oAU, F32, [2, 1024])
            STb = V(oWT, BF16, [2, 1024])
            CmT = V(oEt, BF16, [2, 1024])
            xdm = V(oWT, BF16, [1024])
            T0 = 1024
            bmv = cm[:, M_BM:M_BM + 1024].rearrange("p (s t) -> p s t", s=16)
            for g in range(2):
                tt(CmT[:, g, :].rearrange("p (s t) -> p s t", s=16), bcT[:, 2 + g, T0:T0 + 64].unsqueeze(1).to_broadcast([128, 16, 64]),
                   bmv, ALU.mult, ['xsT', 'cm'], ['CmT'])

            def load_state(s_):
                sl = s_ % 2
                P.dma(('sin', sl), lambda e: e.dma_start(out=Sin[:, sl, :], in_=bass.AP(sst, s_ * 1024 * 128, [[1024, 128], [1, 1024]])),
                      writes=[('sin', sl)])
                return sl
            yacc_ps = ps[:, 2, :].rearrange("p (c l) -> p c l", c=8)
            for s_ in range(16):
                sl = load_state(s_)
                for r in range(8):
                    tr(ps[:, 4 + r // 4, (r % 4) * 128:(r % 4 + 1) * 128], Sin[:, sl, r * 128:(r + 1) * 128], identf,
                       [('sin', sl), 'cm'], [('ps', 4 + r // 4)], r % 4 == 3)
                for hb_ in range(2):
                    act(STb[:, sl, :].rearrange("p (q r) -> p r q", r=8)[:, 4 * hb_:4 * hb_ + 4, :],
                        ps[:, 4 + hb_, :].rearrange("p (r q) -> p r q", r=4), AF.Copy, [('ps', 4 + hb_)], [('STb', sl)])
                for c in range(8):
                    mm(yacc_ps[:, c, :], STb[:, sl, c * 128:(c + 1) * 128], CmT[:, c // 4, s_ * 64:(s_ + 1) * 64],
                       s_ == 0 and c == 0, s_ == 15 and c == 7, [('STb', sl), 'CmT'], [('ps', 2)], c == 7)
            act(yoS, yacc_ps, AF.Copy, [('ps', 2)], ['yoS'])
            P.barrier()

            def yoff_s(c, pa, pk):
                return yoS[:, c, :], ['yoS']
            ssd_chunk(T0, 64, cm[:, M_UB:M_UB + 128], cm[:, M_LB:M_LB + 128], cm[:, M_NEGB:M_NEGB + 128],
                      cm[:, M_MSKB:M_MSKB + 128], yoff_s)
            P.barrier()
            P.op('dve', lambda e: e.tensor_copy(out=aex[:64].rearrange("p (h r) -> p h r", h=16),
                                                in_=abuf[:64].unsqueeze(2).to_broadcast([64, 16, 8])), reads=['abuf'], writes=['aex'])
            mm(ps[:, 3, 0:16], aex[:64, :], cm[:64, M_BMC:M_BMC + 16], True, True, ['aex', 'cm'], [('ps', 3)], True)
            act(cdc, ps[:, 3, 0:16], AF.Exp, [('ps', 3)], ['cdc'])
            for s_ in range(16):
                sl = load_state(s_)
                ts(xdm[:64], xdte[:64], cm[:64, M_BMC + s_:M_BMC + s_ + 1], None, ALU.mult, None, ['xdte', 'cm'], ['xdm'])
                xv = xdm[:64].rearrange("p (q r) -> p r q", r=8)
                for r in range(8):
                    bk = 4 + r // 4
                    for g in range(2):
                        mm(ps[64 * g:64 * g + 64, bk, (r % 4) * 128:(r % 4 + 1) * 128], xv[:, r, 64 * g:64 * g + 64],
                           b_tm[:64, g * 128:(g + 1) * 128], True, True, ['xdm', 'b_tm'], [('ps', bk)], (r % 4 == 3) and g == 1)
                for hb_ in range(2):
                    stt(Sin[:, sl, hb_ * 512:(hb_ + 1) * 512], Sin[:, sl, hb_ * 512:(hb_ + 1) * 512], cdc[:, s_:s_ + 1],
                        ps[:, 4 + hb_, :], ALU.mult, ALU.add, [('sin', sl), ('ps', 4 + hb_), 'cdc'], [('sin', sl)])
                P.dma(('sout', sl), lambda e: e.dma_start(out=bass.AP(ssts, s_ * 1024 * 128, [[1024, 128], [1, 1024]]), in_=Sin[:, sl, :]),
                      reads=[('sin', sl)], writes=[('ssts', s_)])
            P.barrier()
            if stop == 2.6:
                return
            mA = V(R0, F32, [4, 2048])
            mB = V(oA, F32, [5, 2048])
            macc = lambda ti: (mA[:, ti, :] if ti < 4 else mB[:, ti - 4, :])
            linear_tm('wout', lambda k: (aT[:, k, :] if k < 8 else ynT[:, k - 8, :]), TILES, macc)
            P.barrier()
            postnorm(macc, TILES, 1, 1.0, lambda r0, n: rows(x1, r0, n), lambda r0, n: rows(yout if stop == 2 else x2, r0, n), oA + 10240)
            P.barrier()

        mixer()
        if stop is not None and 2 <= stop < 3:
            return nc, ws.req

        def attention():
            SC = 512 ** -0.5
            xT = V(R0, BF16, [16, NTM])
            qT = V(R0 + 8704, BF16, [16, NTM])
            oX = R0 + 18432
            prenorm(lambda r0, n: rows(x2, r0, n), TILES, cst[:, C_PRE['xa']:C_PRE['xa'] + 16], xT, oX)
            P.barrier()
            mT = V(oX, BF16, [16, 256]); o2 = oX + 2048
            prenorm(lambda r0, n: rows(mem, r0, n), [(0, 128), (128, 128)], cst[:, C_PRE['mem']:C_PRE['mem'] + 16], mT, o2)
            P.barrier()
            Kn = V(o2, F32, [2, 2048]); o2 += 4096
            Vn = V(o2, F32, [2, 2048]); o2 += 4096
            for j in range(16):
                def evq(bi, c0, n, pa, pk):
                    act(qT[:, j, c0:c0 + n], pa, AF.Copy, [pk], ['qT'])
                win_fm(j, xT, BLOCKS, evq, name='xq')
            for name, dstt, outd in (('xk', Kn, kout), ('xv', Vn, vout)):
                linear_tm(name, lambda k: mT[:, k, :], [(0, 128), (128, 128)], lambda ti, dstt=dstt: dstt[:, ti, :])
                P.dma(('okv', name), lambda e: e.dma_start(out=outd.ap().rearrange("(t p) d -> p t d", p=128), in_=dstt),
                      reads=[('yacc', 0), ('yacc', 1)], writes=[('okv', name)])
            P.barrier()
            o3 = R0
            KT = V(o3, BF16, [16, 256]); o3 += 2048
            Vb = V(o3, BF16, [2, 2048]); o3 += 2048
            Pn = V(o3, BF16, [4, 256]); o3 += 512
            Pf = V(o3, F32, [4, 256]); o3 += 1024
            PT = V(o3, BF16, [8, 128]); o3 += 512
            Pms = V(o3, BF16, [8, 64]); o3 += 256
            qm = V(o3, BF16, [16, 64]); o3 += 512
            sm4 = V(o3, F32, [16]); o3 += 16
            assert o3 <= R0 + 8704
            otm = V(R0 + 17408, BF16, [2048])
            oT = V(oX, BF16, [16, NTM])
            stg = V(oX + 8704, F32, [2, 2048])

            def make_KT(src_fn, key_fn):
                for mt in range(2):
                    src = src_fn(mt)
                    for half in range(2):
                        for dc in range(8):
                            d = half * 8 + dc
                            bk = 4 + dc // 2
                            tr(ps[:, bk, (dc % 2) * 128:(dc % 2) * 128 + 128], src[:, d * 128:(d + 1) * 128],
                               identf, key_fn(mt) + ['cm'], [('ps', bk)], dc % 2 == 1)
                        for b2 in range(4):
                            act(KT[:, half * 8 + 2 * b2:half * 8 + 2 * b2 + 2, mt * 128:(mt + 1) * 128],
                                ps[:, 4 + b2, 0:256].rearrange("p (a m) -> p a m", a=2), AF.Copy, [('ps', 4 + b2)], ['KT'])

            def softmax_rows(n, nh, sp_views, spk):
                for h in range(nh):
                    sv = sp_views[h]
                    P.op('dve', lambda e: e.tensor_reduce(out=sm4[:n, h:h + 1], in_=sv, axis=AX.X, op=ALU.max), reads=spk, writes=['sm4'])
                    ts(sm4[:n, 4 + h:5 + h], sm4[:n, h:h + 1], -SC, None, ALU.mult, None, ['sm4'], ['sm4'])
                    act(Pf[:n, h, :], sv, AF.Exp, spk + ['sm4'], ['Pf'], scale=SC, bias=sm4[:n, 4 + h:5 + h], accum_out=sm4[:n, 8 + h:9 + h])
                    P.op('dve', lambda e: e.reciprocal(out=sm4[:n, 12 + h:13 + h], in_=sm4[:n, 8 + h:9 + h]), reads=['sm4'], writes=['sm4'])
                    ts(Pn[:n, h, :], Pf[:n, h, :], sm4[:n, 12 + h:13 + h], None, ALU.mult, None, ['Pf', 'sm4'], ['Pn'])

            make_KT(lambda mt: Kn[:, mt, :], lambda mt: [('yacc', 0), ('yacc', 1)])
            act(Vb, Vn, AF.Copy, [('yacc', 0), ('yacc', 1)], [('Vb', 0), ('Vb', 1)])
            P.barrier()
            for ti in range(8):
                t0 = ti * 128
                for hp in range(2):
                    for hh in range(2):
                        h = 2 * hp + hh
                        for dc in range(4):
                            mm(ps[:, hp, hh * 256:(hh + 1) * 256], qT[:, 4 * h + dc, t0:t0 + 128], KT[:, 4 * h + dc, :], dc == 0, dc == 3,
                               ['qT', 'KT'], [('ps', hp)], dc == 3)
                s4 = ps[:, 0:2, :].rearrange("p b (h m) -> p (b h) m", h=2)
                spk = [('ps', 0), ('ps', 1)]
                P.op('dve', lambda e: e.tensor_reduce(out=sm4[:, 0:4], in_=s4, axis=AX.X, op=ALU.max), reads=spk, writes=['sm4'])
                ts(sm4[:, 4:8], sm4[:, 0:4], -SC, None, ALU.mult, None, ['sm4'], ['sm4'])
                for h in range(4):
                    act(Pf[:, h, :], s4[:, h, :], AF.Exp, spk + ['sm4'], ['Pf'], scale=SC, bias=sm4[:, 4 + h:5 + h], accum_out=sm4[:, 8 + h:9 + h])
                P.op('dve', lambda e: e.reciprocal(out=sm4[:, 12:16], in_=sm4[:, 8:12]), reads=['sm4'], writes=['sm4'])
                tt(Pn, Pf, sm4[:, 12:16].unsqueeze(2).to_broadcast([128, 4, 256]), ALU.mult, ['Pf', 'sm4'], ['Pn'])
                pT = PSB(2, 1, [8, 128])
                for h in range(4):
                    for mt in range(2):
                        tr(pT[:, 2 * h + mt, :], Pn[:, h, mt * 128:(mt + 1) * 128], identb, ['Pn'], [('ps', 2)], h == 3 and mt == 1)
                act(PT, pT, AF.Copy, [('ps', 2)], ['PT'])
                for h in range(4):
                    bk = 4 + h
                    for dc in range(4):
                        for mt in range(2):
                            mm(ps[:, bk, dc * 128:(dc + 1) * 128], Vb[:, mt, (4 * h + dc) * 128:(4 * h + dc + 1) * 128], PT[:, 2 * h + mt, :],
                               mt == 0, mt == 1, [('Vb', mt), 'PT'], [('ps', bk)], dc == 3 and mt == 1)
                    act(oT[:, 4 * h:4 * h + 4, t0:t0 + 128], ps[:, bk, :].rearrange("p (a t) -> p a t", a=4), AF.Copy, [('ps', bk)], ['oT'])
            P.barrier()
            T0 = 1024
            bmv = cm[:, M_BM:M_BM + 1024].rearrange("p (s t) -> p s t", s=16)
            sp = [ps[:64, 0, 0:256], ps[:64, 0, 256:512], ps[:64, 1, 0:256], ps[:64, 1, 256:512]]
            for s_ in range(16):
                for mt in range(2):
                    P.dma(('kst', mt), lambda e: e.dma_start(out=stg[:, mt, :], in_=rows(ck, s_ * 256 + mt * 128, 128)),
                          writes=[('stg', mt)])
                make_KT(lambda mt: stg[:, mt, :], lambda mt: [('stg', mt)])
                tt(qm, qT[:, :, T0:T0 + 64], bmv[:, s_, :].unsqueeze(1).to_broadcast([128, 16, 64]), ALU.mult, ['qT', 'cm'], ['qm'])
                for h in range(4):
                    for dc in range(4):
                        first = (s_ == 0 and dc == 0 and h % 2 == 0)
                        last = (s_ == 15 and dc == 3 and h % 2 == 1)
                        mm(sp[h], qm[:, 4 * h + dc, :], KT[:, 4 * h + dc, :], first, last, ['qm', 'KT'], [('ps', h // 2)], dc == 3)
            softmax_rows(64, 4, sp, [('ps', 0), ('ps', 1)])
            pT = PSB(2, 1, [8, 64])
            for h in range(4):
                for mt in range(2):
                    tr(pT[:, 2 * h + mt, :], Pn[:64, h, mt * 128:(mt + 1) * 128], identb[:64, :64], ['Pn'], [('ps', 2)], h == 3 and mt == 1)
            act(PT[:, :, 0:64], pT, AF.Copy, [('ps', 2)], ['PT'])
            for s_ in range(16):
                for mt in range(2):
                    P.dma(('kst', mt), lambda e: e.dma_start(out=stg[:, mt, :], in_=rows(cv, s_ * 256 + mt * 128, 128)),
                          writes=[('stg', mt)])
                    act(Vb[:, mt, :], stg[:, mt, :], AF.Copy, [('stg', mt)], [('Vb', mt)])
                tt(Pms, PT[:, :, 0:64], bmv[:, s_, :].unsqueeze(1).to_broadcast([128, 8, 64]), ALU.mult, ['PT', 'cm'], ['Pms'])
                for h in range(4):
                    for mt in range(2):
                        mm(ps[:64, 4 + h, :], Pms[:, 2 * h + mt, :], Vb[:, mt, h * 512:(h + 1) * 512],
                           s_ == 0 and mt == 0, s_ == 15 and mt == 1, ['Pms', ('Vb', mt)], [('ps', 4 + h)], mt == 1)
            for h in range(4):
                act(otm[:64, h * 512:(h + 1) * 512], ps[:64, 4 + h, :], AF.Copy, [('ps', 4 + h)], ['otm'])
            pT2 = PSB(0, 2, [16, 64])
            for k in range(16):
                tr(pT2[:, k, :], otm[:64, k * 128:(k + 1) * 128], identb[:64, :64], ['otm'], [('ps', 0), ('ps', 1)], k == 15)
            act(oT[:, :, T0:T0 + 64], pT2, AF.Copy, [('ps', 0), ('ps', 1)], ['oT'])
            P.barrier()
            aacc = V(R0, F32, [9, 2048])
            assert R0 + 9 * 2048 <= oX
            linear_tm('xo', lambda k: oT[:, k, :], TILES, lambda ti: aacc[:, ti, :])
            P.barrier()
            postnorm(lambda ti: aacc[:, ti, :], TILES, 2, 1.0, lambda r0, n: rows(x2, r0, n), lambda r0, n: rows(yout if stop == 3 else x3, r0, n), oX)
            P.barrier()

        attention()
        if stop == 3:
            return nc, ws.req

        ffn('f2', 'f2', 3, lambda r0, n: rows(x3, r0, n), lambda r0, n: rows(yout, r0, n))
        P.barrier()
    return nc, ws.req


_CACHE = {}


def kernel(**inp):
    inp = {k: np.asarray(v) for k, v in inp.items()}
    if 'prog' not in _CACHE:
        _, seq = build(None)
        nc, _ = build(seq)
        _CACHE['prog'] = nc
    nc = _CACHE['prog']
    wall = build_wall(inp).reshape(-1, 2048)
    cmat = build_cmat()
    postg = np.stack([inp['ffn1_post_g'][0], inp['mix_post_g'][0], inp['xattn_post_g'][0], inp['ffn2_post_g'][0]]).astype(np.float32)
    in_maps = []
    for cid in range(8):
        b, half = cid // 2, cid % 2
        xin = np.concatenate([inp['x_prompt'][b, half * 1024:(half + 1) * 1024],
                              inp['x_sample'][cid * 16:(cid + 1) * 16].reshape(64, 2048)], axis=0)
        in_maps.append({
            "xin": np.ascontiguousarray(xin, np.float32),
            "mem": np.ascontiguousarray(inp['mem_prompt'][b]),
            "ck": np.ascontiguousarray(inp['cache_mem_k'][0, cid * 16:(cid + 1) * 16].reshape(16 * 256, 2048)),
            "cv": np.ascontiguousarray(inp['cache_mem_v'][0, cid * 16:(cid + 1) * 16].reshape(16 * 256, 2048)),
            "sconv": np.ascontiguousarray(inp['state_conv'][0, cid * 16:(cid + 1) * 16].reshape(16 * 30, 1024)),
            "sssc": np.ascontiguousarray(inp['state_ssm_conv'][0, cid * 16:(cid + 1) * 16].reshape(16 * 3, 1536)),
            "sst": np.ascontiguousarray(inp['state_ssm'][0, cid * 16:(cid + 1) * 16].reshape(16 * 1024, 128)),
            "wall": wall,
            "cst": build_consts(inp, half),
            "cmat": cmat,
            "postg": postg,
        })
    res = run_bass_kernel_spmd(nc, in_maps, core_ids=list(range(8))).results
    yp = np.empty((4, 2048, 2048), np.float32)
    ys = np.empty((128, 4, 2048), np.float32)
    nk = np.empty((1, 4, 256, 4, 512), np.float32)
    nv = np.empty((1, 4, 256, 4, 512), np.float32)
    ncp = np.empty((1, 4, 30, 1024), np.float32)
    nscp = np.empty((1, 4, 3, 1536), np.float32)
    nsp = np.empty((1, 4, 16, 64, 128), np.float32)
    ncs = np.empty((1, 128, 30, 1024), np.float32)
    nscs = np.empty((1, 128, 3, 1536), np.float32)
    nss = np.empty((1, 128, 16, 64, 128), np.float32)
    for cid in range(8):
        b, half = cid // 2, cid % 2
        r = res[cid]
        yp[b, half * 1024:(half + 1) * 1024] = r["yout"][0:1024]
        ys[cid * 16:(cid + 1) * 16] = r["yout"][1024:1088].reshape(16, 4, 2048)
        ncs[0, cid * 16:(cid + 1) * 16] = r["convs"].reshape(16, 30, 1024)
        nscs[0, cid * 16:(cid + 1) * 16] = r["sscs"].reshape(16, 3, 1536)
        nss[0, cid * 16:(cid + 1) * 16] = r["ssts"].reshape(16, 16, 64, 128)
        if half == 0:
            nk[0, b] = r["kout"].reshape(256, 4, 512)
            nv[0, b] = r["vout"].reshape(256, 4, 512)
        else:
            ncp[0, b] = r["convp"]
            nscp[0, b] = r["sscp"]
            nsp[0, b] = r["sstp"].reshape(16, 64, 128)
    return (yp, ys, nk, nv, ncp, nscp, nsp, ncs, nscs, nss)
```
